# Optimizing a Trainium2 kernel written in Bass

```python
import math
import jax, jax.numpy as jnp
from jax import lax
import numpy as np

D_MODEL = 1024
BATCH = 2
SEQ = 16384
DEPTH = 2
DEC_BATCH = 16
DEC_SEQ = 2048
PAST_LEN = 128

N_MIXERS = 2
N_ATTN_LAYERS = (DEPTH + 1) // 2
N_POOL_LAYERS = DEPTH // 2
EXPAND = 2
BRANCH_WIDTH = EXPAND * D_MODEL
HEAD_DIM = 128
N_HEADS = BRANCH_WIDTH // HEAD_DIM
N_KV_HEADS = 4
GQA_GROUP = N_HEADS // N_KV_HEADS
Q_WIDTH = N_HEADS * HEAD_DIM
KV_WIDTH = N_KV_HEADS * HEAD_DIM
ATTN_IN = Q_WIDTH + 2 * KV_WIDTH + BRANCH_WIDTH
WINDOW = 128
BLOCK = 128
ROT_DIM = HEAD_DIM // 4
ROPE_THETA = 500000.0
POOL_WINDOWS = (2, 4, 8, 16)
N_POOL_GROUPS = len(POOL_WINDOWS)
POOL_GROUP_DIM = BRANCH_WIDTH // N_POOL_GROUPS
POOL_IN = 2 * BRANCH_WIDTH
NORM_EPS = 1e-6
NEG_INF = -1e30

kernel_name = "hybrid_window_gqa_multiscale_pool_encoder"


def rms_norm(x, gain):
    xf = x.astype(jnp.float32)
    y = xf * lax.rsqrt(jnp.mean(xf * xf, axis=-1, keepdims=True) + NORM_EPS)
    return (y * gain.astype(jnp.float32)).astype(x.dtype)


def partial_rope(x, pos):
    half = ROT_DIM // 2
    inv_freq = jnp.float32(ROPE_THETA) ** (-(jnp.arange(half, dtype=jnp.float32) * 2.0 / ROT_DIM))
    ang = pos[:, None] * inv_freq[None, :]
    cos = jnp.cos(ang)[None, :, None, :]
    sin = jnp.sin(ang)[None, :, None, :]
    xr = x[..., :ROT_DIM].astype(jnp.float32)
    x1, x2 = xr[..., :half], xr[..., half:]
    rot = jnp.concatenate([x1 * cos - x2 * sin, x2 * cos + x1 * sin], axis=-1)
    return jnp.concatenate([rot.astype(x.dtype), x[..., ROT_DIM:]], axis=-1)


def banded_window_attention(q, k, v, sink):
    B, S = q.shape[0], q.shape[1]
    nb = S // BLOCK
    pad = ((0, 0), (BLOCK, BLOCK), (0, 0), (0, 0))
    kp = jnp.pad(k, pad).reshape(B, nb + 2, BLOCK, N_KV_HEADS, HEAD_DIM)
    vp = jnp.pad(v, pad).reshape(B, nb + 2, BLOCK, N_KV_HEADS, HEAD_DIM)
    kwin = jnp.concatenate([kp[:, 0:nb], kp[:, 1:nb + 1], kp[:, 2:nb + 2]], axis=2)
    vwin = jnp.concatenate([vp[:, 0:nb], vp[:, 1:nb + 1], vp[:, 2:nb + 2]], axis=2)
    qpos = jnp.arange(S).reshape(nb, BLOCK)
    kpos = jnp.arange(nb)[:, None] * BLOCK - BLOCK + jnp.arange(3 * BLOCK)[None, :]
    valid = ((jnp.abs(qpos[:, :, None] - kpos[:, None, :]) <= WINDOW)
             & (kpos >= 0)[:, None, :] & (kpos < S)[:, None, :])
    scale = 1.0 / math.sqrt(HEAD_DIM)
    qb = q.reshape(B, nb, BLOCK, N_KV_HEADS, GQA_GROUP, HEAD_DIM)
    qh = jnp.moveaxis(qb, 3, 0)
    kh = jnp.moveaxis(kwin, 3, 0)
    vh = jnp.moveaxis(vwin, 3, 0)
    sh = sink.reshape(N_KV_HEADS, GQA_GROUP)

    def one_kv_head(args):
        qg, kg, vg, sg = args
        s = jnp.einsum('bnqgd,bnkd->bngqk', qg.astype(jnp.float32), kg.astype(jnp.float32)) * scale
        s = jnp.where(valid[None, :, None], s, NEG_INF)
        sl = sg.astype(jnp.float32)[None, None, :, None, None]
        m = jnp.maximum(jnp.max(s, axis=-1, keepdims=True), sl)
        p = jnp.exp(s - m)
        denom = jnp.sum(p, axis=-1, keepdims=True) + jnp.exp(sl - m)
        o = jnp.einsum('bngqk,bnkd->bnqgd', p / denom, vg.astype(jnp.float32))
        return o.astype(q.dtype)

    out = lax.map(one_kv_head, (qh, kh, vh, sh))
    out = jnp.moveaxis(out, 0, 3)
    return out.reshape(B, S, N_HEADS * HEAD_DIM)


def attention_branch(h, w_in, sink, w_out):
    B, S, _ = h.shape
    proj = h @ w_in
    q, k, v, gate = jnp.split(proj, [Q_WIDTH, Q_WIDTH + KV_WIDTH, Q_WIDTH + 2 * KV_WIDTH], axis=-1)
    q = q.reshape(B, S, N_HEADS, HEAD_DIM)
    k = k.reshape(B, S, N_KV_HEADS, HEAD_DIM)
    v = v.reshape(B, S, N_KV_HEADS, HEAD_DIM)
    pos = jnp.arange(S, dtype=jnp.float32)
    q = partial_rope(q, pos)
    k = partial_rope(k, pos)
    o = banded_window_attention(q, k, v, sink)
    return (o * jax.nn.silu(gate)) @ w_out


def pool_branch(h, w_in, w_group, scale, w_out):
    B, S, _ = h.shape
    u, gate = jnp.split(h @ w_in, 2, axis=-1)
    uf = u.astype(jnp.float32)
    csum = jnp.concatenate([jnp.zeros((B, 1, BRANCH_WIDTH), jnp.float32),
                            jnp.cumsum(uf, axis=1)], axis=1)
    idx = jnp.arange(S)
    groups = []
    for g, w in enumerate(POOL_WINDOWS):
        sl = slice(g * POOL_GROUP_DIM, (g + 1) * POOL_GROUP_DIM)
        lo = jnp.maximum(idx - w // 2, 0)
        hi = jnp.minimum(idx + w // 2 - 1, S - 1)
        cg = csum[..., sl]
        cnt = (hi - lo + 1).astype(jnp.float32)[None, :, None]
        mean = (jnp.take(cg, hi + 1, axis=1) - jnp.take(cg, lo, axis=1)) / cnt
        groups.append(mean - uf[..., sl])
    pooled = jnp.stack(groups, axis=2).astype(h.dtype)
    mixed = jnp.einsum('bsgc,gcd->bsgd', pooled, w_group).reshape(B, S, BRANCH_WIDTH) * scale
    return (mixed * jax.nn.silu(gate)) @ w_out


def trunk(x, norm_pre, norm_post, attn_w_in, attn_sink, attn_w_out,
          pool_w_in, pool_w_group, pool_scale, pool_w_out):
    for i in range(DEPTH):
        h = rms_norm(x, norm_pre[i])
        j = i // N_MIXERS
        if i % N_MIXERS == 0:
            out = attention_branch(h, attn_w_in[j], attn_sink[j], attn_w_out[j])
        else:
            out = pool_branch(h, pool_w_in[j], pool_w_group[j], pool_scale[j], pool_w_out[j])
        x = x + rms_norm(out, norm_post[i])
    return x


def setup_inputs(seed: int = 0) -> dict:
    key = jax.random.key(seed)
    ks = jax.random.split(key, 13)
    f32 = jnp.float32
    return {
        "x_prompt": jax.random.normal(ks[0], (BATCH, SEQ, D_MODEL), f32),
        "x_sample": jax.random.normal(ks[1], (DEC_BATCH, DEC_SEQ, D_MODEL), f32),
        "norm_pre": 1.0 + 0.05 * jax.random.normal(ks[2], (DEPTH, D_MODEL), f32),
        "norm_post": 1.0 + 0.05 * jax.random.normal(ks[3], (DEPTH, D_MODEL), f32),
        "attn_w_in": jax.random.normal(ks[4], (N_ATTN_LAYERS, D_MODEL, ATTN_IN), f32) * D_MODEL ** -0.5,
        "attn_sink": 0.5 * jax.random.normal(ks[5], (N_ATTN_LAYERS, N_HEADS), f32),
        "attn_w_out": jax.random.normal(ks[6], (N_ATTN_LAYERS, BRANCH_WIDTH, D_MODEL), f32) * BRANCH_WIDTH ** -0.5,
        "pool_w_in": jax.random.normal(ks[7], (N_POOL_LAYERS, D_MODEL, POOL_IN), f32) * D_MODEL ** -0.5,
        "pool_w_group": jax.random.normal(ks[8], (N_POOL_LAYERS, N_POOL_GROUPS, POOL_GROUP_DIM, POOL_GROUP_DIM), f32) * POOL_GROUP_DIM ** -0.5,
        "pool_scale": 1.0 + 0.1 * jax.random.normal(ks[9], (N_POOL_LAYERS, BRANCH_WIDTH), f32),
        "pool_w_out": jax.random.normal(ks[10], (N_POOL_LAYERS, BRANCH_WIDTH, D_MODEL), f32) * BRANCH_WIDTH ** -0.5,
    }


def reference(x_prompt, x_sample, norm_pre, norm_post, attn_w_in, attn_sink, attn_w_out,
              pool_w_in, pool_w_group, pool_scale, pool_w_out):
    y_prompt = trunk(x_prompt, norm_pre, norm_post, attn_w_in, attn_sink, attn_w_out,
                     pool_w_in, pool_w_group, pool_scale, pool_w_out)
    y_sample = trunk(x_sample, norm_pre, norm_post, attn_w_in, attn_sink, attn_w_out,
                     pool_w_in, pool_w_group, pool_scale, pool_w_out)
    return (y_prompt, y_sample)
```

```python
import math
import numpy as np
import ml_dtypes
import concourse.bass as bass
import concourse.mybir as mybir
from concourse.bass_utils import run_bass_kernel_spmd

F32 = mybir.dt.float32
BF16 = mybir.dt.bfloat16
AF = mybir.ActivationFunctionType
ALU = mybir.AluOpType

D = 1024
BW = 2048
NH = 16
NKV = 4
HD = 128
ATT_IN = 5120
POOL_IN = 4096
EPS = 1e-6
ROPE_THETA = 500000.0
QSCALE = 1.0 / math.sqrt(128.0)

RX = 10
NW = 5
EPOCH = 4000
SKIP = set()


class Buf:
    __slots__ = ("name", "w", "r", "psum")

    def __init__(self, name, psum=False):
        self.name = name
        self.w = None
        self.r = []
        self.psum = psum


class Op:
    __slots__ = ("eng", "fn", "deps", "inc", "order", "tick", "dma", "sem_i", "tgt", "prev", "tag")


class Sched:
    ENGS = ("pe", "act", "dve", "pool", "sp")
    NDS = {"sp": 8, "pool": 8, "act": 2}

    def __init__(self):
        self.q = {e: [] for e in self.ENGS}
        self.order = 0
        self.dma_cnt = {e: 0 for e in self.ENGS}
        self.dma_last = {}
        self.tag = ""

    def op(self, eng, fn, r=(), w=(), dma=False):
        o = Op()
        o.tag = self.tag
        o.eng = eng
        o.fn = fn
        o.inc = False
        o.dma = dma
        o.order = self.order
        self.order += 1
        o.prev = None
        if dma:
            n = self.NDS[eng]
            o.sem_i = self.dma_cnt[eng] % n
            self.dma_cnt[eng] += 1
            key = (eng, o.sem_i)
            o.prev = self.dma_last.get(key)
            o.tgt = (o.prev.tgt if o.prev is not None else 0) + 16
            self.dma_last[key] = o
        best = {}

        def add(d):
            if d is None or d is o:
                return
            if d.dma:
                key = ("d", d.eng, d.sem_i)
            else:
                if d.eng == "pe" and eng == "pe" and not dma:
                    return
                key = ("e", d.eng)
            c = best.get(key)
            if c is None or d.order > c.order:
                best[key] = d

        for b in r:
            add(b.w)
            if b.psum:
                for x in b.r:
                    if x.eng != eng:
                        add(x)
        for b in w:
            add(b.w)
            for x in b.r:
                add(x)
        o.deps = list(best.values())
        for d in o.deps:
            d.inc = True
        for b in r:
            b.r.append(o)
        for b in w:
            b.w = o
            b.r = []
        self.q[eng].append(o)
        return o

    def emit(self, nc, engines, esems, dsems):
        for e in self.ENGS:
            c = 0
            for o in self.q[e]:
                if o.inc and not o.dma:
                    c += 1
                    o.tick = c
        need = {e: 1 for e in self.ENGS}
        for e in self.ENGS:
            c = sum(1 for o in self.q[e] if o.inc and not o.dma)
            need[e] = max(1, (c + EPOCH - 1) // EPOCH)
        return need

    def run_engine(self, eng, E, esems, dsems):
        seen = {}

        def wait(key, sem, val):
            if seen.get(key, 0) >= val:
                return
            seen[key] = val
            E.wait_ge(sem, val)

        for o in self.q[eng]:
            if o.dma and o.prev is not None:
                wait(("d", o.eng, o.sem_i), dsems[o.eng][o.sem_i], o.prev.tgt)
            for d in o.deps:
                if d.dma:
                    wait(("d", d.eng, d.sem_i), dsems[d.eng][d.sem_i], d.tgt)
                else:
                    ep = (d.tick - 1) // EPOCH
                    wait(("e", d.eng, ep), esems[d.eng][ep], (d.tick - 1) % EPOCH + 1)
            ins = o.fn(E)
            if o.dma:
                ins.then_inc(dsems[o.eng][o.sem_i], 16)
            elif o.inc:
                ep = (o.tick - 1) // EPOCH
                ins.then_inc(esems[eng][ep], 1)


def seg_info(segs):
    out = []
    xo = 0
    yo = 0
    for kind, n_ext in segs:
        n_own = n_ext - 4 if kind == "halo" else n_ext
        out.append(dict(kind=kind, n_ext=n_ext, n_own=n_own, xo=xo, yo=yo))
        xo += n_ext
        yo += n_own
    return out, xo, yo


def build_program(segs, debug=False, limit=None):
    S = Sched()
    sinfo, NEXT, NOWN = seg_info(segs)
    nc = bass.Bass("TRN2", target_bir_lowering=False)

    def din(name, shape, dt):
        return nc.dram_tensor(name, list(shape), dt, kind="ExternalInput").ap()

    x_d = din("x", [NEXT * 128, D], F32)
    rope_d = din("rope", [NEXT * 128, 64], F32)
    masks_d = din("masks", [128, 4, 128], BF16)
    ident_d = din("ident", [128, 128], BF16)
    gpre_d = din("gpre", [128, 2, 8], F32)
    gpost_d = din("gpost", [128, 2, D], F32)
    sink_d = din("sinkb", [128, NH], F32)
    pscale_d = din("pscale", [128, 16], F32)
    icnt_d = din("icnt", [128, 2, 2, 4, 8], F32)
    w_in1 = din("attn_w_in", [D, ATT_IN], F32)
    w_out1 = din("attn_w_out", [BW, D], F32)
    w_in2 = din("pool_w_in", [D, POOL_IN], F32)
    w_grp = din("pool_w_group", [BW, 512], F32)
    w_out2 = din("pool_w_out", [BW, D], F32)
    y_d = nc.dram_tensor("y", [NOWN * 128, D], F32, kind="ExternalOutput").ap()
    dbg_d = nc.dram_tensor("dbg_x1", [NEXT * 128, D], F32, kind="ExternalOutput").ap() if debug else None

    def dscr(name, shape):
        return nc.dram_tensor(name, list(shape), BF16, kind="Internal").ap()

    b_in1 = dscr("b_in1", [D, ATT_IN])
    b_out1 = dscr("b_out1", [BW, D])
    b_in2 = dscr("b_in2", [D, POOL_IN])
    b_grp = dscr("b_grp", [BW, 512])
    b_out2 = dscr("b_out2", [BW, D])

    from contextlib import ExitStack

    es = ExitStack()

    def sb(name, shape, dt):
        return es.enter_context(nc.sbuf_tensor(name, list(shape), dt))

    with es:
        xres = sb("xres", [128, RX, D], F32)
        hb = sb("hb", [128, 4, D], BF16)
        hT = sb("hT", [128, 8, 640], BF16)
        kT = sb("kT", [128, 8, 512], BF16)
        vv = sb("vv", [128, 8, 512], BF16)
        qtok = sb("qtok", [128, 2, 4, 512], BF16)
        ktok = qtok[:, 1]
        qT = sb("qT", [128, 2, 4, 512], BF16)
        pT = sb("pT", [128, 2, 3, 512], BF16)
        rden = sb("rden", [128, 2, 512], F32)
        otmp = sb("otmp", [128, 2, 512], F32)
        aT = sb("aT", [128, 16, 512], BF16)
        h2T = sb("h2T", [128, 8, 528], BF16)
        h2halo = sb("h2halo", [128, 8, 8], BF16)
        uT = sb("uT", [128, 4, 528], F32)
        ptmp = sb("ptmp", [128, 2, 528], F32)
        pfix = sb("pfix", [128, 8], F32)
        pmix = sb("pmix", [128, 2, 512], F32)
        wring = sb("wring", [128, NW, 8, 512], BF16)
        ropet = sb("ropet", [128, 2, 5, 64], F32)
        rtmp = sb("rtmp", [128, 2, 4, 128], F32)
        stat = sb("stat", [128, 16, 4], F32)
        ident = sb("ident_s", [128, 128], BF16)
        ones = sb("ones_s", [128, 128], BF16)
        masks = sb("masks_s", [128, 4, 128], BF16)
        gpre = sb("gpre_s", [128, 2, 8], F32)
        gpost = sb("gpost_s", [128, 2, D], F32)
        esink = sb("esink_s", [128, NH], F32)
        esrow = sb("esrow_s", [1, NH], BF16)
        pscale = sb("pscale_s", [128, 16], F32)
        icnt = sb("icnt_s", [128, 2, 2, 4, 8], F32)
        epsb = sb("epsb", [128, 1], F32)
        ps = es.enter_context(nc.psum_tensor("ps", [128, 8, 512], F32))

        B_x = [Buf(f"x{i}") for i in range(RX)]
        B_hb = [Buf(f"hb{i}") for i in range(4)]
        B_hT = [Buf(f"hT{i}") for i in range(5)]
        B_kT = [Buf(f"kT{i}") for i in range(8)]
        B_v = [Buf(f"v{i}") for i in range(8)]
        B_qtok = [Buf(f"qtok{i}") for i in range(2)]
        B_ktok = B_qtok[1]
        B_qT = [Buf(f"qT{i}") for i in range(2)]
        B_pT = [Buf(f"pT{i}") for i in range(2)]
        B_rden = [Buf(f"rden{i}") for i in range(2)]
        B_otmp = [Buf(f"otmp{i}") for i in range(2)]
        B_aT = [Buf(f"aT{i}") for i in range(4)]
        B_h2T = Buf("h2T")
        B_h2halo = Buf("h2halo")
        B_uT = [Buf(f"uT{i}") for i in range(4)]
        B_ptmp = [Buf("pa"), Buf("pb")]
        B_pfix = Buf("pfix")
        B_pmix = [Buf(f"pmix{i}") for i in range(2)]
        B_w = [Buf(f"w{i}") for i in range(NW)]
        B_rope = [Buf(f"rope{i}") for i in range(2)]
        B_rtmp = [Buf("rta"), Buf("rtb")]
        B_stat = [Buf(f"stat{i}") for i in range(16)]
        B_ps = [Buf(f"ps{i}", psum=True) for i in range(8)]
        B_const = Buf("const")

        state = dict(bank=0, stat=0, hb=0)

        def alloc_banks(n):
            b = state["bank"]
            if b + n > 8:
                b = 0
            state["bank"] = (b + n) % 8
            return b

        def alloc_stat():
            s = state["stat"]
            state["stat"] = (s + 1) % 16
            return s

        def dma(eng, out, in_, r, w):
            return S.op(eng, lambda E, out=out, in_=in_: E.dma_start(out=out, in_=in_), r=r, w=w, dma=True)

        def mm(out, lhsT, rhs, start, stop, r, w):
            return S.op(
                "pe",
                lambda E, out=out, lhsT=lhsT, rhs=rhs, start=start, stop=stop: E.matmul(
                    out, lhsT=lhsT, rhs=rhs, start=start, stop=stop
                ),
                r=r,
                w=w,
            )

        def tr(out, in_, idn, r, w):
            return S.op(
                "pe",
                lambda E, out=out, in_=in_, idn=idn: E.transpose(out=out, in_=in_, identity=idn),
                r=r,
                w=w,
            )

        def act(out, in_, func, r, w, scale=None, bias=None, accum=None):
            def fn(E, out=out, in_=in_, func=func, scale=scale, bias=bias, accum=accum):
                kw = {}
                if scale is not None:
                    kw["scale"] = scale
                if bias is not None:
                    kw["bias"] = bias
                if accum is not None:
                    kw["accum_out"] = accum
                return E.activation(out=out, in_=in_, func=func, **kw)

            return S.op("act", fn, r=r, w=w)

        def tt(eng, out, in0, in1, op, r, w):
            return S.op(
                eng,
                lambda E, out=out, in0=in0, in1=in1, op=op: E.tensor_tensor(out=out, in0=in0, in1=in1, op=op),
                r=r,
                w=w,
            )

        def ts(eng, out, in0, s1, op0, r, w, s2=None, op1=None):
            def fn(E, out=out, in0=in0, s1=s1, op0=op0, s2=s2, op1=op1):
                if op1 is None:
                    return E.tensor_scalar(out=out, in0=in0, scalar1=s1, scalar2=None, op0=op0)
                return E.tensor_scalar(out=out, in0=in0, scalar1=s1, scalar2=s2, op0=op0, op1=op1)

            return S.op(eng, fn, r=r, w=w)

        def stt(out, in0, scalar, in1, op0, op1, r, w):
            return S.op(
                "dve",
                lambda E, out=out, in0=in0, scalar=scalar, in1=in1, op0=op0, op1=op1: E.scalar_tensor_tensor(
                    out=out, in0=in0, scalar=scalar, in1=in1, op0=op0, op1=op1
                ),
                r=r,
                w=w,
            )

        def cp(eng, out, in_, r, w):
            if eng == "act":
                return act(out, in_, AF.Copy, r, w)
            return S.op(eng, lambda E, out=out, in_=in_: E.tensor_copy(out=out, in_=in_), r=r, w=w)

        def recip(out, in_, r, w):
            return S.op("dve", lambda E, out=out, in_=in_: E.reciprocal(out=out, in_=in_), r=r, w=w)

        def memset(eng, ap, val, w):
            return S.op(eng, lambda E, ap=ap, val=val: E.memset(ap, val), r=(), w=w)

        def psbf(bank):
            return ps[:, bank, :].bitcast(BF16)

        for dst, src in (
            (ident[:], ident_d[:, :]),
            (masks[:], masks_d[:, :, :]),
            (gpre[:], gpre_d[:, :, :]),
            (gpost[:], gpost_d[:, :, :]),
            (esink[:], sink_d[:, :]),
            (pscale[:], pscale_d[:, :]),
            (icnt[:], icnt_d[:, :, :, :, :]),
        ):
            dma("sp", dst, src, r=(), w=(B_const,))
        memset("dve", ones[:], 2.0, w=(B_const,))
        memset("dve", epsb[:], EPS, w=(B_const,))
        act(esink[:], esink[:], AF.Exp, r=(B_const,), w=(B_const,))
        cp("dve", esrow[:], esink[0:1, :], r=(B_const,), w=(B_const,))
        if debug:
            for t_, bl in ((hT, B_hT), (kT, B_kT), (vv, B_v), (qtok, B_qtok), (qT, B_qT), (aT, B_aT), (ktok, [B_ktok])):
                memset("pool", t_[:], 0.0, w=tuple(bl))
        v_in1 = b_in1.rearrange("(kc p) n -> p kc n", p=128)
        v_out1 = b_out1.rearrange("(kc p) n -> p kc n", p=128)
        v_in2 = b_in2.rearrange("(kc p) n -> p kc n", p=128)
        v_grp = b_grp.rearrange("(kc p) n -> p kc n", p=128)
        v_out2 = b_out2.rearrange("(kc p) n -> p kc n", p=128)

        def piece_src(key):
            k = key[0]
            if k == "k":
                return v_in1[:, :, 2048:2560], 8, (b_in1, w_in1, 0, D, 2048, 2560)
            if k == "v":
                return v_in1[:, :, 2560:3072], 8, (b_in1, w_in1, 0, D, 2560, 3072)
            if k == "q":
                j = key[1]
                return v_in1[:, :, 512 * j:512 * j + 512], 8, (b_in1, w_in1, 0, D, 512 * j, 512 * j + 512)
            if k == "g":
                j = key[1]
                c0 = 3072 + 512 * j
                return v_in1[:, :, c0:c0 + 512], 8, (b_in1, w_in1, 0, D, c0, c0 + 512)
            if k == "o1":
                h, kk = key[1], key[2]
                return v_out1[:, 8 * kk:8 * kk + 8, 512 * h:512 * h + 512], 8, (b_out1, w_out1, 1024 * kk, 1024 * kk + 1024, 512 * h, 512 * h + 512)
            if k == "u":
                g = key[1]
                return v_in2[:, :, 512 * g:512 * g + 512], 8, (b_in2, w_in2, 0, D, 512 * g, 512 * g + 512)
            if k == "g2":
                g = key[1]
                c0 = 2048 + 512 * g
                return v_in2[:, :, c0:c0 + 512], 8, (b_in2, w_in2, 0, D, c0, c0 + 512)
            if k == "grp":
                g = key[1]
                return v_grp[:, 4 * g:4 * g + 4, :], 4, (b_grp, w_grp, 512 * g, 512 * g + 512, 0, 512)
            if k == "o2":
                h, kk = key[1], key[2]
                return v_out2[:, 8 * kk:8 * kk + 8, 512 * h:512 * h + 512], 8, (b_out2, w_out2, 1024 * kk, 1024 * kk + 1024, 512 * h, 512 * h + 512)
            raise KeyError(key)

        B_cast = {}

        def cast_issue(key):
            if key in B_cast:
                return
            _, _, (dst, src, r0, r1, c0, c1) = piece_src(key)
            B_cast[key] = Buf("cast")
            dma("pool", dst[r0:r1, c0:c1], src[r0:r1, c0:c1], r=(), w=(B_cast[key],))

        steps = []
        for si, sg in enumerate(sinfo):
            kind, n_ext = sg["kind"], sg["n_ext"]
            nst = n_ext // 4
            l2next = 0
            needs_q = (lambda b, n=n_ext, k=kind: (1 <= b <= n - 2) if k == "halo" else (0 <= b <= n - 1))
            needs_l2 = (lambda b, n=n_ext, k=kind: (2 <= b <= n - 3) if k == "halo" else (0 <= b <= n - 1))
            glist = list(range(nst)) + (["flush"] if kind == "full" else [])
            for G in glist:
                if G == "flush":
                    new = []
                    base = n_ext - 1
                    qbs = [n_ext - 1]
                    lim = n_ext - 1
                else:
                    new = list(range(4 * G, 4 * G + 4))
                    base = 4 * G - 1
                    qbs = [b for b in range(4 * G - 1, 4 * G + 3) if b >= 0 and needs_q(b)]
                    lim = 4 * G + 1
                l2bs = [b for b in range(l2next, min(lim, n_ext - 1) + 1) if needs_l2(b)]
                l2next = max(l2next, lim + 1)
                steps.append(dict(si=si, G=G, new=new, base=base, qbs=qbs, l2bs=l2bs))
        for si, sg in enumerate(sinfo):
            mine = [s for s in steps if s["si"] == si and s["l2bs"]]
            for s in steps:
                if s["si"] == si:
                    s["l2first"] = False
                    s["l2last"] = False
            mine[0]["l2first"] = True
            mine[-1]["l2last"] = True

        def phase_pieces(ph, st):
            if ph == "AB":
                return [("k",), ("v",)] if st["new"] else []
            if ph == "C":
                p = []
                for j in range(4):
                    p += [("q", j), ("g", j)]
                return p
            if ph == "DE":
                return [("o1", 0, 0), ("o1", 0, 1), ("o1", 1, 0), ("o1", 1, 1)]
            if ph == "F":
                return [("g2", 3), ("u", 3), ("g2", 2), ("u", 2), ("g2", 1), ("u", 1), ("g2", 0), ("u", 0)]
            if ph == "H":
                return [("o2", 0, 0), ("o2", 0, 1), ("o2", 1, 0), ("o2", 1, 1)]
            return []

        sched = []
        hasA = lambda st: bool(st["new"]) or st["G"] == "flush"
        if hasA(steps[0]):
            sched.append(("AB", 0))
        for i, st in enumerate(steps):
            nxt = steps[i + 1] if i + 1 < len(steps) else None
            if st["qbs"]:
                sched.append(("C", i))
                sched.append(("DE", i))
            if nxt is not None and hasA(nxt):
                sched.append(("AB", i + 1))
            elif st["l2bs"]:
                sched.append(("E2", i))
            if st["l2bs"]:
                sched.append(("F", i))
                sched.append(("H", i))
            sched.append(("END", i))

        allp = []
        for ph, i in sched:
            allp += phase_pieces(ph, steps[i])
        wst = dict(issued=0, used=0, rel=0)

        CAST_AHEAD = 6

        def w_issue_upto(n):
            while wst["issued"] < min(n, len(allp)):
                i = wst["issued"]
                for k2 in allp[i:i + CAST_AHEAD]:
                    cast_issue(k2)
                src, nk, _ = piece_src(allp[i])
                slot = i % NW
                dma("sp", wring[:, slot, 0:nk, :], src, r=(B_cast[allp[i]],), w=(B_w[slot],))
                wst["issued"] += 1

        def wget(key):
            i = wst["used"]
            assert allp[i] == key, (allp[i], key)
            assert i < wst["issued"], "weight piece not issued"
            wst["used"] += 1
            return i % NW

        def wrel(n=1):
            wst["rel"] += n
            assert wst["rel"] <= wst["used"]
            w_issue_upto(wst["rel"] + NW)

        def xslot(si, b):
            return (sinfo[si]["xo"] + b) % RX

        def issue_rope(st):
            si = st["si"]
            xo = sinfo[si]["xo"]
            rb = st["ri"]
            lo = max(st["base"], 0)
            hi = st["base"] + 4 if st["new"] else st["base"]
            i0 = lo - st["base"]
            cnt = hi - lo + 1
            src = rope_d[(xo + lo) * 128:(xo + hi + 1) * 128, :].rearrange("(b p) c -> p b c", p=128)
            dma("sp", ropet[:, rb, i0:i0 + cnt, :], src, r=(), w=(B_rope[rb],))

        def right_halo_block(st):
            sg = sinfo[st["si"]]
            br = st["l2bs"][-1] + 1
            ok = (br <= sg["n_ext"] - 1) and ((1 <= br <= sg["n_ext"] - 2) if sg["kind"] == "halo" else True)
            return br if ok else None

        last_use = {}
        for pos, (ph, i) in enumerate(sched):
            st = steps[i]
            used = set()
            if ph == "AB":
                used |= set(st["new"])
                if i >= 1 and steps[i - 1]["l2bs"]:
                    pst = steps[i - 1]
                    for b in list(pst["l2bs"]) + ([right_halo_block(pst)] if right_halo_block(pst) is not None else []):
                        last_use[(pst["si"], b)] = pos
            elif ph == "DE":
                used |= set(st["qbs"])
                if st["l2bs"] and st["l2first"] and sinfo[st["si"]]["kind"] == "halo":
                    used.add(st["l2bs"][0] - 1)
            elif ph == "E2":
                used |= set(st["l2bs"])
                br = right_halo_block(st)
                if br is not None:
                    used.add(br)
            elif ph == "H":
                used |= set(st["l2bs"])
            for b in used:
                last_use[(st["si"], b)] = pos
        xq = [(i, st["si"], b) for i, st in enumerate(steps) for b in st["new"]]
        xst = dict(next=0)
        slot_free = [True] * RX

        def try_issue_x(max_step):
            while xst["next"] < len(xq):
                i, si, b = xq[xst["next"]]
                if i > max_step:
                    break
                sl = xslot(si, b)
                if not slot_free[sl]:
                    break
                slot_free[sl] = False
                xo = sinfo[si]["xo"]
                dma("sp", xres[:, sl, :], x_d[(xo + b) * 128:(xo + b + 1) * 128, :], r=(), w=(B_x[sl],))
                xst["next"] += 1

        def release_x(pos):
            for (si, b), lu in last_use.items():
                if lu == pos:
                    slot_free[xslot(si, b)] = True

        for i, st in enumerate(steps):
            st["ri"] = i % 2

        def rmsnorm_to_hb(xslot_i, layer):
            hs = state["hb"]
            state["hb"] = (hs + 1) % 4
            s0 = alloc_stat()
            act(hb[:, hs, :], xres[:, xslot_i, :], AF.Square, r=(B_x[xslot_i],), w=(B_hb[hs], B_stat[s0]),
                accum=stat[:, s0, 0:1])
            act(stat[:, s0, 1:2], stat[:, s0, 0:1], AF.Sqrt, r=(B_const,), w=(B_stat[s0],), scale=1.0 / D, bias=epsb[:, 0:1])
            recip(stat[:, s0, 2:3], stat[:, s0, 1:2], r=(), w=(B_stat[s0],))
            ts("dve", hb[:, hs, :], xres[:, xslot_i, :], stat[:, s0, 2:3], ALU.mult, r=(B_x[xslot_i], B_stat[s0]), w=(B_hb[hs],))
            return hs

        def rope_evac(b0, n, dst4, dst_bufs, rb, ri0, extra_r=()):
            psv = ps[:, b0:b0 + n, :].rearrange("p n (h d) -> p n h d", h=4)
            pbufs = tuple(B_ps[b0 + i] for i in range(n))
            tab = ropet[:, rb, ri0:ri0 + n, :]
            cc = tab[:, :, 0:32].unsqueeze(2).to_broadcast([128, n, 4, 32])
            nsn = tab[:, :, 32:48].unsqueeze(2).to_broadcast([128, n, 4, 16])
            psn = tab[:, :, 48:64].unsqueeze(2).to_broadcast([128, n, 4, 16])
            ta = rtmp[:, 0, 0:n, :].rearrange("p n (h d) -> p n h d", h=4)
            tb = rtmp[:, 1, 0:n, :].rearrange("p n (h d) -> p n h d", h=4)
            if "rope_act" not in SKIP:
                cp("act", dst4[:, :, :, 32:128], psv[:, :, :, 32:128], r=pbufs, w=dst_bufs)
            if "rope_dve" not in SKIP:
                tt("dve", ta, psv[:, :, :, 0:32], cc, ALU.mult, r=pbufs + (B_rope[rb],), w=(B_rtmp[0],))
                tt("dve", tb[:, :, :, 0:16], psv[:, :, :, 16:32], nsn, ALU.mult, r=pbufs + (B_rope[rb],), w=(B_rtmp[1],))
                tt("dve", tb[:, :, :, 16:32], psv[:, :, :, 0:16], psn, ALU.mult, r=pbufs + (B_rope[rb],), w=(B_rtmp[1],))
            if "rope_pool" not in SKIP:
                tt("pool", dst4[:, :, :, 0:32], ta, tb, ALU.add, r=(B_rtmp[0], B_rtmp[1]), w=dst_bufs)

        def kslot(b):
            return b % 8

        def phase_AB(st, inject=None):
            si = st["si"]
            new = st["new"]
            n = len(new)
            rb = st["ri"]
            if st["G"] != 0:
                cp("pool", hT[:, :, 0:128], hT[:, :, 512:640], r=(B_hT[4],), w=(B_hT[0],))
            if not new:
                for ch, pe in (inject or []):
                    ch()
                    pe()
                return
            wk = wget(("k",))
            wv = wget(("v",))
            kb0 = alloc_banks(n)
            inject = list(inject) if inject else []
            hsl = {}

            def chain(i):
                S.tag = "A"
                hsl[i] = rmsnorm_to_hb(xslot(si, new[i]), 0)

            def trans(i):
                S.tag = "A"
                cg = new[i] - st["base"]
                hs = hsl[i]
                bank = alloc_banks(1)
                if kb0 <= bank < kb0 + n:
                    state["bank"] = (kb0 + n) % 8
                    bank = alloc_banks(1)
                pb = psbf(bank)
                for kc in range(8):
                    tr(pb[:, kc * 128:(kc + 1) * 128], hb[:, hs, kc * 128:(kc + 1) * 128], ident[:], r=(B_hb[hs], B_const), w=(B_ps[bank],))
                tt("dve", hT[:, :, cg * 128:(cg + 1) * 128], pb.rearrange("p (c t) -> p c t", c=8),
                   gpre[:, 0, :].unsqueeze(2).to_broadcast([128, 8, 128]), ALU.mult, r=(B_ps[bank], B_const), w=(B_hT[cg],))

            chain(0)
            if n > 1:
                chain(1)
            for i in range(n):
                trans(i)
                if i + 2 < n:
                    chain(i + 2)
                if i >= 1:
                    if inject and i + 2 >= n:
                        ch, pe = inject.pop(0)
                        ch()
                        kv_block(st, i - 1, kb0, n, wk, wv)
                        pe()
                    else:
                        kv_block(st, i - 1, kb0, n, wk, wv)
            if inject:
                ch, pe = inject.pop(0)
                ch()
                kv_block(st, n - 1, kb0, n, wk, wv)
                pe()
            else:
                kv_block(st, n - 1, kb0, n, wk, wv)
            for ch, pe in inject:
                ch()
                pe()
            wrel(2)
            S.tag = "B"
            rope_evac(kb0, n, ktok[:, 0:n, :].rearrange("p n (h d) -> p n h d", h=4), (B_ktok,), rb, new[0] - st["base"])
            for i0 in range(0, n, 2):
                bank = alloc_banks(1)
                pb = psbf(bank)
                m = min(2, n - i0)
                for ii in range(m):
                    for h in range(4):
                        tr(pb[:, ii * 512 + h * 128: ii * 512 + (h + 1) * 128], ktok[:, i0 + ii, h * 128:(h + 1) * 128], ident[:],
                           r=(B_ktok, B_const), w=(B_ps[bank],))
                for ii in range(m):
                    sl = kslot(new[i0 + ii])
                    cp("dve", kT[:, sl, :], pb[:, ii * 512:(ii + 1) * 512], r=(B_ps[bank],), w=(B_kT[sl],))

        def kv_block(st, i, kb0, n, wk, wv):
            S.tag = "B"
            b = st["new"][i]
            cg = b - st["base"]
            for kc in range(8):
                mm(ps[:, kb0 + i, :], hT[:, kc, cg * 128:(cg + 1) * 128], wring[:, wk, kc, :], kc == 0, kc == 7,
                   r=(B_hT[cg], B_w[wk]), w=(B_ps[kb0 + i],))
            bank = alloc_banks(1)
            if kb0 <= bank < kb0 + n:
                state["bank"] = (kb0 + n) % 8
                bank = alloc_banks(1)
            for kc in range(8):
                mm(ps[:, bank, :], hT[:, kc, cg * 128:(cg + 1) * 128], wring[:, wv, kc, :], kc == 0, kc == 7,
                   r=(B_hT[cg], B_w[wv]), w=(B_ps[bank],))
            sl = kslot(b)
            cp("act", vv[:, sl, :], ps[:, bank, :], r=(B_ps[bank],), w=(B_v[sl],))

        def phase_C(st):
            si = st["si"]
            sg = sinfo[si]
            kind, n_ext = sg["kind"], sg["n_ext"]
            qbs = st["qbs"]
            n = len(qbs)
            rb = st["ri"]
            cg0 = qbs[0] - st["base"]

            held = {}

            def Qpart(j, i):
                jb = j % 2
                S.tag = "C-q"
                if ("q", j) not in held:
                    held[("q", j)] = wget(("q", j))
                ws = held[("q", j)]
                b0 = alloc_banks(1)
                cg = qbs[i] - st["base"]
                for kc in range(8):
                    mm(ps[:, b0, :], hT[:, kc, cg * 128:(cg + 1) * 128], wring[:, ws, kc, :], kc == 0, kc == 7,
                       r=(B_hT[cg], B_w[ws]), w=(B_ps[b0],))
                rope_evac(b0, 1, qtok[:, jb, i:i + 1, :].rearrange("p n (h d) -> p n h d", h=4), (B_qtok[jb],), rb, cg)

            def Gpart(j, m):
                S.tag = "C-g"
                if ("g", j) not in held:
                    held[("g", j)] = wget(("g", j))
                ws = held[("g", j)]
                bank = alloc_banks(1)
                for kc in range(8):
                    mm(ps[:, bank, 0:n * 128], wring[:, ws, kc, m * 128:(m + 1) * 128], hT[:, kc, cg0 * 128:(cg0 + n) * 128],
                       kc == 0, kc == 7, r=tuple(B_hT[cg0 + i] for i in range(n)) + (B_w[ws],), w=(B_ps[bank],))
                gdst = aT[:, 4 * j + m, 0:n * 128]
                act(gdst, ps[:, bank, 0:n * 128], AF.Tanh, r=(B_ps[bank],), w=tuple(B_aT[i] for i in range(n)), scale=0.5)
                stt(gdst, gdst, 1.0, ps[:, bank, 0:n * 128], ALU.add, ALU.mult, r=(B_ps[bank],), w=tuple(B_aT[i] for i in range(n)))

            def Qp(j):
                jb = j % 2
                S.tag = "C-q"
                ws = wget(("q", j))
                b0 = alloc_banks(n)
                for i, b in enumerate(qbs):
                    cg = b - st["base"]
                    for kc in range(8):
                        mm(ps[:, b0 + i, :], hT[:, kc, cg * 128:(cg + 1) * 128], wring[:, ws, kc, :], kc == 0, kc == 7,
                           r=(B_hT[cg], B_w[ws]), w=(B_ps[b0 + i],))
                wrel()
                rope_evac(b0, n, qtok[:, jb, 0:n, :].rearrange("p n (h d) -> p n h d", h=4), (B_qtok[jb],), rb, cg0)

            def Gp(j):
                for m in range(4):
                    Gpart(j, m)
                wrel()

            def Tq(j):
                jb = j % 2
                S.tag = "C-T"
                for i0 in range(0, n, 2):
                    bank = alloc_banks(1)
                    pb = psbf(bank)
                    m2 = min(2, n - i0)
                    for ii in range(m2):
                        for h in range(4):
                            tr(pb[:, ii * 512 + h * 128: ii * 512 + (h + 1) * 128], qtok[:, jb, i0 + ii, h * 128:(h + 1) * 128], ident[:],
                               r=(B_qtok[jb], B_const), w=(B_ps[bank],))
                    cp("dve", qT[:, jb, i0:i0 + m2, :], pb[:, 0:m2 * 512].rearrange("p (n c) -> p n c", n=m2), r=(B_ps[bank],), w=(B_qT[jb],))

            def key_blocks(b):
                kbs = []
                if b - 1 >= 0:
                    kbs.append((b - 1, 2 if (kind == "halo" and b == 2) else 0))
                kbs.append((b, None))
                if b + 1 <= n_ext - 1:
                    kbs.append((b + 1, 3 if (kind == "halo" and b == n_ext - 3) else 1))
                return kbs

            def Sst(j, i):
                S.tag = "C-S"
                jb = j % 2
                pbi = (j * n + i) % 2
                kbs = key_blocks(qbs[i])
                nk = len(kbs)
                b0 = alloc_banks(nk)
                for kidx, (kb, mi) in enumerate(kbs):
                    bank = b0 + kidx
                    sl = kslot(kb)
                    mm(ps[:, bank, :], kT[:, sl, j * 128:(j + 1) * 128], qT[:, jb, i, :], True, mi is None,
                       r=(B_kT[sl], B_qT[jb]), w=(B_ps[bank],))
                    if mi is not None:
                        mm(ps[:, bank, :], ident[:], masks[:, mi, :].unsqueeze(1).to_broadcast([128, 4, 128]), False, True, r=(B_const,), w=(B_ps[bank],))
                act(pT[:, pbi, 0:nk, :], ps[:, b0:b0 + nk, :], AF.Exp, r=tuple(B_ps[b0 + k] for k in range(nk)), w=(B_pT[pbi],), scale=QSCALE)

            def PVs(j, i):
                S.tag = "C-PV"
                pbi = (j * n + i) % 2
                kbs = key_blocks(qbs[i])
                nk = len(kbs)
                bd = alloc_banks(1)
                for kidx in range(nk):
                    mm(ps[:, bd, :], ones[:], pT[:, pbi, kidx, :], kidx == 0, False,
                       r=(B_const, B_pT[pbi]), w=(B_ps[bd],))
                mm(ps[:, bd, :], ones[0:1, :], esrow[0:1, 4 * j:4 * j + 4].unsqueeze(2).to_broadcast([1, 4, 128]), False, True, r=(B_const,), w=(B_ps[bd],))
                bo = alloc_banks(1)
                for kidx, (kb, mi) in enumerate(kbs):
                    sl = kslot(kb)
                    mm(ps[:, bo, :], vv[:, sl, j * 128:(j + 1) * 128], pT[:, pbi, kidx, :], kidx == 0, kidx == nk - 1,
                       r=(B_v[sl], B_pT[pbi]), w=(B_ps[bo],))
                rd = rden[:, pbi, :]
                rd3 = rd.rearrange("p (h t) -> p h t", h=4)
                ot3 = otmp[:, pbi, :].rearrange("p (h t) -> p h t", h=4)
                av = aT[:, 4 * j:4 * j + 4, i * 128:(i + 1) * 128]
                tt("dve", ot3, ps[:, bo, :].rearrange("p (h t) -> p h t", h=4), av, ALU.mult, r=(B_ps[bo], B_aT[i]), w=(B_otmp[pbi],))
                recip(rd, ps[:, bd, :], r=(B_ps[bd],), w=(B_rden[pbi],))
                tt("pool", av, ot3, rd3, ALU.mult, r=(B_otmp[pbi], B_rden[pbi]), w=(B_aT[i],))

            Qp(0)
            Gp(0)
            Tq(0)
            Qp(1)
            items = [(j, i) for j in range(4) for i in range(n)]
            for idx in range(len(items) + 1):
                if idx < len(items):
                    j, i = items[idx]
                    Sst(j, i)
                    if i == min(1, n - 1) and j + 1 < 4:
                        Gp(j + 1)
                        Tq(j + 1)
                    if i == n - 1 and j + 2 < 4:
                        Qp(j + 2)
                if idx >= 1:
                    PVs(*items[idx - 1])

        def out_proj(st, blocks, layer, okey, after_block=None):
            si = st["si"]
            slots = {}
            for h in range(2):
                for kk in range(2):
                    slots[(h, kk)] = wget((okey, h, kk))
            for i, b in enumerate(blocks):
                S.tag = "D" if layer == 0 else "H"
                sl = xslot(si, b)
                b0 = alloc_banks(2)
                for h in range(2):
                    for c in range(16):
                        ws = slots[(h, c // 8)]
                        mm(ps[:, b0 + h, :], aT[:, c, i * 128:(i + 1) * 128], wring[:, ws, c % 8, :], c == 0, c == 15,
                           r=(B_aT[i], B_w[ws]), w=(B_ps[b0 + h],))
                if after_block is not None:
                    after_block(i)
                pv = ps[:, b0:b0 + 2, :].rearrange("p a b -> p (a b)")
                pbufs = (B_ps[b0], B_ps[b0 + 1])
                s0 = alloc_stat()
                act(otmp[:, 0, :].bitcast(BF16), pv, AF.Square, r=pbufs, w=(B_otmp[0], B_stat[s0]), accum=stat[:, s0, 0:1])
                act(stat[:, s0, 1:2], stat[:, s0, 0:1], AF.Sqrt, r=(B_const,), w=(B_stat[s0],), scale=1.0 / D, bias=epsb[:, 0:1])
                recip(stat[:, s0, 2:3], stat[:, s0, 1:2], r=(), w=(B_stat[s0],))
                tt("dve", pv, pv, gpost[:, layer, :], ALU.mult, r=(B_const,), w=pbufs)
                stt(xres[:, sl, :], pv, stat[:, s0, 2:3], xres[:, sl, :], ALU.mult, ALU.add, r=pbufs + (B_stat[s0],), w=(B_x[sl],))
            wrel(4)

        def E_pe(hs, col0, ncols, first_tok=None):
            g1 = gpre[:, 1, :]
            bank = alloc_banks(1)
            pb = psbf(bank)
            if first_tok == "head8":
                for kc in range(8):
                    tr(pb[:, kc * 8:(kc + 1) * 8], hb[0:8, hs, kc * 128:(kc + 1) * 128], ident[0:8, 0:8], r=(B_hb[hs], B_const), w=(B_ps[bank],))
                src = pb[:, 0:64].rearrange("p (c t) -> p c t", c=8)
            else:
                for kc in range(8):
                    tr(pb[:, kc * 128:(kc + 1) * 128], hb[:, hs, kc * 128:(kc + 1) * 128], ident[:], r=(B_hb[hs], B_const), w=(B_ps[bank],))
                src = pb.rearrange("p (c t) -> p c t", c=8)
                if first_tok == "tail8":
                    src = src[:, :, 120:128]
            tt("dve", h2T[:, :, col0:col0 + ncols], src, g1.unsqueeze(2).to_broadcast([128, 8, ncols]), ALU.mult,
               r=(B_ps[bank], B_const), w=(B_h2T,))

        def E_tr_block(sl, col0, ncols, first_tok=None):
            hs = rmsnorm_to_hb(sl, 1)
            E_pe(hs, col0, ncols, first_tok)

        def E2_units(st):
            si = st["si"]
            l2 = st["l2bs"]
            n = len(l2)
            units = []
            while st["e_done"] < n:
                i = st["e_done"]
                st["e_done"] += 1
                box = {}

                def ch(i=i, box=box):
                    S.tag = "E"
                    box["hs"] = rmsnorm_to_hb(xslot(si, l2[i]), 1)

                def pe(i=i, box=box):
                    S.tag = "E"
                    E_pe(box["hs"], 8 + i * 128, 128)

                units.append((ch, pe))
            c0 = 8 + n * 128
            br = right_halo_block(st)
            box = {}

            def ch_r(box=box):
                S.tag = "E"
                if br is not None:
                    box["hs"] = rmsnorm_to_hb(xslot(si, br), 1)

            def pe_r(box=box):
                S.tag = "E"
                if br is not None:
                    E_pe(box["hs"], c0, 8, "head8")
                else:
                    memset("pool", h2T[:, :, c0:c0 + 8], 0.0, w=(B_h2T,))
                if not st["l2last"]:
                    cp("pool", h2halo[:], h2T[:, :, c0 - 8:c0], r=(B_h2T,), w=(B_h2halo,))

            units.append((ch_r, pe_r))
            return units

        def E_left(st):
            si = st["si"]
            kind = sinfo[si]["kind"]
            l2 = st["l2bs"]
            S.tag = "E"
            if st["l2first"]:
                if kind == "halo":
                    E_tr_block(xslot(si, l2[0] - 1), 0, 8, "tail8")
                else:
                    memset("pool", h2T[:, :, 0:8], 0.0, w=(B_h2T,))
            else:
                cp("pool", h2T[:, :, 0:8], h2halo[:], r=(B_h2halo,), w=(B_h2T,))

        def E_block(st, i):
            S.tag = "E"
            E_tr_block(xslot(st["si"], st["l2bs"][i]), 8 + i * 128, 128)

        def phase_DE(st):
            qbs, l2 = st["qbs"], st["l2bs"]
            st["e_done"] = 0
            if l2:
                E_left(st)

            def ready(bq, upto):
                return (bq not in qbs) or (qbs.index(bq) <= upto)

            pend = []

            def hook(idx):
                while pend:
                    hs, i = pend.pop(0)
                    S.tag = "E"
                    E_pe(hs, 8 + i * 128, 128)
                if l2 and st["e_done"] < len(l2) and ready(l2[st["e_done"]], idx - 1):
                    i = st["e_done"]
                    st["e_done"] += 1
                    S.tag = "E"
                    pend.append((rmsnorm_to_hb(xslot(st["si"], l2[i]), 1), i))

            if l2:
                hook(-1)
            out_proj(st, qbs, 0, "o1", after_block=hook if l2 else None)
            while pend:
                hs, i = pend.pop(0)
                S.tag = "E"
                E_pe(hs, 8 + i * 128, 128)

        def phase_E2(st):
            for ch, pe in E2_units(st):
                ch()
                pe()

        def phase_F(st):
            si = st["si"]
            sg = sinfo[si]
            kidx = 0 if sg["kind"] == "halo" else 1
            l2 = st["l2bs"]
            n = len(l2)
            N = n * 128
            W = N + 16
            if W <= 512:
                ranges = [(0, W)]
            else:
                ranges = [(0, W // 2), (W // 2, W)]
            cnt = dict(k=0)

            def G2p(g):
                S.tag = "F-g2"
                wg = wget(("g2", g))
                for m in range(4):
                    bank = alloc_banks(1)
                    for kc in range(8):
                        mm(ps[:, bank, 0:N], wring[:, wg, kc, m * 128:(m + 1) * 128], h2T[:, kc, 8:8 + N], kc == 0, kc == 7,
                           r=(B_h2T, B_w[wg]), w=(B_ps[bank],))
                    act(aT[:, 4 * g + m, 0:N], ps[:, bank, 0:N], AF.Silu, r=(B_ps[bank],), w=tuple(B_aT[i] for i in range(n)))
                wrel()

            def Up(g):
                S.tag = "F-u"
                wu = wget(("u", g))
                for m in range(4):
                    for (c0, c1) in ranges:
                        bank = alloc_banks(1)
                        for kc in range(8):
                            mm(ps[:, bank, 0:c1 - c0], wring[:, wu, kc, m * 128:(m + 1) * 128], h2T[:, kc, c0:c1], kc == 0, kc == 7,
                               r=(B_h2T, B_w[wu]), w=(B_ps[bank],))
                        cp("act", uT[:, m, c0:c1], ps[:, bank, 0:c1 - c0], r=(B_ps[bank],), w=(B_uT[m],))
                    if m == 3:
                        wrel()
                    U = uT[:, m, :]
                    Bu = B_uT[m]
                    pa = ptmp[:, 0, :]
                    pb_ = ptmp[:, 1, :]
                    Ba, Bb = B_ptmp
                    w_ = 2 ** (g + 1)
                    if g == 0:
                        tt("pool", pa[:, 0:N], U[:, 7:7 + N], U[:, 8:8 + N], ALU.add, r=(Bu,), w=(Ba,))
                        box, Bbox = pa, Ba
                    elif g == 1:
                        tt("pool", pa[:, 0:N + 2], U[:, 6:8 + N], U[:, 7:9 + N], ALU.add, r=(Bu,), w=(Ba,))
                        tt("pool", pb_[:, 0:N], pa[:, 0:N], pa[:, 2:N + 2], ALU.add, r=(Ba,), w=(Bb,))
                        box, Bbox = pb_, Bb
                    elif g == 2:
                        tt("pool", pa[:, 0:N + 6], U[:, 4:10 + N], U[:, 5:11 + N], ALU.add, r=(Bu,), w=(Ba,))
                        tt("pool", pb_[:, 0:N + 4], pa[:, 0:N + 4], pa[:, 2:N + 6], ALU.add, r=(Ba,), w=(Bb,))
                        tt("pool", pa[:, 0:N], pb_[:, 0:N], pb_[:, 4:N + 4], ALU.add, r=(Bb,), w=(Ba,))
                        box, Bbox = pa, Ba
                    else:
                        tt("pool", pa[:, 0:N + 14], U[:, 0:N + 14], U[:, 1:N + 15], ALU.add, r=(Bu,), w=(Ba,))
                        tt("pool", pb_[:, 0:N + 12], pa[:, 0:N + 12], pa[:, 2:N + 14], ALU.add, r=(Ba,), w=(Bb,))
                        tt("pool", pa[:, 0:N + 8], pb_[:, 0:N + 8], pb_[:, 4:N + 12], ALU.add, r=(Bb,), w=(Ba,))
                        tt("pool", pb_[:, 0:N], pa[:, 0:N], pa[:, 8:N + 8], ALU.add, r=(Ba,), w=(Bb,))
                        box, Bbox = pb_, Bb
                    pbi = cnt["k"] % 2
                    cnt["k"] += 1
                    pm = pmix[:, pbi, :]
                    stt(pm[:, 0:N], box[:, 0:N], 1.0 / w_, U[:, 8:8 + N], ALU.mult, ALU.subtract, r=(Bbox, Bu), w=(B_pmix[pbi],))
                    for (flag, fl, cs) in ((st["l2first"], 0, 0), (st["l2last"], 1, N - 8)):
                        if flag:
                            tt("dve", pfix[:, :], box[:, cs:cs + 8], icnt[:, kidx, fl, g, :], ALU.mult, r=(Bbox, B_const), w=(B_pfix,))
                            tt("dve", pm[:, cs:cs + 8], pfix[:, :], U[:, 8 + cs:16 + cs], ALU.subtract,
                               r=(B_pfix, Bu), w=(B_pmix[pbi],))
                    c = 4 * g + m
                    stt(aT[:, c, 0:N], pm[:, 0:N], pscale[:, c:c + 1], aT[:, c, 0:N], ALU.mult, ALU.mult,
                        r=(B_pmix[pbi], B_const), w=tuple(B_aT[i] for i in range(n)))

            for g in (3, 2, 1, 0):
                G2p(g)
                Up(g)

        def phase_store(st):
            si = st["si"]
            sg = sinfo[si]
            for b in st["l2bs"]:
                sl = xslot(si, b)
                ob = b - 2 if sg["kind"] == "halo" else b
                row = (sg["yo"] + ob) * 128
                dma("sp", y_d[row:row + 128, :], xres[:, sl, :], r=(B_x[sl],), w=())

        dbg_outs = {}

        def dbg(name, ap, shape, dt, bufs):
            if not debug or name in dbg_outs:
                return
            t = nc.dram_tensor("dbg_" + name, list(shape), dt, kind="ExternalOutput").ap()
            dbg_outs[name] = t
            dma("sp", t, ap, r=bufs, w=())

        def prologue_fuse():
            S.tag = "P"
            vin2_f = w_in2.rearrange("(kc p) n -> p kc n", p=128)
            vgrp_f = w_grp.rearrange("(kc p) n -> p kc n", p=128)
            sA, sB, sC, sD = 0, 1, 2, 3
            wC = wring[:, sC, :, :].rearrange("p a b -> p (a b)").rearrange("p (cc r) -> p cc r", cc=4)
            for g in range(4):
                B_cast[("u", g)] = Buf("fused")
                dma("pool", wring[:, sA, :, :], vin2_f[:, :, 512 * g:512 * g + 512], r=(), w=(B_w[sA],))
                dma("pool", wring[:, sB, 0:4, :], vgrp_f[:, 4 * g:4 * g + 4, :], r=(), w=(B_w[sB],))
                for cc in range(4):
                    bank = alloc_banks(1)
                    pb = psbf(bank)
                    for m in range(8):
                        tr(pb[:, m * 128:(m + 1) * 128], wring[:, sA, m, cc * 128:(cc + 1) * 128], ident[:], r=(B_w[sA], B_const), w=(B_ps[bank],))
                    cp("dve" if cc % 2 else "act", wC[:, cc, :], pb[:, :], r=(B_ps[bank],), w=(B_w[sC],))
                for m in range(8):
                    bank = alloc_banks(1)
                    for cc in range(4):
                        mm(ps[:, bank, :], wC[:, cc, m * 128:(m + 1) * 128], wring[:, sB, cc, :], cc == 0, cc == 3,
                           r=(B_w[sC], B_w[sB]), w=(B_ps[bank],))
                    cp("dve" if m % 2 else "act", wring[:, sD, m, :], ps[:, bank, :], r=(B_ps[bank],), w=(B_w[sD],))
                dma("sp", v_in2[:, :, 512 * g:512 * g + 512], wring[:, sD, :, :], r=(B_w[sD],), w=(B_cast[("u", g)],))

        prologue_fuse()

        w_issue_upto(NW)
        issue_rope(steps[0])
        if len(steps) > 1:
            issue_rope(steps[1])
        try_issue_x(1)
        nph = 0
        for pos, (ph, i) in enumerate(sched):
            if limit is not None and nph >= limit:
                break
            nph += 1
            st = steps[i]
            S.tag = ph
            if ph == "AB":
                if st["new"]:
                    need_upto = max(k for k, (ii, _, _) in enumerate(xq) if ii == i)
                    assert xst["next"] > need_upto, ("x load not issued in time", i)
                inj = E2_units(steps[i - 1]) if (i >= 1 and steps[i - 1]["l2bs"]) else None
                phase_AB(st, inj)
                dbg("hT", hT[:], [128, 8, 640], BF16, tuple(B_hT))
                dbg("kT", kT[:], [128, 8, 512], BF16, tuple(B_kT))
                dbg("vv", vv[:], [128, 8, 512], BF16, tuple(B_v))
                dbg("ktok", ktok[:], [128, 4, 512], BF16, (B_ktok,))
            elif ph == "C":
                phase_C(st)
                dbg("qtok", qtok[:], [128, 2, 4, 512], BF16, tuple(B_qtok))
                dbg("qT", qT[:], [128, 2, 4, 512], BF16, tuple(B_qT))
                dbg("aT", aT[:], [128, 16, 512], BF16, tuple(B_aT))
            elif ph == "DE":
                phase_DE(st)
                if debug:
                    for b in st["qbs"]:
                        sl = xslot(st["si"], b)
                        row = (sinfo[st["si"]]["xo"] + b) * 128
                        dma("sp", dbg_d[row:row + 128, :], xres[:, sl, :], r=(B_x[sl],), w=())
            elif ph == "E2":
                phase_E2(st)
            elif ph == "F":
                phase_F(st)
            elif ph == "H":
                out_proj(st, st["l2bs"], 1, "o2")
                phase_store(st)
            elif ph == "END":
                if i + 2 < len(steps):
                    issue_rope(steps[i + 2])
            release_x(pos)
            try_issue_x(i + 2)
        last = [o for o in S.q["sp"] if o.dma]
        tail = last[-S.NDS["sp"]:]
        fin = Buf("fin")
        for o in tail:
            fin.r.append(o)
        S.op("sp", lambda E: E.nop(), r=(), w=(fin,))
        assert limit is not None or wst["used"] == len(allp) == wst["rel"], (wst, len(allp))

        need = S.emit(nc, None, None, None)
        sem_stack = ExitStack()
        with sem_stack:
            esems = {e: [sem_stack.enter_context(nc.semaphore(f"s_{e}_{k}")) for k in range(need[e])] for e in S.ENGS}
            dsems = {e: [sem_stack.enter_context(nc.semaphore(f"d_{e}_{k}")) for k in range(n)] for e, n in S.NDS.items()}
            with nc.Block() as block:
                @block.tensor
                def _(E):
                    S.run_engine("pe", E, esems, dsems)

                @block.scalar
                def _(E):
                    S.run_engine("act", E, esems, dsems)

                @block.vector
                def _(E):
                    S.run_engine("dve", E, esems, dsems)

                @block.gpsimd
                def _(E):
                    S.run_engine("pool", E, esems, dsems)

                @block.sync
                def _(E):
                    S.run_engine("sp", E, esems, dsems)
    counts = {e: len(S.q[e]) for e in S.ENGS}
    counts["tags"] = {e: [o.tag for o in S.q[e]] for e in S.ENGS}
    return nc, counts


def rope_table(pos):
    half = 16
    inv_freq = (np.float32(ROPE_THETA) ** (-(np.arange(half, dtype=np.float32) * np.float32(2.0) / np.float32(32)))).astype(np.float32)
    ang = (pos.astype(np.float32)[:, None] * inv_freq[None, :]).astype(np.float32)
    c = np.cos(ang).astype(np.float32)
    s = np.sin(ang).astype(np.float32)
    return np.concatenate([c, c, -s, s], axis=1).astype(np.float32)


def inv_counts(S_len, first):
    out = np.zeros((4, 8), np.float32)
    for g, w in enumerate((2, 4, 8, 16)):
        for i in range(8):
            t = i if first else S_len - 8 + i
            lo = max(t - w // 2, 0)
            hi = min(t + w // 2 - 1, S_len - 1)
            out[g, i] = 1.0 / float(hi - lo + 1)
    return out


def tri_masks():
    kl = np.arange(128)[:, None]
    ql = np.arange(128)[None, :]
    NEG = np.float32(-30000.0)
    triL = np.where(ql <= kl, np.float32(0.0), NEG).astype(np.float32)
    triU = np.where(kl <= ql, np.float32(0.0), NEG).astype(np.float32)
    return triL, triU


def core_inputs(seg_specs, params):
    xs, ropes = [], []
    triL, triU = tri_masks()
    m = np.zeros((128, 4, 128), np.float32)
    m[:, 0] = triL
    m[:, 1] = triU
    ic = np.zeros((2, 2, 4, 8), np.float32)
    for kind, xseq, a, b in seg_specs:
        S_len = xseq.shape[0]
        if kind == "halo":
            lo, hi = a - 256, b + 256
            xe = np.zeros((hi - lo, D), np.float32)
            s0, s1 = max(lo, 0), min(hi, S_len)
            xe[s0 - lo:s1 - lo] = xseq[s0:s1]
            pos = np.clip(np.arange(lo, hi), 0, S_len - 1)
            m[:, 2] = triL if a > 0 else -30000.0
            m[:, 3] = triU if b < S_len else -30000.0
            wv = np.array([[1.0 / w] * 8 for w in (2, 4, 8, 16)], np.float32)
            ic[0, 0] = inv_counts(S_len, True) if a == 0 else wv
            ic[0, 1] = inv_counts(S_len, False) if b == S_len else wv
        else:
            xe = xseq[a:b]
            pos = np.arange(a, b)
            ic[1, 0] = inv_counts(S_len, True)
            ic[1, 1] = inv_counts(S_len, False)
        xs.append(xe)
        ropes.append(rope_table(pos))
    d = dict(params)
    d["x"] = np.ascontiguousarray(np.concatenate(xs, 0))
    d["rope"] = np.ascontiguousarray(np.concatenate(ropes, 0))
    d["masks"] = m.astype(ml_dtypes.bfloat16)
    d["icnt"] = np.ascontiguousarray(np.broadcast_to(ic[None], (128, 2, 2, 4, 8))).astype(np.float32)
    return d


def shared_params(norm_pre, norm_post, attn_w_in, attn_sink, attn_w_out, pool_w_in, pool_w_group, pool_scale, pool_w_out):
    p = {}
    p["ident"] = np.eye(128, dtype=np.float32).astype(ml_dtypes.bfloat16)
    p["gpre"] = np.ascontiguousarray(np.asarray(norm_pre, np.float32).reshape(2, 8, 128).transpose(2, 0, 1))
    p["gpost"] = np.ascontiguousarray(np.broadcast_to(np.asarray(norm_post, np.float32)[None], (128, 2, D)))
    p["sinkb"] = np.ascontiguousarray(np.broadcast_to(np.asarray(attn_sink, np.float32).reshape(1, NH), (128, NH)))
    p["pscale"] = np.ascontiguousarray(np.asarray(pool_scale, np.float32).reshape(16, 128).T)
    p["attn_w_in"] = np.ascontiguousarray(np.asarray(attn_w_in, np.float32).reshape(D, ATT_IN))
    p["attn_w_out"] = np.ascontiguousarray(np.asarray(attn_w_out, np.float32).reshape(BW, D))
    p["pool_w_in"] = np.ascontiguousarray(np.asarray(pool_w_in, np.float32).reshape(D, POOL_IN))
    p["pool_w_group"] = np.ascontiguousarray(np.asarray(pool_w_group, np.float32).reshape(BW, 512))
    p["pool_w_out"] = np.ascontiguousarray(np.asarray(pool_w_out, np.float32).reshape(BW, D))
    return p


_PROG_CACHE = {}


def get_program(segs):
    key = tuple(segs)
    if key not in _PROG_CACHE:
        _PROG_CACHE[key] = build_program(list(segs))
    return _PROG_CACHE[key]


def kernel(x_prompt, x_sample, norm_pre, norm_post, attn_w_in, attn_sink, attn_w_out,
           pool_w_in, pool_w_group, pool_scale, pool_w_out):
    x_prompt = np.asarray(x_prompt, np.float32)
    x_sample = np.asarray(x_sample, np.float32)
    n_cores = 8
    PB, PS = x_prompt.shape[0], x_prompt.shape[1]
    SB, SS = x_sample.shape[0], x_sample.shape[1]
    per = n_cores // PB
    q_len = PS // per
    s_per = SB // n_cores
    segs = [("halo", q_len // 128 + 4)] + [("full", SS // 128)] * s_per
    nc, _ = get_program(tuple(segs))
    params = shared_params(norm_pre, norm_post, attn_w_in, attn_sink, attn_w_out, pool_w_in, pool_w_group, pool_scale, pool_w_out)
    in_maps = []
    for c in range(n_cores):
        pb, qi = c // per, c % per
        specs = [("halo", x_prompt[pb], qi * q_len, (qi + 1) * q_len)]
        for k in range(s_per):
            specs.append(("full", x_sample[c * s_per + k], 0, SS))
        in_maps.append(core_inputs(specs, params))
    res = run_bass_kernel_spmd(nc, in_maps, core_ids=list(range(n_cores)))
    y_prompt = np.empty_like(x_prompt)
    y_sample = np.empty_like(x_sample)
    for c in range(n_cores):
        y = res.results[c]["y"]
        pb, qi = c // per, c % per
        y_prompt[pb, qi * q_len:(qi + 1) * q_len] = y[:q_len]
        for k in range(s_per):
            y_sample[c * s_per + k] = y[q_len + k * SS: q_len + (k + 1) * SS]
    return (y_prompt, y_sample)
```

```python
import math
import numpy as np
import ml_dtypes
import concourse.bass as bass
import concourse.mybir as mybir
from concourse.bass_utils import run_bass_kernel_spmd

F32 = mybir.dt.float32
BF16 = mybir.dt.bfloat16
AF = mybir.ActivationFunctionType
ALU = mybir.AluOpType

D = 1024
BW = 2048
NH = 16
NKV = 4
HD = 128
ATT_IN = 5120
POOL_IN = 4096
EPS = 1e-6
ROPE_THETA = 500000.0
QSCALE = 1.0 / math.sqrt(128.0)

RX = 10
NW = 5
EPOCH = 4000
SKIP = set()


class Buf:
    __slots__ = ("name", "w", "r", "psum")

    def __init__(self, name, psum=False):
        self.name = name
        self.w = None
        self.r = []
        self.psum = psum


class Op:
    __slots__ = ("eng", "fn", "deps", "inc", "order", "tick", "dma", "sem_i", "tgt", "prev", "tag")


class Sched:
    ENGS = ("pe", "act", "dve", "pool", "sp")
    NDS = {"sp": 8, "pool": 8, "act": 2}

    def __init__(self):
        self.q = {e: [] for e in self.ENGS}
        self.order = 0
        self.dma_cnt = {e: 0 for e in self.ENGS}
        self.dma_last = {}
        self.tag = ""

    def op(self, eng, fn, r=(), w=(), dma=False):
        o = Op()
        o.tag = self.tag
        o.eng = eng
        o.fn = fn
        o.inc = False
        o.dma = dma
        o.order = self.order
        self.order += 1
        o.prev = None
        if dma:
            n = self.NDS[eng]
            o.sem_i = self.dma_cnt[eng] % n
            self.dma_cnt[eng] += 1
            key = (eng, o.sem_i)
            o.prev = self.dma_last.get(key)
            o.tgt = (o.prev.tgt if o.prev is not None else 0) + 16
            self.dma_last[key] = o
        best = {}

        def add(d):
            if d is None or d is o:
                return
            if d.dma:
                key = ("d", d.eng, d.sem_i)
            else:
                if d.eng == "pe" and eng == "pe" and not dma:
                    return
                key = ("e", d.eng)
            c = best.get(key)
            if c is None or d.order > c.order:
                best[key] = d

        for b in r:
            add(b.w)
            if b.psum:
                for x in b.r:
                    if x.eng != eng:
                        add(x)
        for b in w:
            add(b.w)
            for x in b.r:
                add(x)
        o.deps = list(best.values())
        for d in o.deps:
            d.inc = True
        for b in r:
            b.r.append(o)
        for b in w:
            b.w = o
            b.r = []
        self.q[eng].append(o)
        return o

    def emit(self, nc, engines, esems, dsems):
        for e in self.ENGS:
            c = 0
            for o in self.q[e]:
                if o.inc and not o.dma:
                    c += 1
                    o.tick = c
        need = {e: 1 for e in self.ENGS}
        for e in self.ENGS:
            c = sum(1 for o in self.q[e] if o.inc and not o.dma)
            need[e] = max(1, (c + EPOCH - 1) // EPOCH)
        return need

    def run_engine(self, eng, E, esems, dsems):
        seen = {}

        def wait(key, sem, val):
            if seen.get(key, 0) >= val:
                return
            seen[key] = val
            E.wait_ge(sem, val)

        for o in self.q[eng]:
            if o.dma and o.prev is not None:
                wait(("d", o.eng, o.sem_i), dsems[o.eng][o.sem_i], o.prev.tgt)
            for d in o.deps:
                if d.dma:
                    wait(("d", d.eng, d.sem_i), dsems[d.eng][d.sem_i], d.tgt)
                else:
                    ep = (d.tick - 1) // EPOCH
                    wait(("e", d.eng, ep), esems[d.eng][ep], (d.tick - 1) % EPOCH + 1)
            ins = o.fn(E)
            if o.dma:
                ins.then_inc(dsems[o.eng][o.sem_i], 16)
            elif o.inc:
                ep = (o.tick - 1) // EPOCH
                ins.then_inc(esems[eng][ep], 1)


def seg_info(segs):
    out = []
    xo = 0
    yo = 0
    for kind, n_ext in segs:
        n_own = n_ext - 4 if kind == "halo" else n_ext
        out.append(dict(kind=kind, n_ext=n_ext, n_own=n_own, xo=xo, yo=yo))
        xo += n_ext
        yo += n_own
    return out, xo, yo


def build_program(segs, debug=False, limit=None):
    S = Sched()
    sinfo, NEXT, NOWN = seg_info(segs)
    nc = bass.Bass("TRN2", target_bir_lowering=False)

    def din(name, shape, dt):
        return nc.dram_tensor(name, list(shape), dt, kind="ExternalInput").ap()

    x_d = din("x", [NEXT * 128, D], F32)
    rope_d = din("rope", [NEXT * 128, 64], F32)
    masks_d = din("masks", [128, 4, 128], BF16)
    ident_d = din("ident", [128, 128], BF16)
    gpre_d = din("gpre", [128, 2, 8], F32)
    gpost_d = din("gpost", [128, 2, D], F32)
    sink_d = din("sinkb", [128, NH], F32)
    pscale_d = din("pscale", [128, 16], F32)
    icnt_d = din("icnt", [128, 2, 2, 4, 8], F32)
    w_in1 = din("attn_w_in", [D, ATT_IN], F32)
    w_out1 = din("attn_w_out", [BW, D], F32)
    w_in2 = din("pool_w_in", [D, POOL_IN], F32)
    w_grp = din("pool_w_group", [BW, 512], F32)
    w_out2 = din("pool_w_out", [BW, D], F32)
    y_d = nc.dram_tensor("y", [NOWN * 128, D], F32, kind="ExternalOutput").ap()
    dbg_d = nc.dram_tensor("dbg_x1", [NEXT * 128, D], F32, kind="ExternalOutput").ap() if debug else None

    def dscr(name, shape):
        return nc.dram_tensor(name, list(shape), BF16, kind="Internal").ap()

    b_in1 = dscr("b_in1", [D, ATT_IN])
    b_out1 = dscr("b_out1", [BW, D])
    b_in2 = dscr("b_in2", [D, POOL_IN])
    b_grp = dscr("b_grp", [BW, 512])
    b_out2 = dscr("b_out2", [BW, D])

    from contextlib import ExitStack

    es = ExitStack()

    def sb(name, shape, dt):
        return es.enter_context(nc.sbuf_tensor(name, list(shape), dt))

    with es:
        xres = sb("xres", [128, RX, D], F32)
        hb = sb("hb", [128, 4, D], BF16)
        hT = sb("hT", [128, 8, 640], BF16)
        kT = sb("kT", [128, 8, 512], BF16)
        vv = sb("vv", [128, 8, 512], BF16)
        qtok = sb("qtok", [128, 2, 4, 512], BF16)
        ktok = qtok[:, 1]
        qT = sb("qT", [128, 2, 4, 512], BF16)
        pT = sb("pT", [128, 2, 3, 512], BF16)
        rden = sb("rden", [128, 2, 512], F32)
        otmp = sb("otmp", [128, 2, 512], F32)
        aT = sb("aT", [128, 16, 512], BF16)
        h2T = sb("h2T", [128, 8, 528], BF16)
        h2halo = sb("h2halo", [128, 8, 8], BF16)
        uT = sb("uT", [128, 4, 528], F32)
        ptmp = sb("ptmp", [128, 4, 528], F32)
        pfix = sb("pfix", [128, 8], F32)
        pmix = sb("pmix", [128, 2, 512], F32)
        wring = sb("wring", [128, NW, 8, 512], BF16)
        ropet = sb("ropet", [128, 2, 5, 64], F32)
        rtmp = sb("rtmp", [128, 2, 4, 128], F32)
        stat = sb("stat", [128, 16, 4], F32)
        ident = sb("ident_s", [128, 128], BF16)
        ones = sb("ones_s", [128, 128], BF16)
        masks = sb("masks_s", [128, 4, 128], BF16)
        gpre = sb("gpre_s", [128, 2, 8], F32)
        gpost = sb("gpost_s", [128, 2, D], F32)
        esink = sb("esink_s", [128, NH], F32)
        esrow = sb("esrow_s", [1, NH], BF16)
        pscale = sb("pscale_s", [128, 16], F32)
        icnt = sb("icnt_s", [128, 2, 2, 4, 8], F32)
        epsb = sb("epsb", [128, 1], F32)
        ps = es.enter_context(nc.psum_tensor("ps", [128, 8, 512], F32))

        B_x = [Buf(f"x{i}") for i in range(RX)]
        B_hb = [Buf(f"hb{i}") for i in range(4)]
        B_hT = [Buf(f"hT{i}") for i in range(5)]
        B_kT = [Buf(f"kT{i}") for i in range(8)]
        B_v = [Buf(f"v{i}") for i in range(8)]
        B_qtok = [Buf(f"qtok{i}") for i in range(2)]
        B_ktok = B_qtok[1]
        B_qT = [Buf(f"qT{i}") for i in range(2)]
        B_pT = [Buf(f"pT{i}") for i in range(2)]
        B_rden = [Buf(f"rden{i}") for i in range(2)]
        B_otmp = [Buf(f"otmp{i}") for i in range(2)]
        B_aT = [Buf(f"aT{i}") for i in range(4)]
        B_h2T = Buf("h2T")
        B_h2halo = Buf("h2halo")
        B_uT = [Buf(f"uT{i}") for i in range(4)]
        B_ptmp = [Buf("pa"), Buf("pb"), Buf("pa2"), Buf("pb2")]
        B_pfix = Buf("pfix")
        B_pmix = [Buf(f"pmix{i}") for i in range(2)]
        B_w = [Buf(f"w{i}") for i in range(NW)]
        B_rope = [Buf(f"rope{i}") for i in range(2)]
        B_rtmp = [Buf("rta"), Buf("rtb")]
        B_stat = [Buf(f"stat{i}") for i in range(16)]
        B_ps = [Buf(f"ps{i}", psum=True) for i in range(8)]
        B_const = Buf("const")

        state = dict(bank=0, stat=0, hb=0)

        def alloc_banks(n):
            b = state["bank"]
            if b + n > 8:
                b = 0
            state["bank"] = (b + n) % 8
            return b

        def alloc_stat():
            s = state["stat"]
            state["stat"] = (s + 1) % 16
            return s

        def dma(eng, out, in_, r, w):
            return S.op(eng, lambda E, out=out, in_=in_: E.dma_start(out=out, in_=in_), r=r, w=w, dma=True)

        def mm(out, lhsT, rhs, start, stop, r, w):
            return S.op(
                "pe",
                lambda E, out=out, lhsT=lhsT, rhs=rhs, start=start, stop=stop: E.matmul(
                    out, lhsT=lhsT, rhs=rhs, start=start, stop=stop
                ),
                r=r,
                w=w,
            )

        def tr(out, in_, idn, r, w):
            return S.op(
                "pe",
                lambda E, out=out, in_=in_, idn=idn: E.transpose(out=out, in_=in_, identity=idn),
                r=r,
                w=w,
            )

        def act(out, in_, func, r, w, scale=None, bias=None, accum=None):
            def fn(E, out=out, in_=in_, func=func, scale=scale, bias=bias, accum=accum):
                kw = {}
                if scale is not None:
                    kw["scale"] = scale
                if bias is not None:
                    kw["bias"] = bias
                if accum is not None:
                    kw["accum_out"] = accum
                return E.activation(out=out, in_=in_, func=func, **kw)

            return S.op("act", fn, r=r, w=w)

        def tt(eng, out, in0, in1, op, r, w):
            return S.op(
                eng,
                lambda E, out=out, in0=in0, in1=in1, op=op: E.tensor_tensor(out=out, in0=in0, in1=in1, op=op),
                r=r,
                w=w,
            )

        def ts(eng, out, in0, s1, op0, r, w, s2=None, op1=None):
            def fn(E, out=out, in0=in0, s1=s1, op0=op0, s2=s2, op1=op1):
                if op1 is None:
                    return E.tensor_scalar(out=out, in0=in0, scalar1=s1, scalar2=None, op0=op0)
                return E.tensor_scalar(out=out, in0=in0, scalar1=s1, scalar2=s2, op0=op0, op1=op1)

            return S.op(eng, fn, r=r, w=w)

        def stt(out, in0, scalar, in1, op0, op1, r, w):
            return S.op(
                "dve",
                lambda E, out=out, in0=in0, scalar=scalar, in1=in1, op0=op0, op1=op1: E.scalar_tensor_tensor(
                    out=out, in0=in0, scalar=scalar, in1=in1, op0=op0, op1=op1
                ),
                r=r,
                w=w,
            )

        def cp(eng, out, in_, r, w):
            if eng == "act":
                return act(out, in_, AF.Copy, r, w)
            return S.op(eng, lambda E, out=out, in_=in_: E.tensor_copy(out=out, in_=in_), r=r, w=w)

        def recip(out, in_, r, w):
            return S.op("dve", lambda E, out=out, in_=in_: E.reciprocal(out=out, in_=in_), r=r, w=w)

        def memset(eng, ap, val, w):
            return S.op(eng, lambda E, ap=ap, val=val: E.memset(ap, val), r=(), w=w)

        def psbf(bank):
            return ps[:, bank, :].bitcast(BF16)

        for dst, src in (
            (ident[:], ident_d[:, :]),
            (masks[:], masks_d[:, :, :]),
            (gpre[:], gpre_d[:, :, :]),
            (gpost[:], gpost_d[:, :, :]),
            (esink[:], sink_d[:, :]),
            (pscale[:], pscale_d[:, :]),
            (icnt[:], icnt_d[:, :, :, :, :]),
        ):
            dma("sp", dst, src, r=(), w=(B_const,))
        memset("dve", ones[:], 2.0, w=(B_const,))
        memset("dve", epsb[:], EPS, w=(B_const,))
        act(esink[:], esink[:], AF.Exp, r=(B_const,), w=(B_const,))
        cp("dve", esrow[:], esink[0:1, :], r=(B_const,), w=(B_const,))
        if debug:
            for t_, bl in ((hT, B_hT), (kT, B_kT), (vv, B_v), (qtok, B_qtok), (qT, B_qT), (aT, B_aT), (ktok, [B_ktok])):
                memset("pool", t_[:], 0.0, w=tuple(bl))
        v_in1 = b_in1.rearrange("(kc p) n -> p kc n", p=128)
        v_out1 = b_out1.rearrange("(kc p) n -> p kc n", p=128)
        v_in2 = b_in2.rearrange("(kc p) n -> p kc n", p=128)
        v_grp = b_grp.rearrange("(kc p) n -> p kc n", p=128)
        v_out2 = b_out2.rearrange("(kc p) n -> p kc n", p=128)

        def piece_src(key):
            k = key[0]
            if k == "k":
                return v_in1[:, :, 2048:2560], 8, (b_in1, w_in1, 0, D, 2048, 2560)
            if k == "v":
                return v_in1[:, :, 2560:3072], 8, (b_in1, w_in1, 0, D, 2560, 3072)
            if k == "q":
                j = key[1]
                return v_in1[:, :, 512 * j:512 * j + 512], 8, (b_in1, w_in1, 0, D, 512 * j, 512 * j + 512)
            if k == "g":
                j = key[1]
                c0 = 3072 + 512 * j
                return v_in1[:, :, c0:c0 + 512], 8, (b_in1, w_in1, 0, D, c0, c0 + 512)
            if k == "o1":
                h, kk = key[1], key[2]
                return v_out1[:, 8 * kk:8 * kk + 8, 512 * h:512 * h + 512], 8, (b_out1, w_out1, 1024 * kk, 1024 * kk + 1024, 512 * h, 512 * h + 512)
            if k == "u":
                g = key[1]
                return v_in2[:, :, 512 * g:512 * g + 512], 8, (b_in2, w_in2, 0, D, 512 * g, 512 * g + 512)
            if k == "g2":
                g = key[1]
                c0 = 2048 + 512 * g
                return v_in2[:, :, c0:c0 + 512], 8, (b_in2, w_in2, 0, D, c0, c0 + 512)
            if k == "grp":
                g = key[1]
                return v_grp[:, 4 * g:4 * g + 4, :], 4, (b_grp, w_grp, 512 * g, 512 * g + 512, 0, 512)
            if k == "o2":
                h, kk = key[1], key[2]
                return v_out2[:, 8 * kk:8 * kk + 8, 512 * h:512 * h + 512], 8, (b_out2, w_out2, 1024 * kk, 1024 * kk + 1024, 512 * h, 512 * h + 512)
            raise KeyError(key)

        B_cast = {}

        def cast_issue(key):
            if key in B_cast:
                return
            _, _, (dst, src, r0, r1, c0, c1) = piece_src(key)
            B_cast[key] = Buf("cast")
            dma("pool", dst[r0:r1, c0:c1], src[r0:r1, c0:c1], r=(), w=(B_cast[key],))

        steps = []
        for si, sg in enumerate(sinfo):
            kind, n_ext = sg["kind"], sg["n_ext"]
            nst = n_ext // 4
            l2next = 0
            needs_q = (lambda b, n=n_ext, k=kind: (1 <= b <= n - 2) if k == "halo" else (0 <= b <= n - 1))
            needs_l2 = (lambda b, n=n_ext, k=kind: (2 <= b <= n - 3) if k == "halo" else (0 <= b <= n - 1))
            glist = list(range(nst)) + (["flush"] if kind == "full" else [])
            for G in glist:
                if G == "flush":
                    new = []
                    base = n_ext - 1
                    qbs = [n_ext - 1]
                    lim = n_ext - 1
                else:
                    new = list(range(4 * G, 4 * G + 4))
                    base = 4 * G - 1
                    qbs = [b for b in range(4 * G - 1, 4 * G + 3) if b >= 0 and needs_q(b)]
                    lim = 4 * G + 1
                l2bs = [b for b in range(l2next, min(lim, n_ext - 1) + 1) if needs_l2(b)]
                l2next = max(l2next, lim + 1)
                steps.append(dict(si=si, G=G, new=new, base=base, qbs=qbs, l2bs=l2bs))
        for si, sg in enumerate(sinfo):
            mine = [s for s in steps if s["si"] == si and s["l2bs"]]
            for s in steps:
                if s["si"] == si:
                    s["l2first"] = False
                    s["l2last"] = False
            mine[0]["l2first"] = True
            mine[-1]["l2last"] = True

        def phase_pieces(ph, st):
            if ph == "AB":
                return [("k",), ("v",)] if st["new"] else []
            if ph == "Cpre":
                return [("q", 0), ("q", 1)]
            if ph == "C":
                return [("g", 0), ("g", 1), ("q", 2), ("g", 2), ("q", 3), ("g", 3)]
            if ph == "DE":
                return [("o1", 0, 0), ("o1", 0, 1), ("o1", 1, 0), ("o1", 1, 1)]
            if ph == "F":
                return [("g2", 3), ("u", 3), ("g2", 0), ("u", 0), ("g2", 2), ("u", 2), ("g2", 1), ("u", 1)]
            if ph == "H":
                return [("o2", 0, 0), ("o2", 0, 1), ("o2", 1, 0), ("o2", 1, 1)]
            return []

        sched = []
        hasA = lambda st: bool(st["new"]) or st["G"] == "flush"
        if hasA(steps[0]):
            sched.append(("AB", 0))
        sched.append(("Cpre", 0))
        for i, st in enumerate(steps):
            nxt = steps[i + 1] if i + 1 < len(steps) else None
            if st["qbs"]:
                sched.append(("C", i))
                sched.append(("DE", i))
            if nxt is not None and hasA(nxt):
                sched.append(("AB", i + 1))
            elif st["l2bs"]:
                sched.append(("E2", i))
            if st["l2bs"]:
                sched.append(("F", i))
            if nxt is not None:
                sched.append(("Cpre", i + 1))
            if st["l2bs"]:
                sched.append(("H", i))
            sched.append(("END", i))

        allp = []
        for ph, i in sched:
            allp += phase_pieces(ph, steps[i])
        wst = dict(issued=0, used=0, rel=0)

        CAST_AHEAD = 6

        def w_issue_upto(n):
            while wst["issued"] < min(n, len(allp)):
                i = wst["issued"]
                for k2 in allp[i:i + CAST_AHEAD]:
                    cast_issue(k2)
                src, nk, _ = piece_src(allp[i])
                slot = i % NW
                dma("sp", wring[:, slot, 0:nk, :], src, r=(B_cast[allp[i]],), w=(B_w[slot],))
                wst["issued"] += 1

        def wget(key):
            i = wst["used"]
            assert allp[i] == key, (allp[i], key)
            assert i < wst["issued"], "weight piece not issued"
            wst["used"] += 1
            return i % NW

        def wrel(n=1):
            wst["rel"] += n
            assert wst["rel"] <= wst["used"]
            w_issue_upto(wst["rel"] + NW)

        def xslot(si, b):
            return (sinfo[si]["xo"] + b) % RX

        def issue_rope(st):
            si = st["si"]
            xo = sinfo[si]["xo"]
            rb = st["ri"]
            lo = max(st["base"], 0)
            hi = st["base"] + 4 if st["new"] else st["base"]
            i0 = lo - st["base"]
            cnt = hi - lo + 1
            src = rope_d[(xo + lo) * 128:(xo + hi + 1) * 128, :].rearrange("(b p) c -> p b c", p=128)
            dma("sp", ropet[:, rb, i0:i0 + cnt, :], src, r=(), w=(B_rope[rb],))

        def right_halo_block(st):
            sg = sinfo[st["si"]]
            br = st["l2bs"][-1] + 1
            ok = (br <= sg["n_ext"] - 1) and ((1 <= br <= sg["n_ext"] - 2) if sg["kind"] == "halo" else True)
            return br if ok else None

        last_use = {}
        for pos, (ph, i) in enumerate(sched):
            st = steps[i]
            used = set()
            if ph == "AB":
                used |= set(st["new"])
                if i >= 1 and steps[i - 1]["l2bs"]:
                    pst = steps[i - 1]
                    for b in list(pst["l2bs"]) + ([right_halo_block(pst)] if right_halo_block(pst) is not None else []):
                        last_use[(pst["si"], b)] = pos
            elif ph == "DE":
                used |= set(st["qbs"])
                if st["l2bs"] and st["l2first"] and sinfo[st["si"]]["kind"] == "halo":
                    used.add(st["l2bs"][0] - 1)
            elif ph == "E2":
                used |= set(st["l2bs"])
                br = right_halo_block(st)
                if br is not None:
                    used.add(br)
            elif ph == "H":
                used |= set(st["l2bs"])
            for b in used:
                last_use[(st["si"], b)] = pos
        xq = [(i, st["si"], b) for i, st in enumerate(steps) for b in st["new"]]
        xst = dict(next=0)
        slot_free = [True] * RX

        def try_issue_x(max_step):
            while xst["next"] < len(xq):
                i, si, b = xq[xst["next"]]
                if i > max_step:
                    break
                sl = xslot(si, b)
                if not slot_free[sl]:
                    break
                slot_free[sl] = False
                xo = sinfo[si]["xo"]
                dma("sp", xres[:, sl, :], x_d[(xo + b) * 128:(xo + b + 1) * 128, :], r=(), w=(B_x[sl],))
                xst["next"] += 1

        def release_x(pos):
            for (si, b), lu in last_use.items():
                if lu == pos:
                    slot_free[xslot(si, b)] = True

        for i, st in enumerate(steps):
            st["ri"] = i % 2

        def rmsnorm_to_hb(xslot_i, layer):
            hs = state["hb"]
            state["hb"] = (hs + 1) % 4
            s0 = alloc_stat()
            act(hb[:, hs, :], xres[:, xslot_i, :], AF.Square, r=(B_x[xslot_i],), w=(B_hb[hs], B_stat[s0]),
                accum=stat[:, s0, 0:1])
            act(stat[:, s0, 1:2], stat[:, s0, 0:1], AF.Sqrt, r=(B_const,), w=(B_stat[s0],), scale=1.0 / D, bias=epsb[:, 0:1])
            recip(stat[:, s0, 2:3], stat[:, s0, 1:2], r=(), w=(B_stat[s0],))
            ts("dve", hb[:, hs, :], xres[:, xslot_i, :], stat[:, s0, 2:3], ALU.mult, r=(B_x[xslot_i], B_stat[s0]), w=(B_hb[hs],))
            return hs

        def rope_evac(b0, n, dst4, dst_bufs, rb, ri0, extra_r=()):
            psv = ps[:, b0:b0 + n, :].rearrange("p n (h d) -> p n h d", h=4)
            pbufs = tuple(B_ps[b0 + i] for i in range(n))
            tab = ropet[:, rb, ri0:ri0 + n, :]
            cc = tab[:, :, 0:32].unsqueeze(2).to_broadcast([128, n, 4, 32])
            nsn = tab[:, :, 32:48].unsqueeze(2).to_broadcast([128, n, 4, 16])
            psn = tab[:, :, 48:64].unsqueeze(2).to_broadcast([128, n, 4, 16])
            ta = rtmp[:, 0, 0:n, :].rearrange("p n (h d) -> p n h d", h=4)
            tb = rtmp[:, 1, 0:n, :].rearrange("p n (h d) -> p n h d", h=4)
            if "rope_act" not in SKIP:
                cp("act", dst4[:, :, :, 32:128], psv[:, :, :, 32:128], r=pbufs, w=dst_bufs)
            if "rope_dve" not in SKIP:
                tt("dve", ta, psv[:, :, :, 0:32], cc, ALU.mult, r=pbufs + (B_rope[rb],), w=(B_rtmp[0],))
                tt("dve", tb[:, :, :, 0:16], psv[:, :, :, 16:32], nsn, ALU.mult, r=pbufs + (B_rope[rb],), w=(B_rtmp[1],))
                tt("dve", tb[:, :, :, 16:32], psv[:, :, :, 0:16], psn, ALU.mult, r=pbufs + (B_rope[rb],), w=(B_rtmp[1],))
            if "rope_pool" not in SKIP:
                tt("pool", dst4[:, :, :, 0:32], ta, tb, ALU.add, r=(B_rtmp[0], B_rtmp[1]), w=dst_bufs)

        def kslot(b):
            return b % 8

        def phase_AB(st, inject=None):
            si = st["si"]
            new = st["new"]
            n = len(new)
            rb = st["ri"]
            if st["G"] != 0:
                cp("pool", hT[:, :, 0:128], hT[:, :, 512:640], r=(B_hT[4],), w=(B_hT[0],))
            if not new:
                for ch, pe in (inject or []):
                    ch()
                    pe()
                return
            wk = wget(("k",))
            wv = wget(("v",))
            kb0 = alloc_banks(n)
            inject = list(inject) if inject else []
            hsl = {}

            def chain(i):
                S.tag = "A"
                hsl[i] = rmsnorm_to_hb(xslot(si, new[i]), 0)

            def trans(i):
                S.tag = "A"
                cg = new[i] - st["base"]
                hs = hsl[i]
                bank = alloc_banks(1)
                if kb0 <= bank < kb0 + n:
                    state["bank"] = (kb0 + n) % 8
                    bank = alloc_banks(1)
                pb = psbf(bank)
                for kc in range(8):
                    tr(pb[:, kc * 128:(kc + 1) * 128], hb[:, hs, kc * 128:(kc + 1) * 128], ident[:], r=(B_hb[hs], B_const), w=(B_ps[bank],))
                tt("dve", hT[:, :, cg * 128:(cg + 1) * 128], pb.rearrange("p (c t) -> p c t", c=8),
                   gpre[:, 0, :].unsqueeze(2).to_broadcast([128, 8, 128]), ALU.mult, r=(B_ps[bank], B_const), w=(B_hT[cg],))

            chain(0)
            if n > 1:
                chain(1)
            for i in range(n):
                trans(i)
                if i + 2 < n:
                    chain(i + 2)
                if i >= 1:
                    if inject and i + 2 >= n:
                        ch, pe = inject.pop(0)
                        ch()
                        kv_block(st, i - 1, kb0, n, wk, wv)
                        pe()
                    else:
                        kv_block(st, i - 1, kb0, n, wk, wv)
            if inject:
                ch, pe = inject.pop(0)
                ch()
                kv_block(st, n - 1, kb0, n, wk, wv)
                pe()
            else:
                kv_block(st, n - 1, kb0, n, wk, wv)
            for ch, pe in inject:
                ch()
                pe()
            wrel(2)
            S.tag = "B"
            rope_evac(kb0, n, ktok[:, 0:n, :].rearrange("p n (h d) -> p n h d", h=4), (B_ktok,), rb, new[0] - st["base"])
            for i0 in range(0, n, 2):
                bank = alloc_banks(1)
                pb = psbf(bank)
                m = min(2, n - i0)
                for ii in range(m):
                    for h in range(4):
                        tr(pb[:, ii * 512 + h * 128: ii * 512 + (h + 1) * 128], ktok[:, i0 + ii, h * 128:(h + 1) * 128], ident[:],
                           r=(B_ktok, B_const), w=(B_ps[bank],))
                for ii in range(m):
                    sl = kslot(new[i0 + ii])
                    cp("dve", kT[:, sl, :], pb[:, ii * 512:(ii + 1) * 512], r=(B_ps[bank],), w=(B_kT[sl],))

        def kv_block(st, i, kb0, n, wk, wv):
            S.tag = "B"
            b = st["new"][i]
            cg = b - st["base"]
            for kc in range(8):
                mm(ps[:, kb0 + i, :], hT[:, kc, cg * 128:(cg + 1) * 128], wring[:, wk, kc, :], kc == 0, kc == 7,
                   r=(B_hT[cg], B_w[wk]), w=(B_ps[kb0 + i],))
            bank = alloc_banks(1)
            if kb0 <= bank < kb0 + n:
                state["bank"] = (kb0 + n) % 8
                bank = alloc_banks(1)
            for kc in range(8):
                mm(ps[:, bank, :], hT[:, kc, cg * 128:(cg + 1) * 128], wring[:, wv, kc, :], kc == 0, kc == 7,
                   r=(B_hT[cg], B_w[wv]), w=(B_ps[bank],))
            sl = kslot(b)
            cp("act", vv[:, sl, :], ps[:, bank, :], r=(B_ps[bank],), w=(B_v[sl],))

        def phase_C(st, part="main"):
            si = st["si"]
            sg = sinfo[si]
            kind, n_ext = sg["kind"], sg["n_ext"]
            qbs = st["qbs"]
            n = len(qbs)
            rb = st["ri"]
            cg0 = qbs[0] - st["base"]

            held = {}

            def Qpart(j, i):
                jb = j % 2
                S.tag = "C-q"
                if ("q", j) not in held:
                    held[("q", j)] = wget(("q", j))
                ws = held[("q", j)]
                b0 = alloc_banks(1)
                cg = qbs[i] - st["base"]
                for kc in range(8):
                    mm(ps[:, b0, :], hT[:, kc, cg * 128:(cg + 1) * 128], wring[:, ws, kc, :], kc == 0, kc == 7,
                       r=(B_hT[cg], B_w[ws]), w=(B_ps[b0],))
                rope_evac(b0, 1, qtok[:, jb, i:i + 1, :].rearrange("p n (h d) -> p n h d", h=4), (B_qtok[jb],), rb, cg)

            def Gpart(j, m):
                S.tag = "C-g"
                if ("g", j) not in held:
                    held[("g", j)] = wget(("g", j))
                ws = held[("g", j)]
                bank = alloc_banks(1)
                for kc in range(8):
                    mm(ps[:, bank, 0:n * 128], wring[:, ws, kc, m * 128:(m + 1) * 128], hT[:, kc, cg0 * 128:(cg0 + n) * 128],
                       kc == 0, kc == 7, r=tuple(B_hT[cg0 + i] for i in range(n)) + (B_w[ws],), w=(B_ps[bank],))
                gdst = aT[:, 4 * j + m, 0:n * 128]
                act(gdst, ps[:, bank, 0:n * 128], AF.Tanh, r=(B_ps[bank],), w=tuple(B_aT[i] for i in range(n)), scale=0.5)
                stt(gdst, gdst, 1.0, ps[:, bank, 0:n * 128], ALU.add, ALU.mult, r=(B_ps[bank],), w=tuple(B_aT[i] for i in range(n)))

            def Qp(j):
                jb = j % 2
                S.tag = "C-q"
                ws = wget(("q", j))
                b0 = alloc_banks(n)
                for i, b in enumerate(qbs):
                    cg = b - st["base"]
                    for kc in range(8):
                        mm(ps[:, b0 + i, :], hT[:, kc, cg * 128:(cg + 1) * 128], wring[:, ws, kc, :], kc == 0, kc == 7,
                           r=(B_hT[cg], B_w[ws]), w=(B_ps[b0 + i],))
                wrel()
                rope_evac(b0, n, qtok[:, jb, 0:n, :].rearrange("p n (h d) -> p n h d", h=4), (B_qtok[jb],), rb, cg0)

            def Gp(j):
                for m in range(4):
                    Gpart(j, m)
                wrel()

            def Tq(j):
                jb = j % 2
                S.tag = "C-T"
                for i0 in range(0, n, 2):
                    bank = alloc_banks(1)
                    pb = psbf(bank)
                    m2 = min(2, n - i0)
                    for ii in range(m2):
                        for h in range(4):
                            tr(pb[:, ii * 512 + h * 128: ii * 512 + (h + 1) * 128], qtok[:, jb, i0 + ii, h * 128:(h + 1) * 128], ident[:],
                               r=(B_qtok[jb], B_const), w=(B_ps[bank],))
                    cp("dve", qT[:, jb, i0:i0 + m2, :], pb[:, 0:m2 * 512].rearrange("p (n c) -> p n c", n=m2), r=(B_ps[bank],), w=(B_qT[jb],))

            def key_blocks(b):
                kbs = []
                if b - 1 >= 0:
                    kbs.append((b - 1, 2 if (kind == "halo" and b == 2) else 0))
                kbs.append((b, None))
                if b + 1 <= n_ext - 1:
                    kbs.append((b + 1, 3 if (kind == "halo" and b == n_ext - 3) else 1))
                return kbs

            def Sst(j, i):
                S.tag = "C-S"
                jb = j % 2
                pbi = (j * n + i) % 2
                kbs = key_blocks(qbs[i])
                nk = len(kbs)
                b0 = alloc_banks(nk)
                for kidx, (kb, mi) in enumerate(kbs):
                    bank = b0 + kidx
                    sl = kslot(kb)
                    mm(ps[:, bank, :], kT[:, sl, j * 128:(j + 1) * 128], qT[:, jb, i, :], True, mi is None,
                       r=(B_kT[sl], B_qT[jb]), w=(B_ps[bank],))
                    if mi is not None:
                        mm(ps[:, bank, :], ident[:], masks[:, mi, :].unsqueeze(1).to_broadcast([128, 4, 128]), False, True, r=(B_const,), w=(B_ps[bank],))
                act(pT[:, pbi, 0:nk, :], ps[:, b0:b0 + nk, :], AF.Exp, r=tuple(B_ps[b0 + k] for k in range(nk)), w=(B_pT[pbi],), scale=QSCALE)

            def PVs(j, i):
                S.tag = "C-PV"
                pbi = (j * n + i) % 2
                kbs = key_blocks(qbs[i])
                nk = len(kbs)
                bd = alloc_banks(1)
                for kidx in range(nk):
                    mm(ps[:, bd, :], ones[:], pT[:, pbi, kidx, :], kidx == 0, False,
                       r=(B_const, B_pT[pbi]), w=(B_ps[bd],))
                mm(ps[:, bd, :], ones[0:1, :], esrow[0:1, 4 * j:4 * j + 4].unsqueeze(2).to_broadcast([1, 4, 128]), False, True, r=(B_const,), w=(B_ps[bd],))
                bo = alloc_banks(1)
                for kidx, (kb, mi) in enumerate(kbs):
                    sl = kslot(kb)
                    mm(ps[:, bo, :], vv[:, sl, j * 128:(j + 1) * 128], pT[:, pbi, kidx, :], kidx == 0, kidx == nk - 1,
                       r=(B_v[sl], B_pT[pbi]), w=(B_ps[bo],))
                rd = rden[:, pbi, :]
                rd3 = rd.rearrange("p (h t) -> p h t", h=4)
                ot3 = otmp[:, pbi, :].rearrange("p (h t) -> p h t", h=4)
                av = aT[:, 4 * j:4 * j + 4, i * 128:(i + 1) * 128]
                tt("dve", ot3, ps[:, bo, :].rearrange("p (h t) -> p h t", h=4), av, ALU.mult, r=(B_ps[bo], B_aT[i]), w=(B_otmp[pbi],))
                recip(rd, ps[:, bd, :], r=(B_ps[bd],), w=(B_rden[pbi],))
                tt("pool", av, ot3, rd3, ALU.mult, r=(B_otmp[pbi], B_rden[pbi]), w=(B_aT[i],))

            if part == "pre":
                Qp(0)
                Tq(0)
                Qp(1)
                return
            Gp(0)
            items = [(j, i) for j in range(4) for i in range(n)]
            for idx in range(len(items) + 1):
                if idx < len(items):
                    j, i = items[idx]
                    Sst(j, i)
                    if i == min(1, n - 1) and j + 1 < 4:
                        Gp(j + 1)
                        Tq(j + 1)
                    if i == n - 1 and j + 2 < 4:
                        Qp(j + 2)
                if idx >= 1:
                    PVs(*items[idx - 1])

        def out_proj(st, blocks, layer, okey, after_block=None):
            si = st["si"]
            slots = {}
            for h in range(2):
                for kk in range(2):
                    slots[(h, kk)] = wget((okey, h, kk))
            for i, b in enumerate(blocks):
                S.tag = "D" if layer == 0 else "H"
                sl = xslot(si, b)
                b0 = alloc_banks(2)
                for h in range(2):
                    for c in range(16):
                        ws = slots[(h, c // 8)]
                        mm(ps[:, b0 + h, :], aT[:, c, i * 128:(i + 1) * 128], wring[:, ws, c % 8, :], c == 0, c == 15,
                           r=(B_aT[i], B_w[ws]), w=(B_ps[b0 + h],))
                if after_block is not None:
                    after_block(i)
                pv = ps[:, b0:b0 + 2, :].rearrange("p a b -> p (a b)")
                pbufs = (B_ps[b0], B_ps[b0 + 1])
                s0 = alloc_stat()
                act(otmp[:, 0, :].bitcast(BF16), pv, AF.Square, r=pbufs, w=(B_otmp[0], B_stat[s0]), accum=stat[:, s0, 0:1])
                act(stat[:, s0, 1:2], stat[:, s0, 0:1], AF.Sqrt, r=(B_const,), w=(B_stat[s0],), scale=1.0 / D, bias=epsb[:, 0:1])
                recip(stat[:, s0, 2:3], stat[:, s0, 1:2], r=(), w=(B_stat[s0],))
                tt("dve", pv, pv, gpost[:, layer, :], ALU.mult, r=(B_const,), w=pbufs)
                stt(xres[:, sl, :], pv, stat[:, s0, 2:3], xres[:, sl, :], ALU.mult, ALU.add, r=pbufs + (B_stat[s0],), w=(B_x[sl],))
            wrel(4)

        def E_pe(hs, col0, ncols, first_tok=None):
            g1 = gpre[:, 1, :]
            bank = alloc_banks(1)
            pb = psbf(bank)
            if first_tok == "head8":
                for kc in range(8):
                    tr(pb[:, kc * 8:(kc + 1) * 8], hb[0:8, hs, kc * 128:(kc + 1) * 128], ident[0:8, 0:8], r=(B_hb[hs], B_const), w=(B_ps[bank],))
                src = pb[:, 0:64].rearrange("p (c t) -> p c t", c=8)
            else:
                for kc in range(8):
                    tr(pb[:, kc * 128:(kc + 1) * 128], hb[:, hs, kc * 128:(kc + 1) * 128], ident[:], r=(B_hb[hs], B_const), w=(B_ps[bank],))
                src = pb.rearrange("p (c t) -> p c t", c=8)
                if first_tok == "tail8":
                    src = src[:, :, 120:128]
            tt("dve", h2T[:, :, col0:col0 + ncols], src, g1.unsqueeze(2).to_broadcast([128, 8, ncols]), ALU.mult,
               r=(B_ps[bank], B_const), w=(B_h2T,))

        def E_tr_block(sl, col0, ncols, first_tok=None):
            hs = rmsnorm_to_hb(sl, 1)
            E_pe(hs, col0, ncols, first_tok)

        def E2_units(st):
            si = st["si"]
            l2 = st["l2bs"]
            n = len(l2)
            units = []
            while st["e_done"] < n:
                i = st["e_done"]
                st["e_done"] += 1
                box = {}

                def ch(i=i, box=box):
                    S.tag = "E"
                    box["hs"] = rmsnorm_to_hb(xslot(si, l2[i]), 1)

                def pe(i=i, box=box):
                    S.tag = "E"
                    E_pe(box["hs"], 8 + i * 128, 128)

                units.append((ch, pe))
            c0 = 8 + n * 128
            br = right_halo_block(st)
            box = {}

            def ch_r(box=box):
                S.tag = "E"
                if br is not None:
                    box["hs"] = rmsnorm_to_hb(xslot(si, br), 1)

            def pe_r(box=box):
                S.tag = "E"
                if br is not None:
                    E_pe(box["hs"], c0, 8, "head8")
                else:
                    memset("pool", h2T[:, :, c0:c0 + 8], 0.0, w=(B_h2T,))
                if not st["l2last"]:
                    cp("pool", h2halo[:], h2T[:, :, c0 - 8:c0], r=(B_h2T,), w=(B_h2halo,))

            units.append((ch_r, pe_r))
            return units

        def E_left(st):
            si = st["si"]
            kind = sinfo[si]["kind"]
            l2 = st["l2bs"]
            S.tag = "E"
            if st["l2first"]:
                if kind == "halo":
                    E_tr_block(xslot(si, l2[0] - 1), 0, 8, "tail8")
                else:
                    memset("pool", h2T[:, :, 0:8], 0.0, w=(B_h2T,))
            else:
                cp("pool", h2T[:, :, 0:8], h2halo[:], r=(B_h2halo,), w=(B_h2T,))

        def E_block(st, i):
            S.tag = "E"
            E_tr_block(xslot(st["si"], st["l2bs"][i]), 8 + i * 128, 128)

        def phase_DE(st):
            qbs, l2 = st["qbs"], st["l2bs"]
            st["e_done"] = 0
            if l2:
                E_left(st)

            def ready(bq, upto):
                return (bq not in qbs) or (qbs.index(bq) <= upto)

            pend = []

            def hook(idx):
                while pend:
                    hs, i = pend.pop(0)
                    S.tag = "E"
                    E_pe(hs, 8 + i * 128, 128)
                if l2 and st["e_done"] < len(l2) and ready(l2[st["e_done"]], idx - 1):
                    i = st["e_done"]
                    st["e_done"] += 1
                    S.tag = "E"
                    pend.append((rmsnorm_to_hb(xslot(st["si"], l2[i]), 1), i))

            if l2:
                hook(-1)
            out_proj(st, qbs, 0, "o1", after_block=hook if l2 else None)
            while pend:
                hs, i = pend.pop(0)
                S.tag = "E"
                E_pe(hs, 8 + i * 128, 128)

        def phase_E2(st):
            for ch, pe in E2_units(st):
                ch()
                pe()

        def phase_F(st):
            si = st["si"]
            sg = sinfo[si]
            kidx = 0 if sg["kind"] == "halo" else 1
            l2 = st["l2bs"]
            n = len(l2)
            N = n * 128
            W = N + 16
            if W <= 512:
                ranges = [(0, W)]
            else:
                ranges = [(0, W // 2), (W // 2, W)]
            cnt = dict(k=0)

            def G2p(g):
                S.tag = "F-g2"
                wg = wget(("g2", g))
                for m in range(4):
                    bank = alloc_banks(1)
                    for kc in range(8):
                        mm(ps[:, bank, 0:N], wring[:, wg, kc, m * 128:(m + 1) * 128], h2T[:, kc, 8:8 + N], kc == 0, kc == 7,
                           r=(B_h2T, B_w[wg]), w=(B_ps[bank],))
                    act(aT[:, 4 * g + m, 0:N], ps[:, bank, 0:N], AF.Silu, r=(B_ps[bank],), w=tuple(B_aT[i] for i in range(n)))
                wrel()

            def Up(g):
                S.tag = "F-u"
                wu = wget(("u", g))
                for m in range(4):
                    for (c0, c1) in ranges:
                        bank = alloc_banks(1)
                        for kc in range(8):
                            mm(ps[:, bank, 0:c1 - c0], wring[:, wu, kc, m * 128:(m + 1) * 128], h2T[:, kc, c0:c1], kc == 0, kc == 7,
                               r=(B_h2T, B_w[wu]), w=(B_ps[bank],))
                        cp("act", uT[:, m, c0:c1], ps[:, bank, 0:c1 - c0], r=(B_ps[bank],), w=(B_uT[m],))
                    if m == 3:
                        wrel()
                    U = uT[:, m, :]
                    Bu = B_uT[m]
                    pe_ = "pool" if m % 2 == 0 else "dve"
                    o_ = 0 if m % 2 == 0 else 2
                    pa = ptmp[:, o_, :]
                    pb_ = ptmp[:, o_ + 1, :]
                    Ba, Bb = B_ptmp[o_], B_ptmp[o_ + 1]
                    w_ = 2 ** (g + 1)
                    if g == 0:
                        tt(pe_, pa[:, 0:N], U[:, 7:7 + N], U[:, 8:8 + N], ALU.add, r=(Bu,), w=(Ba,))
                        box, Bbox = pa, Ba
                    elif g == 1:
                        tt(pe_, pa[:, 0:N + 2], U[:, 6:8 + N], U[:, 7:9 + N], ALU.add, r=(Bu,), w=(Ba,))
                        tt(pe_, pb_[:, 0:N], pa[:, 0:N], pa[:, 2:N + 2], ALU.add, r=(Ba,), w=(Bb,))
                        box, Bbox = pb_, Bb
                    elif g == 2:
                        tt(pe_, pa[:, 0:N + 6], U[:, 4:10 + N], U[:, 5:11 + N], ALU.add, r=(Bu,), w=(Ba,))
                        tt(pe_, pb_[:, 0:N + 4], pa[:, 0:N + 4], pa[:, 2:N + 6], ALU.add, r=(Ba,), w=(Bb,))
                        tt(pe_, pa[:, 0:N], pb_[:, 0:N], pb_[:, 4:N + 4], ALU.add, r=(Bb,), w=(Ba,))
                        box, Bbox = pa, Ba
                    else:
                        tt(pe_, pa[:, 0:N + 14], U[:, 0:N + 14], U[:, 1:N + 15], ALU.add, r=(Bu,), w=(Ba,))
                        tt(pe_, pb_[:, 0:N + 12], pa[:, 0:N + 12], pa[:, 2:N + 14], ALU.add, r=(Ba,), w=(Bb,))
                        tt(pe_, pa[:, 0:N + 8], pb_[:, 0:N + 8], pb_[:, 4:N + 12], ALU.add, r=(Bb,), w=(Ba,))
                        tt(pe_, pb_[:, 0:N], pa[:, 0:N], pa[:, 8:N + 8], ALU.add, r=(Ba,), w=(Bb,))
                        box, Bbox = pb_, Bb
                    pbi = cnt["k"] % 2
                    cnt["k"] += 1
                    pm = pmix[:, pbi, :]
                    stt(pm[:, 0:N], box[:, 0:N], 1.0 / w_, U[:, 8:8 + N], ALU.mult, ALU.subtract, r=(Bbox, Bu), w=(B_pmix[pbi],))
                    for (flag, fl, cs) in ((st["l2first"], 0, 0), (st["l2last"], 1, N - 8)):
                        if flag:
                            tt("dve", pfix[:, :], box[:, cs:cs + 8], icnt[:, kidx, fl, g, :], ALU.mult, r=(Bbox, B_const), w=(B_pfix,))
                            tt("dve", pm[:, cs:cs + 8], pfix[:, :], U[:, 8 + cs:16 + cs], ALU.subtract,
                               r=(B_pfix, Bu), w=(B_pmix[pbi],))
                    c = 4 * g + m
                    stt(aT[:, c, 0:N], pm[:, 0:N], pscale[:, c:c + 1], aT[:, c, 0:N], ALU.mult, ALU.mult,
                        r=(B_pmix[pbi], B_const), w=tuple(B_aT[i] for i in range(n)))

            for g in (3, 0, 2, 1):
                G2p(g)
                Up(g)

        def phase_store(st):
            si = st["si"]
            sg = sinfo[si]
            for b in st["l2bs"]:
                sl = xslot(si, b)
                ob = b - 2 if sg["kind"] == "halo" else b
                row = (sg["yo"] + ob) * 128
                dma("sp", y_d[row:row + 128, :], xres[:, sl, :], r=(B_x[sl],), w=())

        dbg_outs = {}

        def dbg(name, ap, shape, dt, bufs):
            if not debug or name in dbg_outs:
                return
            t = nc.dram_tensor("dbg_" + name, list(shape), dt, kind="ExternalOutput").ap()
            dbg_outs[name] = t
            dma("sp", t, ap, r=bufs, w=())

        def prologue_fuse():
            S.tag = "P"
            vin2_f = w_in2.rearrange("(kc p) n -> p kc n", p=128)
            vgrp_f = w_grp.rearrange("(kc p) n -> p kc n", p=128)
            sA, sB, sC, sD = 0, 1, 2, 3
            wC = wring[:, sC, :, :].rearrange("p a b -> p (a b)").rearrange("p (cc r) -> p cc r", cc=4)
            for g in range(4):
                B_cast[("u", g)] = Buf("fused")
                dma("pool", wring[:, sA, :, :], vin2_f[:, :, 512 * g:512 * g + 512], r=(), w=(B_w[sA],))
                dma("pool", wring[:, sB, 0:4, :], vgrp_f[:, 4 * g:4 * g + 4, :], r=(), w=(B_w[sB],))
                for cc in range(4):
                    bank = alloc_banks(1)
                    pb = psbf(bank)
                    for m in range(8):
                        tr(pb[:, m * 128:(m + 1) * 128], wring[:, sA, m, cc * 128:(cc + 1) * 128], ident[:], r=(B_w[sA], B_const), w=(B_ps[bank],))
                    cp("dve" if cc % 2 else "act", wC[:, cc, :], pb[:, :], r=(B_ps[bank],), w=(B_w[sC],))
                for m in range(8):
                    bank = alloc_banks(1)
                    for cc in range(4):
                        mm(ps[:, bank, :], wC[:, cc, m * 128:(m + 1) * 128], wring[:, sB, cc, :], cc == 0, cc == 3,
                           r=(B_w[sC], B_w[sB]), w=(B_ps[bank],))
                    cp("dve" if m % 2 else "act", wring[:, sD, m, :], ps[:, bank, :], r=(B_ps[bank],), w=(B_w[sD],))
                dma("sp", v_in2[:, :, 512 * g:512 * g + 512], wring[:, sD, :, :], r=(B_w[sD],), w=(B_cast[("u", g)],))

        prologue_fuse()

        w_issue_upto(NW)
        issue_rope(steps[0])
        if len(steps) > 1:
            issue_rope(steps[1])
        try_issue_x(1)
        nph = 0
        for pos, (ph, i) in enumerate(sched):
            if limit is not None and nph >= limit:
                break
            nph += 1
            st = steps[i]
            S.tag = ph
            if ph == "AB":
                if st["new"]:
                    need_upto = max(k for k, (ii, _, _) in enumerate(xq) if ii == i)
                    assert xst["next"] > need_upto, ("x load not issued in time", i)
                inj = E2_units(steps[i - 1]) if (i >= 1 and steps[i - 1]["l2bs"]) else None
                phase_AB(st, inj)
                dbg("hT", hT[:], [128, 8, 640], BF16, tuple(B_hT))
                dbg("kT", kT[:], [128, 8, 512], BF16, tuple(B_kT))
                dbg("vv", vv[:], [128, 8, 512], BF16, tuple(B_v))
                dbg("ktok", ktok[:], [128, 4, 512], BF16, (B_ktok,))
            elif ph == "Cpre":
                phase_C(st, "pre")
            elif ph == "C":
                phase_C(st)
                dbg("qtok", qtok[:], [128, 2, 4, 512], BF16, tuple(B_qtok))
                dbg("qT", qT[:], [128, 2, 4, 512], BF16, tuple(B_qT))
                dbg("aT", aT[:], [128, 16, 512], BF16, tuple(B_aT))
            elif ph == "DE":
                phase_DE(st)
                if debug:
                    for b in st["qbs"]:
                        sl = xslot(st["si"], b)
                        row = (sinfo[st["si"]]["xo"] + b) * 128
                        dma("sp", dbg_d[row:row + 128, :], xres[:, sl, :], r=(B_x[sl],), w=())
            elif ph == "E2":
                phase_E2(st)
            elif ph == "F":
                phase_F(st)
            elif ph == "H":
                out_proj(st, st["l2bs"], 1, "o2")
                phase_store(st)
            elif ph == "END":
                if i + 2 < len(steps):
                    issue_rope(steps[i + 2])
            release_x(pos)
            try_issue_x(i + 2)
        last = [o for o in S.q["sp"] if o.dma]
        tail = last[-S.NDS["sp"]:]
        fin = Buf("fin")
        for o in tail:
            fin.r.append(o)
        S.op("sp", lambda E: E.nop(), r=(), w=(fin,))
        assert limit is not None or wst["used"] == len(allp) == wst["rel"], (wst, len(allp))

        need = S.emit(nc, None, None, None)
        sem_stack = ExitStack()
        with sem_stack:
            esems = {e: [sem_stack.enter_context(nc.semaphore(f"s_{e}_{k}")) for k in range(need[e])] for e in S.ENGS}
            dsems = {e: [sem_stack.enter_context(nc.semaphore(f"d_{e}_{k}")) for k in range(n)] for e, n in S.NDS.items()}
            with nc.Block() as block:
                @block.tensor
                def _(E):
                    S.run_engine("pe", E, esems, dsems)

                @block.scalar
                def _(E):
                    S.run_engine("act", E, esems, dsems)

                @block.vector
                def _(E):
                    S.run_engine("dve", E, esems, dsems)

                @block.gpsimd
                def _(E):
                    S.run_engine("pool", E, esems, dsems)

                @block.sync
                def _(E):
                    S.run_engine("sp", E, esems, dsems)
    counts = {e: len(S.q[e]) for e in S.ENGS}
    counts["tags"] = {e: [o.tag for o in S.q[e]] for e in S.ENGS}
    return nc, counts


def rope_table(pos):
    half = 16
    inv_freq = (np.float32(ROPE_THETA) ** (-(np.arange(half, dtype=np.float32) * np.float32(2.0) / np.float32(32)))).astype(np.float32)
    ang = (pos.astype(np.float32)[:, None] * inv_freq[None, :]).astype(np.float32)
    c = np.cos(ang).astype(np.float32)
    s = np.sin(ang).astype(np.float32)
    return np.concatenate([c, c, -s, s], axis=1).astype(np.float32)


def inv_counts(S_len, first):
    out = np.zeros((4, 8), np.float32)
    for g, w in enumerate((2, 4, 8, 16)):
        for i in range(8):
            t = i if first else S_len - 8 + i
            lo = max(t - w // 2, 0)
            hi = min(t + w // 2 - 1, S_len - 1)
            out[g, i] = 1.0 / float(hi - lo + 1)
    return out


def tri_masks():
    kl = np.arange(128)[:, None]
    ql = np.arange(128)[None, :]
    NEG = np.float32(-30000.0)
    triL = np.where(ql <= kl, np.float32(0.0), NEG).astype(np.float32)
    triU = np.where(kl <= ql, np.float32(0.0), NEG).astype(np.float32)
    return triL, triU


def core_inputs(seg_specs, params):
    xs, ropes = [], []
    triL, triU = tri_masks()
    m = np.zeros((128, 4, 128), np.float32)
    m[:, 0] = triL
    m[:, 1] = triU
    ic = np.zeros((2, 2, 4, 8), np.float32)
    for kind, xseq, a, b in seg_specs:
        S_len = xseq.shape[0]
        if kind == "halo":
            lo, hi = a - 256, b + 256
            xe = np.zeros((hi - lo, D), np.float32)
            s0, s1 = max(lo, 0), min(hi, S_len)
            xe[s0 - lo:s1 - lo] = xseq[s0:s1]
            pos = np.clip(np.arange(lo, hi), 0, S_len - 1)
            m[:, 2] = triL if a > 0 else -30000.0
            m[:, 3] = triU if b < S_len else -30000.0
            wv = np.array([[1.0 / w] * 8 for w in (2, 4, 8, 16)], np.float32)
            ic[0, 0] = inv_counts(S_len, True) if a == 0 else wv
            ic[0, 1] = inv_counts(S_len, False) if b == S_len else wv
        else:
            xe = xseq[a:b]
            pos = np.arange(a, b)
            ic[1, 0] = inv_counts(S_len, True)
            ic[1, 1] = inv_counts(S_len, False)
        xs.append(xe)
        ropes.append(rope_table(pos))
    d = dict(params)
    d["x"] = np.ascontiguousarray(np.concatenate(xs, 0))
    d["rope"] = np.ascontiguousarray(np.concatenate(ropes, 0))
    d["masks"] = m.astype(ml_dtypes.bfloat16)
    d["icnt"] = np.ascontiguousarray(np.broadcast_to(ic[None], (128, 2, 2, 4, 8))).astype(np.float32)
    return d


def shared_params(norm_pre, norm_post, attn_w_in, attn_sink, attn_w_out, pool_w_in, pool_w_group, pool_scale, pool_w_out):
    p = {}
    p["ident"] = np.eye(128, dtype=np.float32).astype(ml_dtypes.bfloat16)
    p["gpre"] = np.ascontiguousarray(np.asarray(norm_pre, np.float32).reshape(2, 8, 128).transpose(2, 0, 1))
    p["gpost"] = np.ascontiguousarray(np.broadcast_to(np.asarray(norm_post, np.float32)[None], (128, 2, D)))
    p["sinkb"] = np.ascontiguousarray(np.broadcast_to(np.asarray(attn_sink, np.float32).reshape(1, NH), (128, NH)))
    p["pscale"] = np.ascontiguousarray(np.asarray(pool_scale, np.float32).reshape(16, 128).T)
    p["attn_w_in"] = np.ascontiguousarray(np.asarray(attn_w_in, np.float32).reshape(D, ATT_IN))
    p["attn_w_out"] = np.ascontiguousarray(np.asarray(attn_w_out, np.float32).reshape(BW, D))
    p["pool_w_in"] = np.ascontiguousarray(np.asarray(pool_w_in, np.float32).reshape(D, POOL_IN))
    p["pool_w_group"] = np.ascontiguousarray(np.asarray(pool_w_group, np.float32).reshape(BW, 512))
    p["pool_w_out"] = np.ascontiguousarray(np.asarray(pool_w_out, np.float32).reshape(BW, D))
    return p


_PROG_CACHE = {}


def get_program(segs):
    key = tuple(segs)
    if key not in _PROG_CACHE:
        _PROG_CACHE[key] = build_program(list(segs))
    return _PROG_CACHE[key]


def kernel(x_prompt, x_sample, norm_pre, norm_post, attn_w_in, attn_sink, attn_w_out,
           pool_w_in, pool_w_group, pool_scale, pool_w_out):
    x_prompt = np.asarray(x_prompt, np.float32)
    x_sample = np.asarray(x_sample, np.float32)
    n_cores = 8
    PB, PS = x_prompt.shape[0], x_prompt.shape[1]
    SB, SS = x_sample.shape[0], x_sample.shape[1]
    per = n_cores // PB
    q_len = PS // per
    s_per = SB // n_cores
    segs = [("halo", q_len // 128 + 4)] + [("full", SS // 128)] * s_per
    nc, _ = get_program(tuple(segs))
    params = shared_params(norm_pre, norm_post, attn_w_in, attn_sink, attn_w_out, pool_w_in, pool_w_group, pool_scale, pool_w_out)
    in_maps = []
    for c in range(n_cores):
        pb, qi = c // per, c % per
        specs = [("halo", x_prompt[pb], qi * q_len, (qi + 1) * q_len)]
        for k in range(s_per):
            specs.append(("full", x_sample[c * s_per + k], 0, SS))
        in_maps.append(core_inputs(specs, params))
    res = run_bass_kernel_spmd(nc, in_maps, core_ids=list(range(n_cores)))
    y_prompt = np.empty_like(x_prompt)
    y_sample = np.empty_like(x_sample)
    for c in range(n_cores):
        y = res.results[c]["y"]
        pb, qi = c // per, c % per
        y_prompt[pb, qi * q_len:(qi + 1) * q_len] = y[:q_len]
        for k in range(s_per):
            y_sample[c * s_per + k] = y[q_len + k * SS: q_len + (k + 1) * SS]
    return (y_prompt, y_sample)
```

```python
import math
import numpy as np
import ml_dtypes
import concourse.bass as bass
import concourse.mybir as mybir
from concourse.bass_utils import run_bass_kernel_spmd

F32 = mybir.dt.float32
BF16 = mybir.dt.bfloat16
AF = mybir.ActivationFunctionType
ALU = mybir.AluOpType

D = 1024
BW = 2048
NH = 16
NKV = 4
HD = 128
ATT_IN = 5120
POOL_IN = 4096
EPS = 1e-6
ROPE_THETA = 500000.0
QSCALE = 1.0 / math.sqrt(128.0)

RX = 10
NW = 5
EPOCH = 4000
SKIP = set()


class Buf:
    __slots__ = ("name", "w", "r", "psum")

    def __init__(self, name, psum=False):
        self.name = name
        self.w = None
        self.r = []
        self.psum = psum


class Op:
    __slots__ = ("eng", "fn", "deps", "inc", "order", "tick", "dma", "sem_i", "tgt", "prev", "tag")


class Sched:
    ENGS = ("pe", "act", "dve", "pool", "sp")
    NDS = {"sp": 8, "pool": 8, "act": 2}

    def __init__(self):
        self.q = {e: [] for e in self.ENGS}
        self.order = 0
        self.dma_cnt = {e: 0 for e in self.ENGS}
        self.dma_last = {}
        self.tag = ""

    def op(self, eng, fn, r=(), w=(), dma=False):
        o = Op()
        o.tag = self.tag
        o.eng = eng
        o.fn = fn
        o.inc = False
        o.dma = dma
        o.order = self.order
        self.order += 1
        o.prev = None
        if dma:
            n = self.NDS[eng]
            o.sem_i = self.dma_cnt[eng] % n
            self.dma_cnt[eng] += 1
            key = (eng, o.sem_i)
            o.prev = self.dma_last.get(key)
            o.tgt = (o.prev.tgt if o.prev is not None else 0) + 16
            self.dma_last[key] = o
        best = {}

        def add(d):
            if d is None or d is o:
                return
            if d.dma:
                key = ("d", d.eng, d.sem_i)
            else:
                if d.eng == "pe" and eng == "pe" and not dma:
                    return
                key = ("e", d.eng)
            c = best.get(key)
            if c is None or d.order > c.order:
                best[key] = d

        for b in r:
            add(b.w)
            if b.psum:
                for x in b.r:
                    if x.eng != eng:
                        add(x)
        for b in w:
            add(b.w)
            for x in b.r:
                add(x)
        o.deps = list(best.values())
        for d in o.deps:
            d.inc = True
        for b in r:
            b.r.append(o)
        for b in w:
            b.w = o
            b.r = []
        self.q[eng].append(o)
        return o

    def emit(self, nc, engines, esems, dsems):
        for e in self.ENGS:
            c = 0
            for o in self.q[e]:
                if o.inc and not o.dma:
                    c += 1
                    o.tick = c
        need = {e: 1 for e in self.ENGS}
        for e in self.ENGS:
            c = sum(1 for o in self.q[e] if o.inc and not o.dma)
            need[e] = max(1, (c + EPOCH - 1) // EPOCH)
        return need

    def run_engine(self, eng, E, esems, dsems):
        seen = {}

        def wait(key, sem, val):
            if seen.get(key, 0) >= val:
                return
            seen[key] = val
            E.wait_ge(sem, val)

        for o in self.q[eng]:
            if o.dma and o.prev is not None:
                wait(("d", o.eng, o.sem_i), dsems[o.eng][o.sem_i], o.prev.tgt)
            for d in o.deps:
                if d.dma:
                    wait(("d", d.eng, d.sem_i), dsems[d.eng][d.sem_i], d.tgt)
                else:
                    ep = (d.tick - 1) // EPOCH
                    wait(("e", d.eng, ep), esems[d.eng][ep], (d.tick - 1) % EPOCH + 1)
            ins = o.fn(E)
            if o.dma:
                ins.then_inc(dsems[o.eng][o.sem_i], 16)
            elif o.inc:
                ep = (o.tick - 1) // EPOCH
                ins.then_inc(esems[eng][ep], 1)


def seg_info(segs):
    out = []
    xo = 0
    yo = 0
    for kind, n_ext in segs:
        n_own = n_ext - 4 if kind == "halo" else n_ext
        out.append(dict(kind=kind, n_ext=n_ext, n_own=n_own, xo=xo, yo=yo))
        xo += n_ext
        yo += n_own
    return out, xo, yo


def build_program(segs, debug=False, limit=None):
    S = Sched()
    sinfo, NEXT, NOWN = seg_info(segs)
    nc = bass.Bass("TRN2", target_bir_lowering=False)

    def din(name, shape, dt):
        return nc.dram_tensor(name, list(shape), dt, kind="ExternalInput").ap()

    x_d = din("x", [NEXT * 128, D], F32)
    rope_d = din("rope", [NEXT * 128, 64], F32)
    masks_d = din("masks", [128, 4, 128], BF16)
    ident_d = din("ident", [128, 128], BF16)
    gpre_d = din("gpre", [128, 2, 8], F32)
    gpost_d = din("gpost", [128, 2, D], F32)
    sink_d = din("sinkb", [128, NH], F32)
    pscale_d = din("pscale", [128, 16], F32)
    icnt_d = din("icnt", [128, 2, 2, 4, 8], F32)
    w_in1 = din("attn_w_in", [D, ATT_IN], F32)
    w_out1 = din("attn_w_out", [BW, D], F32)
    w_in2 = din("pool_w_in", [D, POOL_IN], F32)
    w_grp = din("pool_w_group", [BW, 512], F32)
    w_out2 = din("pool_w_out", [BW, D], F32)
    y_d = nc.dram_tensor("y", [NOWN * 128, D], F32, kind="ExternalOutput").ap()
    dbg_d = nc.dram_tensor("dbg_x1", [NEXT * 128, D], F32, kind="ExternalOutput").ap() if debug else None

    def dscr(name, shape):
        return nc.dram_tensor(name, list(shape), BF16, kind="Internal").ap()

    b_in1 = dscr("b_in1", [D, ATT_IN])
    b_out1 = dscr("b_out1", [BW, D])
    b_in2 = dscr("b_in2", [D, POOL_IN])
    b_grp = dscr("b_grp", [BW, 512])
    b_out2 = dscr("b_out2", [BW, D])

    from contextlib import ExitStack

    es = ExitStack()

    def sb(name, shape, dt):
        return es.enter_context(nc.sbuf_tensor(name, list(shape), dt))

    with es:
        xres = sb("xres", [128, RX, D], F32)
        hb = sb("hb", [128, 4, D], BF16)
        hT = sb("hT", [128, 8, 640], BF16)
        kT = sb("kT", [128, 8, 512], BF16)
        vv = sb("vv", [128, 8, 512], BF16)
        qtok = sb("qtok", [128, 2, 4, 512], BF16)
        ktok = qtok[:, 1]
        qT = sb("qT", [128, 2, 4, 512], BF16)
        pT = sb("pT", [128, 2, 3, 512], BF16)
        rden = sb("rden", [128, 2, 512], F32)
        otmp = sb("otmp", [128, 2, 512], F32)
        aT = sb("aT", [128, 16, 512], BF16)
        h2T = sb("h2T", [128, 8, 528], BF16)
        h2halo = sb("h2halo", [128, 8, 8], BF16)
        uT = sb("uT", [128, 4, 528], F32)
        ptmp = sb("ptmp", [128, 4, 528], F32)
        pfix = sb("pfix", [128, 8], F32)
        pmix = sb("pmix", [128, 2, 512], F32)
        wring = sb("wring", [128, NW, 8, 512], BF16)
        ropet = sb("ropet", [128, 2, 5, 64], F32)
        rtmp = sb("rtmp", [128, 2, 4, 128], F32)
        stat = sb("stat", [128, 16, 4], F32)
        ident = sb("ident_s", [128, 128], BF16)
        ones = sb("ones_s", [128, 128], BF16)
        masks = sb("masks_s", [128, 4, 128], BF16)
        gpre = sb("gpre_s", [128, 2, 8], F32)
        gpost = sb("gpost_s", [128, 2, D], F32)
        esink = sb("esink_s", [128, NH], F32)
        esrow = sb("esrow_s", [1, NH], BF16)
        pscale = sb("pscale_s", [128, 16], F32)
        icnt = sb("icnt_s", [128, 2, 2, 4, 8], F32)
        epsb = sb("epsb", [128, 1], F32)
        ps = es.enter_context(nc.psum_tensor("ps", [128, 8, 512], F32))

        B_x = [Buf(f"x{i}") for i in range(RX)]
        B_hb = [Buf(f"hb{i}") for i in range(4)]
        B_hT = [Buf(f"hT{i}") for i in range(5)]
        B_kT = [Buf(f"kT{i}") for i in range(8)]
        B_v = [Buf(f"v{i}") for i in range(8)]
        B_qtok = [Buf(f"qtok{i}") for i in range(2)]
        B_ktok = B_qtok[1]
        B_qT = [Buf(f"qT{i}") for i in range(2)]
        B_pT = [Buf(f"pT{i}") for i in range(2)]
        B_rden = [Buf(f"rden{i}") for i in range(2)]
        B_otmp = [Buf(f"otmp{i}") for i in range(2)]
        B_aT = [Buf(f"aT{i}") for i in range(4)]
        B_h2T = Buf("h2T")
        B_h2halo = Buf("h2halo")
        B_uT = [Buf(f"uT{i}") for i in range(4)]
        B_ptmp = [Buf("pa"), Buf("pb"), Buf("pa2"), Buf("pb2")]
        B_pfix = Buf("pfix")
        B_pmix = [Buf(f"pmix{i}") for i in range(2)]
        B_w = [Buf(f"w{i}") for i in range(NW)]
        B_rope = [Buf(f"rope{i}") for i in range(2)]
        B_rtmp = [Buf("rta"), Buf("rtb")]
        B_stat = [Buf(f"stat{i}") for i in range(16)]
        B_ps = [Buf(f"ps{i}", psum=True) for i in range(8)]
        B_const = Buf("const")

        state = dict(bank=0, stat=0, hb=0)
        hb_out = set()

        def alloc_banks(n):
            b = state["bank"]
            if b + n > 8:
                b = 0
            state["bank"] = (b + n) % 8
            return b

        def alloc_stat():
            s = state["stat"]
            state["stat"] = (s + 1) % 16
            return s

        def dma(eng, out, in_, r, w):
            return S.op(eng, lambda E, out=out, in_=in_: E.dma_start(out=out, in_=in_), r=r, w=w, dma=True)

        def mm(out, lhsT, rhs, start, stop, r, w):
            return S.op(
                "pe",
                lambda E, out=out, lhsT=lhsT, rhs=rhs, start=start, stop=stop: E.matmul(
                    out, lhsT=lhsT, rhs=rhs, start=start, stop=stop
                ),
                r=r,
                w=w,
            )

        def tr(out, in_, idn, r, w):
            return S.op(
                "pe",
                lambda E, out=out, in_=in_, idn=idn: E.transpose(out=out, in_=in_, identity=idn),
                r=r,
                w=w,
            )

        def act(out, in_, func, r, w, scale=None, bias=None, accum=None):
            def fn(E, out=out, in_=in_, func=func, scale=scale, bias=bias, accum=accum):
                kw = {}
                if scale is not None:
                    kw["scale"] = scale
                if bias is not None:
                    kw["bias"] = bias
                if accum is not None:
                    kw["accum_out"] = accum
                return E.activation(out=out, in_=in_, func=func, **kw)

            return S.op("act", fn, r=r, w=w)

        def tt(eng, out, in0, in1, op, r, w):
            return S.op(
                eng,
                lambda E, out=out, in0=in0, in1=in1, op=op: E.tensor_tensor(out=out, in0=in0, in1=in1, op=op),
                r=r,
                w=w,
            )

        def ts(eng, out, in0, s1, op0, r, w, s2=None, op1=None):
            def fn(E, out=out, in0=in0, s1=s1, op0=op0, s2=s2, op1=op1):
                if op1 is None:
                    return E.tensor_scalar(out=out, in0=in0, scalar1=s1, scalar2=None, op0=op0)
                return E.tensor_scalar(out=out, in0=in0, scalar1=s1, scalar2=s2, op0=op0, op1=op1)

            return S.op(eng, fn, r=r, w=w)

        def stt(out, in0, scalar, in1, op0, op1, r, w):
            return S.op(
                "dve",
                lambda E, out=out, in0=in0, scalar=scalar, in1=in1, op0=op0, op1=op1: E.scalar_tensor_tensor(
                    out=out, in0=in0, scalar=scalar, in1=in1, op0=op0, op1=op1
                ),
                r=r,
                w=w,
            )

        def cp(eng, out, in_, r, w):
            if eng == "act":
                return act(out, in_, AF.Copy, r, w)
            return S.op(eng, lambda E, out=out, in_=in_: E.tensor_copy(out=out, in_=in_), r=r, w=w)

        def recip(out, in_, r, w):
            return S.op("dve", lambda E, out=out, in_=in_: E.reciprocal(out=out, in_=in_), r=r, w=w)

        def memset(eng, ap, val, w):
            return S.op(eng, lambda E, ap=ap, val=val: E.memset(ap, val), r=(), w=w)

        def psbf(bank):
            return ps[:, bank, :].bitcast(BF16)

        for dst, src in (
            (ident[:], ident_d[:, :]),
            (masks[:], masks_d[:, :, :]),
            (gpre[:], gpre_d[:, :, :]),
            (gpost[:], gpost_d[:, :, :]),
            (esink[:], sink_d[:, :]),
            (pscale[:], pscale_d[:, :]),
            (icnt[:], icnt_d[:, :, :, :, :]),
        ):
            dma("sp", dst, src, r=(), w=(B_const,))
        memset("dve", ones[:], 2.0, w=(B_const,))
        memset("dve", epsb[:], EPS, w=(B_const,))
        act(esink[:], esink[:], AF.Exp, r=(B_const,), w=(B_const,))
        cp("dve", esrow[:], esink[0:1, :], r=(B_const,), w=(B_const,))
        if debug:
            for t_, bl in ((hT, B_hT), (kT, B_kT), (vv, B_v), (qtok, B_qtok), (qT, B_qT), (aT, B_aT), (ktok, [B_ktok])):
                memset("pool", t_[:], 0.0, w=tuple(bl))
        v_in1 = b_in1.rearrange("(kc p) n -> p kc n", p=128)
        v_out1 = b_out1.rearrange("(kc p) n -> p kc n", p=128)
        v_in2 = b_in2.rearrange("(kc p) n -> p kc n", p=128)
        v_grp = b_grp.rearrange("(kc p) n -> p kc n", p=128)
        v_out2 = b_out2.rearrange("(kc p) n -> p kc n", p=128)

        def piece_src(key):
            k = key[0]
            if k == "k":
                return v_in1[:, :, 2048:2560], 8, (b_in1, w_in1, 0, D, 2048, 2560)
            if k == "v":
                return v_in1[:, :, 2560:3072], 8, (b_in1, w_in1, 0, D, 2560, 3072)
            if k == "q":
                j = key[1]
                return v_in1[:, :, 512 * j:512 * j + 512], 8, (b_in1, w_in1, 0, D, 512 * j, 512 * j + 512)
            if k == "g":
                j = key[1]
                c0 = 3072 + 512 * j
                return v_in1[:, :, c0:c0 + 512], 8, (b_in1, w_in1, 0, D, c0, c0 + 512)
            if k == "o1":
                h, kk = key[1], key[2]
                return v_out1[:, 8 * kk:8 * kk + 8, 512 * h:512 * h + 512], 8, (b_out1, w_out1, 1024 * kk, 1024 * kk + 1024, 512 * h, 512 * h + 512)
            if k == "u":
                g = key[1]
                return v_in2[:, :, 512 * g:512 * g + 512], 8, (b_in2, w_in2, 0, D, 512 * g, 512 * g + 512)
            if k == "g2":
                g = key[1]
                c0 = 2048 + 512 * g
                return v_in2[:, :, c0:c0 + 512], 8, (b_in2, w_in2, 0, D, c0, c0 + 512)
            if k == "grp":
                g = key[1]
                return v_grp[:, 4 * g:4 * g + 4, :], 4, (b_grp, w_grp, 512 * g, 512 * g + 512, 0, 512)
            if k == "o2":
                h, kk = key[1], key[2]
                return v_out2[:, 8 * kk:8 * kk + 8, 512 * h:512 * h + 512], 8, (b_out2, w_out2, 1024 * kk, 1024 * kk + 1024, 512 * h, 512 * h + 512)
            raise KeyError(key)

        B_cast = {}

        def cast_issue(key):
            if key in B_cast:
                return
            _, _, (dst, src, r0, r1, c0, c1) = piece_src(key)
            B_cast[key] = Buf("cast")
            dma("pool", dst[r0:r1, c0:c1], src[r0:r1, c0:c1], r=(), w=(B_cast[key],))

        steps = []
        for si, sg in enumerate(sinfo):
            kind, n_ext = sg["kind"], sg["n_ext"]
            nst = n_ext // 4
            l2next = 0
            needs_q = (lambda b, n=n_ext, k=kind: (1 <= b <= n - 2) if k == "halo" else (0 <= b <= n - 1))
            needs_l2 = (lambda b, n=n_ext, k=kind: (2 <= b <= n - 3) if k == "halo" else (0 <= b <= n - 1))
            glist = list(range(nst)) + (["flush"] if kind == "full" else [])
            for G in glist:
                if G == "flush":
                    new = []
                    base = n_ext - 1
                    qbs = [n_ext - 1]
                    lim = n_ext - 1
                else:
                    new = list(range(4 * G, 4 * G + 4))
                    base = 4 * G - 1
                    qbs = [b for b in range(4 * G - 1, 4 * G + 3) if b >= 0 and needs_q(b)]
                    lim = 4 * G + 1
                l2bs = [b for b in range(l2next, min(lim, n_ext - 1) + 1) if needs_l2(b)]
                l2next = max(l2next, lim + 1)
                steps.append(dict(si=si, G=G, new=new, base=base, qbs=qbs, l2bs=l2bs))
        for si, sg in enumerate(sinfo):
            mine = [s for s in steps if s["si"] == si and s["l2bs"]]
            for s in steps:
                if s["si"] == si:
                    s["l2first"] = False
                    s["l2last"] = False
            mine[0]["l2first"] = True
            mine[-1]["l2last"] = True

        def phase_pieces(ph, st):
            if ph == "AB":
                return [("k",), ("v",)] if st["new"] else []
            if ph == "Cpre":
                return [("q", 0), ("q", 1)]
            if ph == "C":
                return [("g", 0), ("g", 1), ("q", 2), ("g", 2), ("q", 3), ("g", 3)]
            if ph == "DE":
                return [("o1", 0, 0), ("o1", 0, 1), ("o1", 1, 0), ("o1", 1, 1)]
            if ph == "F":
                return [("g2", 3), ("u", 3), ("g2", 0), ("u", 0), ("g2", 2), ("u", 2), ("g2", 1), ("u", 1)]
            if ph == "H":
                return [("o2", 0, 0), ("o2", 0, 1), ("o2", 1, 0), ("o2", 1, 1)]
            return []

        sched = []
        hasA = lambda st: bool(st["new"]) or st["G"] == "flush"
        if hasA(steps[0]):
            sched.append(("AB", 0))
        sched.append(("Cpre", 0))
        for i, st in enumerate(steps):
            nxt = steps[i + 1] if i + 1 < len(steps) else None
            if st["qbs"]:
                sched.append(("C", i))
                sched.append(("DE", i))
            if nxt is not None and hasA(nxt):
                sched.append(("AB", i + 1))
            elif st["l2bs"]:
                sched.append(("E2", i))
            if st["l2bs"]:
                sched.append(("F", i))
            if nxt is not None:
                sched.append(("Cpre", i + 1))
            if st["l2bs"]:
                sched.append(("H", i))
            sched.append(("END", i))

        allp = []
        for ph, i in sched:
            allp += phase_pieces(ph, steps[i])
        wst = dict(issued=0, used=0, rel=0)

        CAST_AHEAD = 6

        def w_issue_upto(n):
            while wst["issued"] < min(n, len(allp)):
                i = wst["issued"]
                for k2 in allp[i:i + CAST_AHEAD]:
                    cast_issue(k2)
                src, nk, _ = piece_src(allp[i])
                slot = i % NW
                dma("sp", wring[:, slot, 0:nk, :], src, r=(B_cast[allp[i]],), w=(B_w[slot],))
                wst["issued"] += 1

        def wget(key):
            i = wst["used"]
            assert allp[i] == key, (allp[i], key)
            assert i < wst["issued"], "weight piece not issued"
            wst["used"] += 1
            return i % NW

        def wrel(n=1):
            wst["rel"] += n
            assert wst["rel"] <= wst["used"]
            w_issue_upto(wst["rel"] + NW)

        def xslot(si, b):
            return (sinfo[si]["xo"] + b) % RX

        def issue_rope(st):
            si = st["si"]
            xo = sinfo[si]["xo"]
            rb = st["ri"]
            lo = max(st["base"], 0)
            hi = st["base"] + 4 if st["new"] else st["base"]
            i0 = lo - st["base"]
            cnt = hi - lo + 1
            src = rope_d[(xo + lo) * 128:(xo + hi + 1) * 128, :].rearrange("(b p) c -> p b c", p=128)
            dma("sp", ropet[:, rb, i0:i0 + cnt, :], src, r=(), w=(B_rope[rb],))

        def right_halo_block(st):
            sg = sinfo[st["si"]]
            br = st["l2bs"][-1] + 1
            ok = (br <= sg["n_ext"] - 1) and ((1 <= br <= sg["n_ext"] - 2) if sg["kind"] == "halo" else True)
            return br if ok else None

        last_use = {}
        for pos, (ph, i) in enumerate(sched):
            st = steps[i]
            used = set()
            if ph == "AB":
                used |= set(st["new"])
                if i >= 1 and steps[i - 1]["l2bs"]:
                    pst = steps[i - 1]
                    for b in list(pst["l2bs"]) + ([right_halo_block(pst)] if right_halo_block(pst) is not None else []):
                        last_use[(pst["si"], b)] = pos
            elif ph == "DE":
                used |= set(st["qbs"])
                if st["l2bs"] and st["l2first"] and sinfo[st["si"]]["kind"] == "halo":
                    used.add(st["l2bs"][0] - 1)
            elif ph == "E2":
                used |= set(st["l2bs"])
                br = right_halo_block(st)
                if br is not None:
                    used.add(br)
            elif ph == "H":
                used |= set(st["l2bs"])
            for b in used:
                last_use[(st["si"], b)] = pos
        xq = [(i, st["si"], b) for i, st in enumerate(steps) for b in st["new"]]
        xst = dict(next=0)
        slot_free = [True] * RX

        def try_issue_x(max_step):
            while xst["next"] < len(xq):
                i, si, b = xq[xst["next"]]
                if i > max_step:
                    break
                sl = xslot(si, b)
                if not slot_free[sl]:
                    break
                slot_free[sl] = False
                xo = sinfo[si]["xo"]
                dma("sp", xres[:, sl, :], x_d[(xo + b) * 128:(xo + b + 1) * 128, :], r=(), w=(B_x[sl],))
                xst["next"] += 1

        def release_x(pos):
            for (si, b), lu in last_use.items():
                if lu == pos:
                    slot_free[xslot(si, b)] = True

        for i, st in enumerate(steps):
            st["ri"] = i % 2

        def rmsnorm_to_hb(xslot_i, layer):
            hs = state["hb"]
            for _ in range(4):
                if hs not in hb_out:
                    break
                hs = (hs + 1) % 4
            assert hs not in hb_out, "hb staging ring exhausted"
            hb_out.add(hs)
            state["hb"] = (hs + 1) % 4
            s0 = alloc_stat()
            act(hb[:, hs, :], xres[:, xslot_i, :], AF.Square, r=(B_x[xslot_i],), w=(B_hb[hs], B_stat[s0]),
                accum=stat[:, s0, 0:1])
            act(stat[:, s0, 1:2], stat[:, s0, 0:1], AF.Sqrt, r=(B_const,), w=(B_stat[s0],), scale=1.0 / D, bias=epsb[:, 0:1])
            recip(stat[:, s0, 2:3], stat[:, s0, 1:2], r=(), w=(B_stat[s0],))
            ts("dve", hb[:, hs, :], xres[:, xslot_i, :], stat[:, s0, 2:3], ALU.mult, r=(B_x[xslot_i], B_stat[s0]), w=(B_hb[hs],))
            return hs

        def rope_evac(b0, n, dst4, dst_bufs, rb, ri0, extra_r=()):
            psv = ps[:, b0:b0 + n, :].rearrange("p n (h d) -> p n h d", h=4)
            pbufs = tuple(B_ps[b0 + i] for i in range(n))
            tab = ropet[:, rb, ri0:ri0 + n, :]
            cc = tab[:, :, 0:32].unsqueeze(2).to_broadcast([128, n, 4, 32])
            nsn = tab[:, :, 32:48].unsqueeze(2).to_broadcast([128, n, 4, 16])
            psn = tab[:, :, 48:64].unsqueeze(2).to_broadcast([128, n, 4, 16])
            ta = rtmp[:, 0, 0:n, :].rearrange("p n (h d) -> p n h d", h=4)
            tb = rtmp[:, 1, 0:n, :].rearrange("p n (h d) -> p n h d", h=4)
            if "rope_act" not in SKIP:
                cp("act", dst4[:, :, :, 32:128], psv[:, :, :, 32:128], r=pbufs, w=dst_bufs)
            if "rope_dve" not in SKIP:
                tt("dve", ta, psv[:, :, :, 0:32], cc, ALU.mult, r=pbufs + (B_rope[rb],), w=(B_rtmp[0],))
                tt("dve", tb[:, :, :, 0:16], psv[:, :, :, 16:32], nsn, ALU.mult, r=pbufs + (B_rope[rb],), w=(B_rtmp[1],))
                tt("dve", tb[:, :, :, 16:32], psv[:, :, :, 0:16], psn, ALU.mult, r=pbufs + (B_rope[rb],), w=(B_rtmp[1],))
            if "rope_pool" not in SKIP:
                tt("pool", dst4[:, :, :, 0:32], ta, tb, ALU.add, r=(B_rtmp[0], B_rtmp[1]), w=dst_bufs)

        def kslot(b):
            return b % 8

        def phase_AB(st, inject=None):
            si = st["si"]
            new = st["new"]
            n = len(new)
            rb = st["ri"]
            if st["G"] != 0:
                cp("pool", hT[:, :, 0:128], hT[:, :, 512:640], r=(B_hT[4],), w=(B_hT[0],))
            if not new:
                for ch, pe in (inject or []):
                    ch()
                    pe()
                return
            wk = wget(("k",))
            wv = wget(("v",))
            kb0 = alloc_banks(n)
            inject = list(inject) if inject else []
            hsl = {}

            def chain(i):
                S.tag = "A"
                hsl[i] = rmsnorm_to_hb(xslot(si, new[i]), 0)

            def trans(i):
                S.tag = "A"
                cg = new[i] - st["base"]
                hs = hsl[i]
                bank = alloc_banks(1)
                if kb0 <= bank < kb0 + n:
                    state["bank"] = (kb0 + n) % 8
                    bank = alloc_banks(1)
                pb = psbf(bank)
                for kc in range(8):
                    tr(pb[:, kc * 128:(kc + 1) * 128], hb[:, hs, kc * 128:(kc + 1) * 128], ident[:], r=(B_hb[hs], B_const), w=(B_ps[bank],))
                hb_out.discard(hs)
                tt("dve", hT[:, :, cg * 128:(cg + 1) * 128], pb.rearrange("p (c t) -> p c t", c=8),
                   gpre[:, 0, :].unsqueeze(2).to_broadcast([128, 8, 128]), ALU.mult, r=(B_ps[bank], B_const), w=(B_hT[cg],))

            pre = st.get("pre_chains")
            if pre is not None:
                hsl.update(pre)
            else:
                chain(0)
                if n > 1:
                    chain(1)
            for i in range(n):
                trans(i)
                if i + 2 < n:
                    chain(i + 2)
                if i >= 1:
                    if inject and i + 2 >= n:
                        ch, pe = inject.pop(0)
                        ch()
                        kv_block(st, i - 1, kb0, n, wk, wv)
                        pe()
                    else:
                        kv_block(st, i - 1, kb0, n, wk, wv)
            if inject:
                ch, pe = inject.pop(0)
                ch()
                kv_block(st, n - 1, kb0, n, wk, wv)
                pe()
            else:
                kv_block(st, n - 1, kb0, n, wk, wv)
            for ch, pe in inject:
                ch()
                pe()
            wrel(2)
            S.tag = "B"
            rope_evac(kb0, n, ktok[:, 0:n, :].rearrange("p n (h d) -> p n h d", h=4), (B_ktok,), rb, new[0] - st["base"])
            for i0 in range(0, n, 2):
                bank = alloc_banks(1)
                pb = psbf(bank)
                m = min(2, n - i0)
                for ii in range(m):
                    for h in range(4):
                        tr(pb[:, ii * 512 + h * 128: ii * 512 + (h + 1) * 128], ktok[:, i0 + ii, h * 128:(h + 1) * 128], ident[:],
                           r=(B_ktok, B_const), w=(B_ps[bank],))
                for ii in range(m):
                    sl = kslot(new[i0 + ii])
                    cp("dve", kT[:, sl, :], pb[:, ii * 512:(ii + 1) * 512], r=(B_ps[bank],), w=(B_kT[sl],))

        def kv_block(st, i, kb0, n, wk, wv):
            S.tag = "B"
            b = st["new"][i]
            cg = b - st["base"]
            for kc in range(8):
                mm(ps[:, kb0 + i, :], hT[:, kc, cg * 128:(cg + 1) * 128], wring[:, wk, kc, :], kc == 0, kc == 7,
                   r=(B_hT[cg], B_w[wk]), w=(B_ps[kb0 + i],))
            bank = alloc_banks(1)
            if kb0 <= bank < kb0 + n:
                state["bank"] = (kb0 + n) % 8
                bank = alloc_banks(1)
            for kc in range(8):
                mm(ps[:, bank, :], hT[:, kc, cg * 128:(cg + 1) * 128], wring[:, wv, kc, :], kc == 0, kc == 7,
                   r=(B_hT[cg], B_w[wv]), w=(B_ps[bank],))
            sl = kslot(b)
            cp("act", vv[:, sl, :], ps[:, bank, :], r=(B_ps[bank],), w=(B_v[sl],))

        def phase_C(st, part="main"):
            si = st["si"]
            sg = sinfo[si]
            kind, n_ext = sg["kind"], sg["n_ext"]
            qbs = st["qbs"]
            n = len(qbs)
            rb = st["ri"]
            cg0 = qbs[0] - st["base"]

            held = {}

            def Qpart(j, i):
                jb = j % 2
                S.tag = "C-q"
                if ("q", j) not in held:
                    held[("q", j)] = wget(("q", j))
                ws = held[("q", j)]
                b0 = alloc_banks(1)
                cg = qbs[i] - st["base"]
                for kc in range(8):
                    mm(ps[:, b0, :], hT[:, kc, cg * 128:(cg + 1) * 128], wring[:, ws, kc, :], kc == 0, kc == 7,
                       r=(B_hT[cg], B_w[ws]), w=(B_ps[b0],))
                rope_evac(b0, 1, qtok[:, jb, i:i + 1, :].rearrange("p n (h d) -> p n h d", h=4), (B_qtok[jb],), rb, cg)

            def Gpart(j, m):
                S.tag = "C-g"
                if ("g", j) not in held:
                    held[("g", j)] = wget(("g", j))
                ws = held[("g", j)]
                bank = alloc_banks(1)
                for kc in range(8):
                    mm(ps[:, bank, 0:n * 128], wring[:, ws, kc, m * 128:(m + 1) * 128], hT[:, kc, cg0 * 128:(cg0 + n) * 128],
                       kc == 0, kc == 7, r=tuple(B_hT[cg0 + i] for i in range(n)) + (B_w[ws],), w=(B_ps[bank],))
                gdst = aT[:, 4 * j + m, 0:n * 128]
                act(gdst, ps[:, bank, 0:n * 128], AF.Tanh, r=(B_ps[bank],), w=tuple(B_aT[i] for i in range(n)), scale=0.5)
                stt(gdst, gdst, 1.0, ps[:, bank, 0:n * 128], ALU.add, ALU.mult, r=(B_ps[bank],), w=tuple(B_aT[i] for i in range(n)))

            def Qp(j):
                jb = j % 2
                S.tag = "C-q"
                ws = wget(("q", j))
                b0 = alloc_banks(n)
                for i, b in enumerate(qbs):
                    cg = b - st["base"]
                    for kc in range(8):
                        mm(ps[:, b0 + i, :], hT[:, kc, cg * 128:(cg + 1) * 128], wring[:, ws, kc, :], kc == 0, kc == 7,
                           r=(B_hT[cg], B_w[ws]), w=(B_ps[b0 + i],))
                wrel()
                rope_evac(b0, n, qtok[:, jb, 0:n, :].rearrange("p n (h d) -> p n h d", h=4), (B_qtok[jb],), rb, cg0)

            def Gp(j):
                for m in range(4):
                    Gpart(j, m)
                wrel()

            def Tq(j):
                jb = j % 2
                S.tag = "C-T"
                for i0 in range(0, n, 2):
                    bank = alloc_banks(1)
                    pb = psbf(bank)
                    m2 = min(2, n - i0)
                    for ii in range(m2):
                        for h in range(4):
                            tr(pb[:, ii * 512 + h * 128: ii * 512 + (h + 1) * 128], qtok[:, jb, i0 + ii, h * 128:(h + 1) * 128], ident[:],
                               r=(B_qtok[jb], B_const), w=(B_ps[bank],))
                    cp("dve", qT[:, jb, i0:i0 + m2, :], pb[:, 0:m2 * 512].rearrange("p (n c) -> p n c", n=m2), r=(B_ps[bank],), w=(B_qT[jb],))

            def key_blocks(b):
                kbs = []
                if b - 1 >= 0:
                    kbs.append((b - 1, 2 if (kind == "halo" and b == 2) else 0))
                kbs.append((b, None))
                if b + 1 <= n_ext - 1:
                    kbs.append((b + 1, 3 if (kind == "halo" and b == n_ext - 3) else 1))
                return kbs

            def Sst(j, i):
                S.tag = "C-S"
                jb = j % 2
                pbi = (j * n + i) % 2
                kbs = key_blocks(qbs[i])
                nk = len(kbs)
                b0 = alloc_banks(nk)
                for kidx, (kb, mi) in enumerate(kbs):
                    bank = b0 + kidx
                    sl = kslot(kb)
                    mm(ps[:, bank, :], kT[:, sl, j * 128:(j + 1) * 128], qT[:, jb, i, :], True, mi is None,
                       r=(B_kT[sl], B_qT[jb]), w=(B_ps[bank],))
                    if mi is not None:
                        mm(ps[:, bank, :], ident[:], masks[:, mi, :].unsqueeze(1).to_broadcast([128, 4, 128]), False, True, r=(B_const,), w=(B_ps[bank],))
                act(pT[:, pbi, 0:nk, :], ps[:, b0:b0 + nk, :], AF.Exp, r=tuple(B_ps[b0 + k] for k in range(nk)), w=(B_pT[pbi],), scale=QSCALE)

            def PVs(j, i):
                S.tag = "C-PV"
                pbi = (j * n + i) % 2
                kbs = key_blocks(qbs[i])
                nk = len(kbs)
                bd = alloc_banks(1)
                for kidx in range(nk):
                    mm(ps[:, bd, :], ones[:], pT[:, pbi, kidx, :], kidx == 0, False,
                       r=(B_const, B_pT[pbi]), w=(B_ps[bd],))
                mm(ps[:, bd, :], ones[0:1, :], esrow[0:1, 4 * j:4 * j + 4].unsqueeze(2).to_broadcast([1, 4, 128]), False, True, r=(B_const,), w=(B_ps[bd],))
                bo = alloc_banks(1)
                for kidx, (kb, mi) in enumerate(kbs):
                    sl = kslot(kb)
                    mm(ps[:, bo, :], vv[:, sl, j * 128:(j + 1) * 128], pT[:, pbi, kidx, :], kidx == 0, kidx == nk - 1,
                       r=(B_v[sl], B_pT[pbi]), w=(B_ps[bo],))
                rd = rden[:, pbi, :]
                rd3 = rd.rearrange("p (h t) -> p h t", h=4)
                ot3 = otmp[:, pbi, :].rearrange("p (h t) -> p h t", h=4)
                av = aT[:, 4 * j:4 * j + 4, i * 128:(i + 1) * 128]
                tt("dve", ot3, ps[:, bo, :].rearrange("p (h t) -> p h t", h=4), av, ALU.mult, r=(B_ps[bo], B_aT[i]), w=(B_otmp[pbi],))
                recip(rd, ps[:, bd, :], r=(B_ps[bd],), w=(B_rden[pbi],))
                tt("pool", av, ot3, rd3, ALU.mult, r=(B_otmp[pbi], B_rden[pbi]), w=(B_aT[i],))

            if part == "pre":
                Qp(0)
                Qp(1)
                Tq(0)
                return
            Gp(0)
            items = [(j, i) for j in range(4) for i in range(n)]
            for idx in range(len(items) + 1):
                if idx < len(items):
                    j, i = items[idx]
                    Sst(j, i)
                    if i == min(1, n - 1) and j + 1 < 4:
                        Gp(j + 1)
                        Tq(j + 1)
                    if i == n - 1 and j + 2 < 4:
                        Qp(j + 2)
                if idx >= 1:
                    PVs(*items[idx - 1])

        def out_proj(st, blocks, layer, okey, after_block=None):
            si = st["si"]
            slots = {}
            for h in range(2):
                for kk in range(2):
                    slots[(h, kk)] = wget((okey, h, kk))
            for i, b in enumerate(blocks):
                S.tag = "D" if layer == 0 else "H"
                sl = xslot(si, b)
                b0 = alloc_banks(2)
                for h in range(2):
                    for c in range(16):
                        ws = slots[(h, c // 8)]
                        mm(ps[:, b0 + h, :], aT[:, c, i * 128:(i + 1) * 128], wring[:, ws, c % 8, :], c == 0, c == 15,
                           r=(B_aT[i], B_w[ws]), w=(B_ps[b0 + h],))
                if after_block is not None:
                    after_block(i)
                pv = ps[:, b0:b0 + 2, :].rearrange("p a b -> p (a b)")
                pbufs = (B_ps[b0], B_ps[b0 + 1])
                s0 = alloc_stat()
                act(otmp[:, 0, :].bitcast(BF16), pv, AF.Square, r=pbufs, w=(B_otmp[0], B_stat[s0]), accum=stat[:, s0, 0:1])
                act(stat[:, s0, 1:2], stat[:, s0, 0:1], AF.Sqrt, r=(B_const,), w=(B_stat[s0],), scale=1.0 / D, bias=epsb[:, 0:1])
                recip(stat[:, s0, 2:3], stat[:, s0, 1:2], r=(), w=(B_stat[s0],))
                tt("dve", pv, pv, gpost[:, layer, :], ALU.mult, r=(B_const,), w=pbufs)
                stt(xres[:, sl, :], pv, stat[:, s0, 2:3], xres[:, sl, :], ALU.mult, ALU.add, r=pbufs + (B_stat[s0],), w=(B_x[sl],))
            wrel(4)

        def E_pe(hs, col0, ncols, first_tok=None):
            g1 = gpre[:, 1, :]
            bank = alloc_banks(1)
            pb = psbf(bank)
            if first_tok == "head8":
                for kc in range(8):
                    tr(pb[:, kc * 8:(kc + 1) * 8], hb[0:8, hs, kc * 128:(kc + 1) * 128], ident[0:8, 0:8], r=(B_hb[hs], B_const), w=(B_ps[bank],))
                src = pb[:, 0:64].rearrange("p (c t) -> p c t", c=8)
            else:
                for kc in range(8):
                    tr(pb[:, kc * 128:(kc + 1) * 128], hb[:, hs, kc * 128:(kc + 1) * 128], ident[:], r=(B_hb[hs], B_const), w=(B_ps[bank],))
                src = pb.rearrange("p (c t) -> p c t", c=8)
                if first_tok == "tail8":
                    src = src[:, :, 120:128]
            hb_out.discard(hs)
            tt("dve", h2T[:, :, col0:col0 + ncols], src, g1.unsqueeze(2).to_broadcast([128, 8, ncols]), ALU.mult,
               r=(B_ps[bank], B_const), w=(B_h2T,))

        def E_tr_block(sl, col0, ncols, first_tok=None):
            hs = rmsnorm_to_hb(sl, 1)
            E_pe(hs, col0, ncols, first_tok)

        def E2_units(st):
            si = st["si"]
            l2 = st["l2bs"]
            n = len(l2)
            units = []
            while st["e_done"] < n:
                i = st["e_done"]
                st["e_done"] += 1
                box = {}

                def ch(i=i, box=box):
                    S.tag = "E"
                    box["hs"] = rmsnorm_to_hb(xslot(si, l2[i]), 1)

                def pe(i=i, box=box):
                    S.tag = "E"
                    E_pe(box["hs"], 8 + i * 128, 128)

                units.append((ch, pe))
            c0 = 8 + n * 128
            br = right_halo_block(st)
            box = {}

            def ch_r(box=box):
                S.tag = "E"
                if br is not None:
                    box["hs"] = rmsnorm_to_hb(xslot(si, br), 1)

            def pe_r(box=box):
                S.tag = "E"
                if br is not None:
                    E_pe(box["hs"], c0, 8, "head8")
                else:
                    memset("pool", h2T[:, :, c0:c0 + 8], 0.0, w=(B_h2T,))
                if not st["l2last"]:
                    cp("pool", h2halo[:], h2T[:, :, c0 - 8:c0], r=(B_h2T,), w=(B_h2halo,))

            units.append((ch_r, pe_r))
            return units

        def E_left(st):
            si = st["si"]
            kind = sinfo[si]["kind"]
            l2 = st["l2bs"]
            S.tag = "E"
            if st["l2first"]:
                if kind == "halo":
                    E_tr_block(xslot(si, l2[0] - 1), 0, 8, "tail8")
                else:
                    memset("pool", h2T[:, :, 0:8], 0.0, w=(B_h2T,))
            else:
                cp("pool", h2T[:, :, 0:8], h2halo[:], r=(B_h2halo,), w=(B_h2T,))

        def E_block(st, i):
            S.tag = "E"
            E_tr_block(xslot(st["si"], st["l2bs"][i]), 8 + i * 128, 128)

        def phase_DE(st, nxt=None):
            qbs, l2 = st["qbs"], st["l2bs"]
            st["e_done"] = 0
            if nxt is not None and len(nxt["new"]) >= 2:
                S.tag = "A"
                nxt["pre_chains"] = {k: rmsnorm_to_hb(xslot(nxt["si"], nxt["new"][k]), 0) for k in range(2)}
            if l2:
                E_left(st)

            def ready(bq, upto):
                return (bq not in qbs) or (qbs.index(bq) <= upto)

            pend = []

            def hook(idx):
                while pend:
                    hs, i = pend.pop(0)
                    S.tag = "E"
                    E_pe(hs, 8 + i * 128, 128)
                if l2 and st["e_done"] < len(l2) and ready(l2[st["e_done"]], idx - 1):
                    i = st["e_done"]
                    st["e_done"] += 1
                    S.tag = "E"
                    pend.append((rmsnorm_to_hb(xslot(st["si"], l2[i]), 1), i))

            if l2:
                hook(-1)
            out_proj(st, qbs, 0, "o1", after_block=hook if l2 else None)
            while pend:
                hs, i = pend.pop(0)
                S.tag = "E"
                E_pe(hs, 8 + i * 128, 128)

        def phase_E2(st):
            for ch, pe in E2_units(st):
                ch()
                pe()

        def phase_F(st):
            si = st["si"]
            sg = sinfo[si]
            kidx = 0 if sg["kind"] == "halo" else 1
            l2 = st["l2bs"]
            n = len(l2)
            N = n * 128
            W = N + 16
            if W <= 512:
                ranges = [(0, W)]
            else:
                ranges = [(0, W // 2), (W // 2, W)]
            cnt = dict(k=0)

            def G2p(g):
                S.tag = "F-g2"
                wg = wget(("g2", g))
                for m in range(4):
                    bank = alloc_banks(1)
                    for kc in range(8):
                        mm(ps[:, bank, 0:N], wring[:, wg, kc, m * 128:(m + 1) * 128], h2T[:, kc, 8:8 + N], kc == 0, kc == 7,
                           r=(B_h2T, B_w[wg]), w=(B_ps[bank],))
                    act(aT[:, 4 * g + m, 0:N], ps[:, bank, 0:N], AF.Silu, r=(B_ps[bank],), w=tuple(B_aT[i] for i in range(n)))
                wrel()

            def Up(g):
                S.tag = "F-u"
                wu = wget(("u", g))
                for m in range(4):
                    for (c0, c1) in ranges:
                        bank = alloc_banks(1)
                        for kc in range(8):
                            mm(ps[:, bank, 0:c1 - c0], wring[:, wu, kc, m * 128:(m + 1) * 128], h2T[:, kc, c0:c1], kc == 0, kc == 7,
                               r=(B_h2T, B_w[wu]), w=(B_ps[bank],))
                        cp("act", uT[:, m, c0:c1], ps[:, bank, 0:c1 - c0], r=(B_ps[bank],), w=(B_uT[m],))
                    if m == 3:
                        wrel()
                    U = uT[:, m, :]
                    Bu = B_uT[m]
                    pe_ = "pool" if m % 2 == 0 else "dve"
                    o_ = 0 if m % 2 == 0 else 2
                    pa = ptmp[:, o_, :]
                    pb_ = ptmp[:, o_ + 1, :]
                    Ba, Bb = B_ptmp[o_], B_ptmp[o_ + 1]
                    w_ = 2 ** (g + 1)
                    if g == 0:
                        tt(pe_, pa[:, 0:N], U[:, 7:7 + N], U[:, 8:8 + N], ALU.add, r=(Bu,), w=(Ba,))
                        box, Bbox = pa, Ba
                    elif g == 1:
                        tt(pe_, pa[:, 0:N + 2], U[:, 6:8 + N], U[:, 7:9 + N], ALU.add, r=(Bu,), w=(Ba,))
                        tt(pe_, pb_[:, 0:N], pa[:, 0:N], pa[:, 2:N + 2], ALU.add, r=(Ba,), w=(Bb,))
                        box, Bbox = pb_, Bb
                    elif g == 2:
                        tt(pe_, pa[:, 0:N + 6], U[:, 4:10 + N], U[:, 5:11 + N], ALU.add, r=(Bu,), w=(Ba,))
                        tt(pe_, pb_[:, 0:N + 4], pa[:, 0:N + 4], pa[:, 2:N + 6], ALU.add, r=(Ba,), w=(Bb,))
                        tt(pe_, pa[:, 0:N], pb_[:, 0:N], pb_[:, 4:N + 4], ALU.add, r=(Bb,), w=(Ba,))
                        box, Bbox = pa, Ba
                    else:
                        tt(pe_, pa[:, 0:N + 14], U[:, 0:N + 14], U[:, 1:N + 15], ALU.add, r=(Bu,), w=(Ba,))
                        tt(pe_, pb_[:, 0:N + 12], pa[:, 0:N + 12], pa[:, 2:N + 14], ALU.add, r=(Ba,), w=(Bb,))
                        tt(pe_, pa[:, 0:N + 8], pb_[:, 0:N + 8], pb_[:, 4:N + 12], ALU.add, r=(Bb,), w=(Ba,))
                        tt(pe_, pb_[:, 0:N], pa[:, 0:N], pa[:, 8:N + 8], ALU.add, r=(Ba,), w=(Bb,))
                        box, Bbox = pb_, Bb
                    pbi = cnt["k"] % 2
                    cnt["k"] += 1
                    pm = pmix[:, pbi, :]
                    stt(pm[:, 0:N], box[:, 0:N], 1.0 / w_, U[:, 8:8 + N], ALU.mult, ALU.subtract, r=(Bbox, Bu), w=(B_pmix[pbi],))
                    for (flag, fl, cs) in ((st["l2first"], 0, 0), (st["l2last"], 1, N - 8)):
                        if flag:
                            tt("dve", pfix[:, :], box[:, cs:cs + 8], icnt[:, kidx, fl, g, :], ALU.mult, r=(Bbox, B_const), w=(B_pfix,))
                            tt("dve", pm[:, cs:cs + 8], pfix[:, :], U[:, 8 + cs:16 + cs], ALU.subtract,
                               r=(B_pfix, Bu), w=(B_pmix[pbi],))
                    c = 4 * g + m
                    stt(aT[:, c, 0:N], pm[:, 0:N], pscale[:, c:c + 1], aT[:, c, 0:N], ALU.mult, ALU.mult,
                        r=(B_pmix[pbi], B_const), w=tuple(B_aT[i] for i in range(n)))

            for g in (3, 0, 2, 1):
                G2p(g)
                Up(g)

        def phase_store(st):
            si = st["si"]
            sg = sinfo[si]
            for b in st["l2bs"]:
                sl = xslot(si, b)
                ob = b - 2 if sg["kind"] == "halo" else b
                row = (sg["yo"] + ob) * 128
                dma("sp", y_d[row:row + 128, :], xres[:, sl, :], r=(B_x[sl],), w=())

        dbg_outs = {}

        def dbg(name, ap, shape, dt, bufs):
            if not debug or name in dbg_outs:
                return
            t = nc.dram_tensor("dbg_" + name, list(shape), dt, kind="ExternalOutput").ap()
            dbg_outs[name] = t
            dma("sp", t, ap, r=bufs, w=())

        def prologue_fuse():
            S.tag = "P"
            vin2_f = w_in2.rearrange("(kc p) n -> p kc n", p=128)
            vgrp_f = w_grp.rearrange("(kc p) n -> p kc n", p=128)
            sA, sB, sC, sD = 0, 1, 2, 3
            wC = wring[:, sC, :, :].rearrange("p a b -> p (a b)").rearrange("p (cc r) -> p cc r", cc=4)
            for g in range(4):
                B_cast[("u", g)] = Buf("fused")
                dma("pool", wring[:, sA, :, :], vin2_f[:, :, 512 * g:512 * g + 512], r=(), w=(B_w[sA],))
                dma("pool", wring[:, sB, 0:4, :], vgrp_f[:, 4 * g:4 * g + 4, :], r=(), w=(B_w[sB],))
                for cc in range(4):
                    bank = alloc_banks(1)
                    pb = psbf(bank)
                    for m in range(8):
                        tr(pb[:, m * 128:(m + 1) * 128], wring[:, sA, m, cc * 128:(cc + 1) * 128], ident[:], r=(B_w[sA], B_const), w=(B_ps[bank],))
                    cp("dve" if cc % 2 else "act", wC[:, cc, :], pb[:, :], r=(B_ps[bank],), w=(B_w[sC],))
                for m in range(8):
                    bank = alloc_banks(1)
                    for cc in range(4):
                        mm(ps[:, bank, :], wC[:, cc, m * 128:(m + 1) * 128], wring[:, sB, cc, :], cc == 0, cc == 3,
                           r=(B_w[sC], B_w[sB]), w=(B_ps[bank],))
                    cp("dve" if m % 2 else "act", wring[:, sD, m, :], ps[:, bank, :], r=(B_ps[bank],), w=(B_w[sD],))
                dma("sp", v_in2[:, :, 512 * g:512 * g + 512], wring[:, sD, :, :], r=(B_w[sD],), w=(B_cast[("u", g)],))

        prologue_fuse()

        w_issue_upto(NW)
        issue_rope(steps[0])
        if len(steps) > 1:
            issue_rope(steps[1])
        try_issue_x(1)
        nph = 0
        for pos, (ph, i) in enumerate(sched):
            if limit is not None and nph >= limit:
                break
            nph += 1
            st = steps[i]
            S.tag = ph
            if ph == "AB":
                if st["new"]:
                    need_upto = max(k for k, (ii, _, _) in enumerate(xq) if ii == i)
                    assert xst["next"] > need_upto, ("x load not issued in time", i)
                inj = E2_units(steps[i - 1]) if (i >= 1 and steps[i - 1]["l2bs"]) else None
                phase_AB(st, inj)
                dbg("hT", hT[:], [128, 8, 640], BF16, tuple(B_hT))
                dbg("kT", kT[:], [128, 8, 512], BF16, tuple(B_kT))
                dbg("vv", vv[:], [128, 8, 512], BF16, tuple(B_v))
                dbg("ktok", ktok[:], [128, 4, 512], BF16, (B_ktok,))
            elif ph == "Cpre":
                phase_C(st, "pre")
            elif ph == "C":
                phase_C(st)
                dbg("qtok", qtok[:], [128, 2, 4, 512], BF16, tuple(B_qtok))
                dbg("qT", qT[:], [128, 2, 4, 512], BF16, tuple(B_qT))
                dbg("aT", aT[:], [128, 16, 512], BF16, tuple(B_aT))
            elif ph == "DE":
                nx = steps[i + 1] if i + 1 < len(steps) else None
                if nx is not None and nx["new"]:
                    need_upto = max(k for k, (ii, _, _) in enumerate(xq) if ii == i + 1)
                    if xst["next"] <= need_upto:
                        nx = None
                phase_DE(st, nx)
                if debug:
                    for b in st["qbs"]:
                        sl = xslot(st["si"], b)
                        row = (sinfo[st["si"]]["xo"] + b) * 128
                        dma("sp", dbg_d[row:row + 128, :], xres[:, sl, :], r=(B_x[sl],), w=())
            elif ph == "E2":
                phase_E2(st)
            elif ph == "F":
                phase_F(st)
            elif ph == "H":
                out_proj(st, st["l2bs"], 1, "o2")
                phase_store(st)
            elif ph == "END":
                if i + 2 < len(steps):
                    issue_rope(steps[i + 2])
            release_x(pos)
            try_issue_x(i + 2)
        last = [o for o in S.q["sp"] if o.dma]
        tail = last[-S.NDS["sp"]:]
        fin = Buf("fin")
        for o in tail:
            fin.r.append(o)
        S.op("sp", lambda E: E.nop(), r=(), w=(fin,))
        assert limit is not None or wst["used"] == len(allp) == wst["rel"], (wst, len(allp))

        need = S.emit(nc, None, None, None)
        sem_stack = ExitStack()
        with sem_stack:
            esems = {e: [sem_stack.enter_context(nc.semaphore(f"s_{e}_{k}")) for k in range(need[e])] for e in S.ENGS}
            dsems = {e: [sem_stack.enter_context(nc.semaphore(f"d_{e}_{k}")) for k in range(n)] for e, n in S.NDS.items()}
            with nc.Block() as block:
                @block.tensor
                def _(E):
                    S.run_engine("pe", E, esems, dsems)

                @block.scalar
                def _(E):
                    S.run_engine("act", E, esems, dsems)

                @block.vector
                def _(E):
                    S.run_engine("dve", E, esems, dsems)

                @block.gpsimd
                def _(E):
                    S.run_engine("pool", E, esems, dsems)

                @block.sync
                def _(E):
                    S.run_engine("sp", E, esems, dsems)
    counts = {e: len(S.q[e]) for e in S.ENGS}
    counts["tags"] = {e: [o.tag for o in S.q[e]] for e in S.ENGS}
    return nc, counts


def rope_table(pos):
    half = 16
    inv_freq = (np.float32(ROPE_THETA) ** (-(np.arange(half, dtype=np.float32) * np.float32(2.0) / np.float32(32)))).astype(np.float32)
    ang = (pos.astype(np.float32)[:, None] * inv_freq[None, :]).astype(np.float32)
    c = np.cos(ang).astype(np.float32)
    s = np.sin(ang).astype(np.float32)
    return np.concatenate([c, c, -s, s], axis=1).astype(np.float32)


def inv_counts(S_len, first):
    out = np.zeros((4, 8), np.float32)
    for g, w in enumerate((2, 4, 8, 16)):
        for i in range(8):
            t = i if first else S_len - 8 + i
            lo = max(t - w // 2, 0)
            hi = min(t + w // 2 - 1, S_len - 1)
            out[g, i] = 1.0 / float(hi - lo + 1)
    return out


def tri_masks():
    kl = np.arange(128)[:, None]
    ql = np.arange(128)[None, :]
    NEG = np.float32(-30000.0)
    triL = np.where(ql <= kl, np.float32(0.0), NEG).astype(np.float32)
    triU = np.where(kl <= ql, np.float32(0.0), NEG).astype(np.float32)
    return triL, triU


def core_inputs(seg_specs, params):
    xs, ropes = [], []
    triL, triU = tri_masks()
    m = np.zeros((128, 4, 128), np.float32)
    m[:, 0] = triL
    m[:, 1] = triU
    ic = np.zeros((2, 2, 4, 8), np.float32)
    for kind, xseq, a, b in seg_specs:
        S_len = xseq.shape[0]
        if kind == "halo":
            lo, hi = a - 256, b + 256
            xe = np.zeros((hi - lo, D), np.float32)
            s0, s1 = max(lo, 0), min(hi, S_len)
            xe[s0 - lo:s1 - lo] = xseq[s0:s1]
            pos = np.clip(np.arange(lo, hi), 0, S_len - 1)
            m[:, 2] = triL if a > 0 else -30000.0
            m[:, 3] = triU if b < S_len else -30000.0
            wv = np.array([[1.0 / w] * 8 for w in (2, 4, 8, 16)], np.float32)
            ic[0, 0] = inv_counts(S_len, True) if a == 0 else wv
            ic[0, 1] = inv_counts(S_len, False) if b == S_len else wv
        else:
            xe = xseq[a:b]
            pos = np.arange(a, b)
            ic[1, 0] = inv_counts(S_len, True)
            ic[1, 1] = inv_counts(S_len, False)
        xs.append(xe)
        ropes.append(rope_table(pos))
    d = dict(params)
    d["x"] = np.ascontiguousarray(np.concatenate(xs, 0))
    d["rope"] = np.ascontiguousarray(np.concatenate(ropes, 0))
    d["masks"] = m.astype(ml_dtypes.bfloat16)
    d["icnt"] = np.ascontiguousarray(np.broadcast_to(ic[None], (128, 2, 2, 4, 8))).astype(np.float32)
    return d


def shared_params(norm_pre, norm_post, attn_w_in, attn_sink, attn_w_out, pool_w_in, pool_w_group, pool_scale, pool_w_out):
    p = {}
    p["ident"] = np.eye(128, dtype=np.float32).astype(ml_dtypes.bfloat16)
    p["gpre"] = np.ascontiguousarray(np.asarray(norm_pre, np.float32).reshape(2, 8, 128).transpose(2, 0, 1))
    p["gpost"] = np.ascontiguousarray(np.broadcast_to(np.asarray(norm_post, np.float32)[None], (128, 2, D)))
    p["sinkb"] = np.ascontiguousarray(np.broadcast_to(np.asarray(attn_sink, np.float32).reshape(1, NH), (128, NH)))
    p["pscale"] = np.ascontiguousarray(np.asarray(pool_scale, np.float32).reshape(16, 128).T)
    p["attn_w_in"] = np.ascontiguousarray(np.asarray(attn_w_in, np.float32).reshape(D, ATT_IN))
    p["attn_w_out"] = np.ascontiguousarray(np.asarray(attn_w_out, np.float32).reshape(BW, D))
    p["pool_w_in"] = np.ascontiguousarray(np.asarray(pool_w_in, np.float32).reshape(D, POOL_IN))
    p["pool_w_group"] = np.ascontiguousarray(np.asarray(pool_w_group, np.float32).reshape(BW, 512))
    p["pool_w_out"] = np.ascontiguousarray(np.asarray(pool_w_out, np.float32).reshape(BW, D))
    return p


_PROG_CACHE = {}


def get_program(segs):
    key = tuple(segs)
    if key not in _PROG_CACHE:
        _PROG_CACHE[key] = build_program(list(segs))
    return _PROG_CACHE[key]


def kernel(x_prompt, x_sample, norm_pre, norm_post, attn_w_in, attn_sink, attn_w_out,
           pool_w_in, pool_w_group, pool_scale, pool_w_out):
    x_prompt = np.asarray(x_prompt, np.float32)
    x_sample = np.asarray(x_sample, np.float32)
    n_cores = 8
    PB, PS = x_prompt.shape[0], x_prompt.shape[1]
    SB, SS = x_sample.shape[0], x_sample.shape[1]
    per = n_cores // PB
    q_len = PS // per
    s_per = SB // n_cores
    segs = [("halo", q_len // 128 + 4)] + [("full", SS // 128)] * s_per
    nc, _ = get_program(tuple(segs))
    params = shared_params(norm_pre, norm_post, attn_w_in, attn_sink, attn_w_out, pool_w_in, pool_w_group, pool_scale, pool_w_out)
    in_maps = []
    for c in range(n_cores):
        pb, qi = c // per, c % per
        specs = [("halo", x_prompt[pb], qi * q_len, (qi + 1) * q_len)]
        for k in range(s_per):
            specs.append(("full", x_sample[c * s_per + k], 0, SS))
        in_maps.append(core_inputs(specs, params))
    res = run_bass_kernel_spmd(nc, in_maps, core_ids=list(range(n_cores)))
    y_prompt = np.empty_like(x_prompt)
    y_sample = np.empty_like(x_sample)
    for c in range(n_cores):
        y = res.results[c]["y"]
        pb, qi = c // per, c % per
        y_prompt[pb, qi * q_len:(qi + 1) * q_len] = y[:q_len]
        for k in range(s_per):
            y_sample[c * s_per + k] = y[q_len + k * SS: q_len + (k + 1) * SS]
    return (y_prompt, y_sample)
```

```python
import math
import numpy as np
import ml_dtypes
import concourse.bass as bass
import concourse.mybir as mybir
from concourse.bass_utils import run_bass_kernel_spmd

F32 = mybir.dt.float32
BF16 = mybir.dt.bfloat16
AF = mybir.ActivationFunctionType
ALU = mybir.AluOpType

D = 1024
BW = 2048
NH = 16
NKV = 4
HD = 128
ATT_IN = 5120
POOL_IN = 4096
EPS = 1e-6
ROPE_THETA = 500000.0
QSCALE = 1.0 / math.sqrt(128.0)

RX = 10
NW = 5
EPOCH = 4000
SKIP = set()


class Buf:
    __slots__ = ("name", "w", "r", "psum")

    def __init__(self, name, psum=False):
        self.name = name
        self.w = None
        self.r = []
        self.psum = psum


class Op:
    __slots__ = ("eng", "fn", "deps", "inc", "order", "tick", "dma", "sem_i", "tgt", "prev", "tag")


class Sched:
    ENGS = ("pe", "act", "dve", "pool", "sp")
    NDS = {"sp": 8, "pool": 8, "act": 2}

    def __init__(self):
        self.q = {e: [] for e in self.ENGS}
        self.order = 0
        self.dma_cnt = {e: 0 for e in self.ENGS}
        self.dma_last = {}
        self.tag = ""

    def op(self, eng, fn, r=(), w=(), dma=False):
        o = Op()
        o.tag = self.tag
        o.eng = eng
        o.fn = fn
        o.inc = False
        o.dma = dma
        o.order = self.order
        self.order += 1
        o.prev = None
        if dma:
            n = self.NDS[eng]
            o.sem_i = self.dma_cnt[eng] % n
            self.dma_cnt[eng] += 1
            key = (eng, o.sem_i)
            o.prev = self.dma_last.get(key)
            o.tgt = (o.prev.tgt if o.prev is not None else 0) + 16
            self.dma_last[key] = o
        best = {}

        def add(d):
            if d is None or d is o:
                return
            if d.dma:
                key = ("d", d.eng, d.sem_i)
            else:
                if d.eng == "pe" and eng == "pe" and not dma:
                    return
                key = ("e", d.eng)
            c = best.get(key)
            if c is None or d.order > c.order:
                best[key] = d

        for b in r:
            add(b.w)
            if b.psum:
                for x in b.r:
                    if x.eng != eng:
                        add(x)
        for b in w:
            add(b.w)
            for x in b.r:
                add(x)
        o.deps = list(best.values())
        for d in o.deps:
            d.inc = True
        for b in r:
            b.r.append(o)
        for b in w:
            b.w = o
            b.r = []
        self.q[eng].append(o)
        return o

    def emit(self, nc, engines, esems, dsems):
        for e in self.ENGS:
            c = 0
            for o in self.q[e]:
                if o.inc and not o.dma:
                    c += 1
                    o.tick = c
        need = {e: 1 for e in self.ENGS}
        for e in self.ENGS:
            c = sum(1 for o in self.q[e] if o.inc and not o.dma)
            need[e] = max(1, (c + EPOCH - 1) // EPOCH)
        return need

    def run_engine(self, eng, E, esems, dsems):
        seen = {}

        def wait(key, sem, val):
            if seen.get(key, 0) >= val:
                return
            seen[key] = val
            E.wait_ge(sem, val)

        for o in self.q[eng]:
            if o.dma and o.prev is not None:
                wait(("d", o.eng, o.sem_i), dsems[o.eng][o.sem_i], o.prev.tgt)
            for d in o.deps:
                if d.dma:
                    wait(("d", d.eng, d.sem_i), dsems[d.eng][d.sem_i], d.tgt)
                else:
                    ep = (d.tick - 1) // EPOCH
                    wait(("e", d.eng, ep), esems[d.eng][ep], (d.tick - 1) % EPOCH + 1)
            ins = o.fn(E)
            if o.dma:
                ins.then_inc(dsems[o.eng][o.sem_i], 16)
            elif o.inc:
                ep = (o.tick - 1) // EPOCH
                ins.then_inc(esems[eng][ep], 1)


def seg_info(segs):
    out = []
    xo = 0
    yo = 0
    for kind, n_ext in segs:
        n_own = n_ext - 4 if kind == "halo" else n_ext
        out.append(dict(kind=kind, n_ext=n_ext, n_own=n_own, xo=xo, yo=yo))
        xo += n_ext
        yo += n_own
    return out, xo, yo


def build_program(segs, debug=False, limit=None):
    S = Sched()
    sinfo, NEXT, NOWN = seg_info(segs)
    nc = bass.Bass("TRN2", target_bir_lowering=False)

    def din(name, shape, dt):
        return nc.dram_tensor(name, list(shape), dt, kind="ExternalInput").ap()

    x_d = din("x", [NEXT * 128, D], F32)
    rope_d = din("rope", [NEXT * 128, 64], F32)
    masks_d = din("masks", [128, 4, 128], BF16)
    ident_d = din("ident", [128, 128], BF16)
    gpre_d = din("gpre", [128, 2, 8], F32)
    gpost_d = din("gpost", [128, 2, D], F32)
    sink_d = din("sinkb", [128, NH], F32)
    pscale_d = din("pscale", [128, 4, 512], F32)
    icnt_d = din("icnt", [128, 2, 2, 4, 8], F32)
    w_in1 = din("attn_w_in", [D, ATT_IN], F32)
    w_out1 = din("attn_w_out", [BW, D], F32)
    w_in2 = din("pool_w_in", [D, POOL_IN], F32)
    w_grp = din("pool_w_group", [BW, 512], F32)
    w_out2 = din("pool_w_out", [BW, D], F32)
    y_d = nc.dram_tensor("y", [NOWN * 128, D], F32, kind="ExternalOutput").ap()
    dbg_d = nc.dram_tensor("dbg_x1", [NEXT * 128, D], F32, kind="ExternalOutput").ap() if debug else None

    def dscr(name, shape):
        return nc.dram_tensor(name, list(shape), BF16, kind="Internal").ap()

    b_in1 = dscr("b_in1", [D, ATT_IN])
    b_out1 = dscr("b_out1", [BW, D])
    b_in2 = dscr("b_in2", [D, POOL_IN])
    b_grp = dscr("b_grp", [BW, 512])
    b_out2 = dscr("b_out2", [BW, D])

    from contextlib import ExitStack

    es = ExitStack()

    def sb(name, shape, dt):
        return es.enter_context(nc.sbuf_tensor(name, list(shape), dt))

    with es:
        xres = sb("xres", [128, RX, D], F32)
        hb = sb("hb", [128, 4, D], BF16)
        hT = sb("hT", [128, 8, 640], BF16)
        kT = sb("kT", [128, 8, 512], BF16)
        vv = sb("vv", [128, 8, 512], BF16)
        qtok = sb("qtok", [128, 2, 4, 512], BF16)
        ktok = qtok[:, 1]
        qT = sb("qT", [128, 2, 4, 512], BF16)
        pT = sb("pT", [128, 2, 3, 512], BF16)
        rden = sb("rden", [128, 2, 512], F32)
        otmp = sb("otmp", [128, 2, 512], F32)
        aT = sb("aT", [128, 16, 512], BF16)
        h2T = sb("h2T", [128, 8, 528], BF16)
        h2halo = sb("h2halo", [128, 8, 8], BF16)
        uT = sb("uT", [128, 4, 528], F32)
        ptmp = sb("ptmp", [128, 4, 528], F32)
        pfix = sb("pfix", [128, 8], F32)
        pmix = sb("pmix", [128, 2, 512], F32)
        wring = sb("wring", [128, NW, 8, 512], BF16)
        ropet = sb("ropet", [128, 2, 5, 64], F32)
        rtmp = sb("rtmp", [128, 2, 4, 128], F32)
        stat = sb("stat", [128, 16, 4], F32)
        ident = sb("ident_s", [128, 128], BF16)
        ones = sb("ones_s", [128, 128], BF16)
        masks = sb("masks_s", [128, 4, 128], BF16)
        gpre = sb("gpre_s", [128, 2, 8], F32)
        gpost = sb("gpost_s", [128, 2, D], F32)
        esink = sb("esink_s", [128, NH], F32)
        esrow = sb("esrow_s", [1, NH], BF16)
        icnt = sb("icnt_s", [128, 2, 2, 4, 8], F32)
        epsb = sb("epsb", [128, 1], F32)
        ps = es.enter_context(nc.psum_tensor("ps", [128, 8, 512], F32))

        B_x = [Buf(f"x{i}") for i in range(RX)]
        B_hb = [Buf(f"hb{i}") for i in range(4)]
        B_hT = [Buf(f"hT{i}") for i in range(5)]
        B_kT = [Buf(f"kT{i}") for i in range(8)]
        B_v = [Buf(f"v{i}") for i in range(8)]
        B_qtok = [Buf(f"qtok{i}") for i in range(2)]
        B_ktok = B_qtok[1]
        B_qT = [Buf(f"qT{i}") for i in range(2)]
        B_pT = [Buf(f"pT{i}") for i in range(2)]
        B_rden = [Buf(f"rden{i}") for i in range(2)]
        B_otmp = [Buf(f"otmp{i}") for i in range(2)]
        B_aT = [Buf(f"aT{i}") for i in range(4)]
        B_h2T = Buf("h2T")
        B_h2halo = Buf("h2halo")
        B_uT = [Buf(f"uT{i}") for i in range(4)]
        B_ptmp = [Buf("pa"), Buf("pb"), Buf("pa2"), Buf("pb2")]
        B_pfix = Buf("pfix")
        B_pmix = [Buf(f"pmix{i}") for i in range(2)]
        B_w = [Buf(f"w{i}") for i in range(NW)]
        B_rope = [Buf(f"rope{i}") for i in range(2)]
        B_rtmp = [Buf("rta"), Buf("rtb")]
        B_stat = [Buf(f"stat{i}") for i in range(16)]
        B_ps = [Buf(f"ps{i}", psum=True) for i in range(8)]
        B_const = Buf("const")

        state = dict(bank=0, stat=0, hb=0)
        hb_out = set()

        def alloc_banks(n):
            b = state["bank"]
            if b + n > 8:
                b = 0
            state["bank"] = (b + n) % 8
            return b

        def alloc_stat():
            s = state["stat"]
            state["stat"] = (s + 1) % 16
            return s

        def dma(eng, out, in_, r, w):
            return S.op(eng, lambda E, out=out, in_=in_: E.dma_start(out=out, in_=in_), r=r, w=w, dma=True)

        def mm(out, lhsT, rhs, start, stop, r, w):
            return S.op(
                "pe",
                lambda E, out=out, lhsT=lhsT, rhs=rhs, start=start, stop=stop: E.matmul(
                    out, lhsT=lhsT, rhs=rhs, start=start, stop=stop
                ),
                r=r,
                w=w,
            )

        def tr(out, in_, idn, r, w):
            return S.op(
                "pe",
                lambda E, out=out, in_=in_, idn=idn: E.transpose(out=out, in_=in_, identity=idn),
                r=r,
                w=w,
            )

        def act(out, in_, func, r, w, scale=None, bias=None, accum=None):
            def fn(E, out=out, in_=in_, func=func, scale=scale, bias=bias, accum=accum):
                kw = {}
                if scale is not None:
                    kw["scale"] = scale
                if bias is not None:
                    kw["bias"] = bias
                if accum is not None:
                    kw["accum_out"] = accum
                return E.activation(out=out, in_=in_, func=func, **kw)

            return S.op("act", fn, r=r, w=w)

        def tt(eng, out, in0, in1, op, r, w):
            return S.op(
                eng,
                lambda E, out=out, in0=in0, in1=in1, op=op: E.tensor_tensor(out=out, in0=in0, in1=in1, op=op),
                r=r,
                w=w,
            )

        def ts(eng, out, in0, s1, op0, r, w, s2=None, op1=None):
            def fn(E, out=out, in0=in0, s1=s1, op0=op0, s2=s2, op1=op1):
                if op1 is None:
                    return E.tensor_scalar(out=out, in0=in0, scalar1=s1, scalar2=None, op0=op0)
                return E.tensor_scalar(out=out, in0=in0, scalar1=s1, scalar2=s2, op0=op0, op1=op1)

            return S.op(eng, fn, r=r, w=w)

        def stt(out, in0, scalar, in1, op0, op1, r, w):
            return S.op(
                "dve",
                lambda E, out=out, in0=in0, scalar=scalar, in1=in1, op0=op0, op1=op1: E.scalar_tensor_tensor(
                    out=out, in0=in0, scalar=scalar, in1=in1, op0=op0, op1=op1
                ),
                r=r,
                w=w,
            )

        def cp(eng, out, in_, r, w):
            if eng == "act":
                return act(out, in_, AF.Copy, r, w)
            return S.op(eng, lambda E, out=out, in_=in_: E.tensor_copy(out=out, in_=in_), r=r, w=w)

        def recip(out, in_, r, w):
            return S.op("dve", lambda E, out=out, in_=in_: E.reciprocal(out=out, in_=in_), r=r, w=w)

        def memset(eng, ap, val, w):
            return S.op(eng, lambda E, ap=ap, val=val: E.memset(ap, val), r=(), w=w)

        def psbf(bank):
            return ps[:, bank, :].bitcast(BF16)

        for dst, src in (
            (ident[:], ident_d[:, :]),
            (masks[:], masks_d[:, :, :]),
            (gpre[:], gpre_d[:, :, :]),
            (gpost[:], gpost_d[:, :, :]),
            (esink[:], sink_d[:, :]),
            (icnt[:], icnt_d[:, :, :, :, :]),
        ):
            dma("sp", dst, src, r=(), w=(B_const,))
        memset("dve", ones[:], 2.0, w=(B_const,))
        memset("dve", epsb[:], EPS, w=(B_const,))
        act(esink[:], esink[:], AF.Exp, r=(B_const,), w=(B_const,))
        cp("dve", esrow[:], esink[0:1, :], r=(B_const,), w=(B_const,))
        if debug:
            for t_, bl in ((hT, B_hT), (kT, B_kT), (vv, B_v), (qtok, B_qtok), (qT, B_qT), (aT, B_aT), (ktok, [B_ktok])):
                memset("pool", t_[:], 0.0, w=tuple(bl))
        v_in1 = b_in1.rearrange("(kc p) n -> p kc n", p=128)
        v_out1 = b_out1.rearrange("(kc p) n -> p kc n", p=128)
        v_in2 = b_in2.rearrange("(kc p) n -> p kc n", p=128)
        v_grp = b_grp.rearrange("(kc p) n -> p kc n", p=128)
        v_out2 = b_out2.rearrange("(kc p) n -> p kc n", p=128)

        def piece_src(key):
            k = key[0]
            if k == "k":
                return v_in1[:, :, 2048:2560], 8, (b_in1, w_in1, 0, D, 2048, 2560)
            if k == "v":
                return v_in1[:, :, 2560:3072], 8, (b_in1, w_in1, 0, D, 2560, 3072)
            if k == "q":
                j = key[1]
                return v_in1[:, :, 512 * j:512 * j + 512], 8, (b_in1, w_in1, 0, D, 512 * j, 512 * j + 512)
            if k == "g":
                j = key[1]
                c0 = 3072 + 512 * j
                return v_in1[:, :, c0:c0 + 512], 8, (b_in1, w_in1, 0, D, c0, c0 + 512)
            if k == "o1":
                h, kk = key[1], key[2]
                return v_out1[:, 8 * kk:8 * kk + 8, 512 * h:512 * h + 512], 8, (b_out1, w_out1, 1024 * kk, 1024 * kk + 1024, 512 * h, 512 * h + 512)
            if k == "u":
                g = key[1]
                return v_in2[:, :, 512 * g:512 * g + 512], 8, (b_in2, w_in2, 0, D, 512 * g, 512 * g + 512)
            if k == "g2":
                g = key[1]
                c0 = 2048 + 512 * g
                return v_in2[:, :, c0:c0 + 512], 8, (b_in2, w_in2, 0, D, c0, c0 + 512)
            if k == "grp":
                g = key[1]
                return v_grp[:, 4 * g:4 * g + 4, :], 4, (b_grp, w_grp, 512 * g, 512 * g + 512, 0, 512)
            if k == "o2":
                h, kk = key[1], key[2]
                return v_out2[:, 8 * kk:8 * kk + 8, 512 * h:512 * h + 512], 8, (b_out2, w_out2, 1024 * kk, 1024 * kk + 1024, 512 * h, 512 * h + 512)
            raise KeyError(key)

        B_cast = {}

        def cast_issue(key):
            if key in B_cast:
                return
            _, _, (dst, src, r0, r1, c0, c1) = piece_src(key)
            B_cast[key] = Buf("cast")
            dma("pool", dst[r0:r1, c0:c1], src[r0:r1, c0:c1], r=(), w=(B_cast[key],))

        steps = []
        for si, sg in enumerate(sinfo):
            kind, n_ext = sg["kind"], sg["n_ext"]
            nst = n_ext // 4
            l2next = 0
            needs_q = (lambda b, n=n_ext, k=kind: (1 <= b <= n - 2) if k == "halo" else (0 <= b <= n - 1))
            needs_l2 = (lambda b, n=n_ext, k=kind: (2 <= b <= n - 3) if k == "halo" else (0 <= b <= n - 1))
            glist = list(range(nst)) + (["flush"] if kind == "full" else [])
            for G in glist:
                if G == "flush":
                    new = []
                    base = n_ext - 1
                    qbs = [n_ext - 1]
                    lim = n_ext - 1
                else:
                    new = list(range(4 * G, 4 * G + 4))
                    base = 4 * G - 1
                    qbs = [b for b in range(4 * G - 1, 4 * G + 3) if b >= 0 and needs_q(b)]
                    lim = 4 * G + 1
                l2bs = [b for b in range(l2next, min(lim, n_ext - 1) + 1) if needs_l2(b)]
                l2next = max(l2next, lim + 1)
                steps.append(dict(si=si, G=G, new=new, base=base, qbs=qbs, l2bs=l2bs))
        for si, sg in enumerate(sinfo):
            mine = [s for s in steps if s["si"] == si and s["l2bs"]]
            for s in steps:
                if s["si"] == si:
                    s["l2first"] = False
                    s["l2last"] = False
            mine[0]["l2first"] = True
            mine[-1]["l2last"] = True

        def phase_pieces(ph, st):
            if ph == "AB":
                return [("k",), ("v",)] if st["new"] else []
            if ph == "Cpre":
                return [("q", 0), ("q", 1)]
            if ph == "C":
                return [("g", 0), ("g", 1), ("q", 2), ("g", 2), ("q", 3), ("g", 3)]
            if ph == "DE":
                return [("o1", 0, 0), ("o1", 0, 1), ("o1", 1, 0), ("o1", 1, 1)]
            if ph == "F":
                return [("g2", 3), ("u", 3), ("g2", 0), ("u", 0), ("g2", 2), ("u", 2), ("g2", 1), ("u", 1)]
            if ph == "H":
                return [("o2", 0, 0), ("o2", 0, 1), ("o2", 1, 0), ("o2", 1, 1)]
            return []

        sched = []
        hasA = lambda st: bool(st["new"]) or st["G"] == "flush"
        if hasA(steps[0]):
            sched.append(("AB", 0))
        sched.append(("Cpre", 0))
        for i, st in enumerate(steps):
            nxt = steps[i + 1] if i + 1 < len(steps) else None
            if st["qbs"]:
                sched.append(("C", i))
                sched.append(("DE", i))
            if nxt is not None and hasA(nxt):
                sched.append(("AB", i + 1))
            elif st["l2bs"]:
                sched.append(("E2", i))
            if st["l2bs"]:
                sched.append(("F", i))
            if nxt is not None:
                sched.append(("Cpre", i + 1))
            if st["l2bs"]:
                sched.append(("H", i))
            sched.append(("END", i))

        allp = []
        for ph, i in sched:
            allp += phase_pieces(ph, steps[i])
        wst = dict(issued=0, used=0, rel=0)

        CAST_AHEAD = 6

        def w_issue_upto(n):
            while wst["issued"] < min(n, len(allp)):
                i = wst["issued"]
                for k2 in allp[i:i + CAST_AHEAD]:
                    cast_issue(k2)
                src, nk, _ = piece_src(allp[i])
                slot = i % NW
                dma("sp", wring[:, slot, 0:nk, :], src, r=(B_cast[allp[i]],), w=(B_w[slot],))
                wst["issued"] += 1

        def wget(key):
            i = wst["used"]
            assert allp[i] == key, (allp[i], key)
            assert i < wst["issued"], "weight piece not issued"
            wst["used"] += 1
            return i % NW

        def wrel(n=1):
            wst["rel"] += n
            assert wst["rel"] <= wst["used"]
            w_issue_upto(wst["rel"] + NW)

        def xslot(si, b):
            return (sinfo[si]["xo"] + b) % RX

        def issue_rope(st):
            si = st["si"]
            xo = sinfo[si]["xo"]
            rb = st["ri"]
            lo = max(st["base"], 0)
            hi = st["base"] + 4 if st["new"] else st["base"]
            i0 = lo - st["base"]
            cnt = hi - lo + 1
            src = rope_d[(xo + lo) * 128:(xo + hi + 1) * 128, :].rearrange("(b p) c -> p b c", p=128)
            dma("sp", ropet[:, rb, i0:i0 + cnt, :], src, r=(), w=(B_rope[rb],))

        def right_halo_block(st):
            sg = sinfo[st["si"]]
            br = st["l2bs"][-1] + 1
            ok = (br <= sg["n_ext"] - 1) and ((1 <= br <= sg["n_ext"] - 2) if sg["kind"] == "halo" else True)
            return br if ok else None

        last_use = {}
        for pos, (ph, i) in enumerate(sched):
            st = steps[i]
            used = set()
            if ph == "AB":
                used |= set(st["new"])
                if i >= 1 and steps[i - 1]["l2bs"]:
                    pst = steps[i - 1]
                    for b in list(pst["l2bs"]) + ([right_halo_block(pst)] if right_halo_block(pst) is not None else []):
                        last_use[(pst["si"], b)] = pos
            elif ph == "DE":
                used |= set(st["qbs"])
                if st["l2bs"] and st["l2first"] and sinfo[st["si"]]["kind"] == "halo":
                    used.add(st["l2bs"][0] - 1)
            elif ph == "E2":
                used |= set(st["l2bs"])
                br = right_halo_block(st)
                if br is not None:
                    used.add(br)
            elif ph == "H":
                used |= set(st["l2bs"])
            for b in used:
                last_use[(st["si"], b)] = pos
        xq = [(i, st["si"], b) for i, st in enumerate(steps) for b in st["new"]]
        xst = dict(next=0)
        slot_free = [True] * RX

        def try_issue_x(max_step):
            while xst["next"] < len(xq):
                i, si, b = xq[xst["next"]]
                if i > max_step:
                    break
                sl = xslot(si, b)
                if not slot_free[sl]:
                    break
                slot_free[sl] = False
                xo = sinfo[si]["xo"]
                dma("sp", xres[:, sl, :], x_d[(xo + b) * 128:(xo + b + 1) * 128, :], r=(), w=(B_x[sl],))
                xst["next"] += 1

        def release_x(pos):
            for (si, b), lu in last_use.items():
                if lu == pos:
                    slot_free[xslot(si, b)] = True

        for i, st in enumerate(steps):
            st["ri"] = i % 2

        def rmsnorm_to_hb(xslot_i, layer):
            hs = state["hb"]
            for _ in range(4):
                if hs not in hb_out:
                    break
                hs = (hs + 1) % 4
            assert hs not in hb_out, "hb staging ring exhausted"
            hb_out.add(hs)
            state["hb"] = (hs + 1) % 4
            s0 = alloc_stat()
            act(hb[:, hs, :], xres[:, xslot_i, :], AF.Square, r=(B_x[xslot_i],), w=(B_hb[hs], B_stat[s0]),
                accum=stat[:, s0, 0:1])
            act(stat[:, s0, 1:2], stat[:, s0, 0:1], AF.Sqrt, r=(B_const,), w=(B_stat[s0],), scale=1.0 / D, bias=epsb[:, 0:1])
            recip(stat[:, s0, 2:3], stat[:, s0, 1:2], r=(), w=(B_stat[s0],))
            ts("dve", hb[:, hs, :], xres[:, xslot_i, :], stat[:, s0, 2:3], ALU.mult, r=(B_x[xslot_i], B_stat[s0]), w=(B_hb[hs],))
            return hs

        def rope_evac(b0, n, dst4, dst_bufs, rb, ri0, extra_r=()):
            psv = ps[:, b0:b0 + n, :].rearrange("p n (h d) -> p n h d", h=4)
            pbufs = tuple(B_ps[b0 + i] for i in range(n))
            tab = ropet[:, rb, ri0:ri0 + n, :]
            cc = tab[:, :, 0:32].unsqueeze(2).to_broadcast([128, n, 4, 32])
            nsn = tab[:, :, 32:48].unsqueeze(2).to_broadcast([128, n, 4, 16])
            psn = tab[:, :, 48:64].unsqueeze(2).to_broadcast([128, n, 4, 16])
            ta = rtmp[:, 0, 0:n, :].rearrange("p n (h d) -> p n h d", h=4)
            tb = rtmp[:, 1, 0:n, :].rearrange("p n (h d) -> p n h d", h=4)
            if "rope_act" not in SKIP:
                cp("act", dst4[:, :, :, 32:128], psv[:, :, :, 32:128], r=pbufs, w=dst_bufs)
            if "rope_dve" not in SKIP:
                tt("dve", ta, psv[:, :, :, 0:32], cc, ALU.mult, r=pbufs + (B_rope[rb],), w=(B_rtmp[0],))
                tt("dve", tb[:, :, :, 0:16], psv[:, :, :, 16:32], nsn, ALU.mult, r=pbufs + (B_rope[rb],), w=(B_rtmp[1],))
                tt("dve", tb[:, :, :, 16:32], psv[:, :, :, 0:16], psn, ALU.mult, r=pbufs + (B_rope[rb],), w=(B_rtmp[1],))
            if "rope_pool" not in SKIP:
                tt("pool", dst4[:, :, :, 0:32], ta, tb, ALU.add, r=(B_rtmp[0], B_rtmp[1]), w=dst_bufs)

        def kslot(b):
            return b % 8

        pending_k = []

        def flush_k():
            while pending_k:
                pending_k.pop(0)()

        def phase_AB(st, inject=None):
            si = st["si"]
            new = st["new"]
            n = len(new)
            rb = st["ri"]
            if st["G"] != 0:
                cp("pool", hT[:, :, 0:128], hT[:, :, 512:640], r=(B_hT[4],), w=(B_hT[0],))
            if not new:
                for ch, pe in (inject or []):
                    ch()
                    pe()
                return
            wk = wget(("k",))
            wv = wget(("v",))
            kb0 = alloc_banks(n)
            inject = list(inject) if inject else []
            hsl = {}

            def chain(i):
                S.tag = "A"
                hsl[i] = rmsnorm_to_hb(xslot(si, new[i]), 0)

            def trans(i):
                S.tag = "A"
                cg = new[i] - st["base"]
                hs = hsl[i]
                bank = alloc_banks(1)
                if kb0 <= bank < kb0 + n:
                    state["bank"] = (kb0 + n) % 8
                    bank = alloc_banks(1)
                pb = psbf(bank)
                for kc in range(8):
                    tr(pb[:, kc * 128:(kc + 1) * 128], hb[:, hs, kc * 128:(kc + 1) * 128], ident[:], r=(B_hb[hs], B_const), w=(B_ps[bank],))
                hb_out.discard(hs)
                tt("dve", hT[:, :, cg * 128:(cg + 1) * 128], pb.rearrange("p (c t) -> p c t", c=8),
                   gpre[:, 0, :].unsqueeze(2).to_broadcast([128, 8, 128]), ALU.mult, r=(B_ps[bank], B_const), w=(B_hT[cg],))

            pre = st.get("pre_chains")
            if pre is not None:
                hsl.update(pre)
            else:
                chain(0)
                if n > 1:
                    chain(1)
            for i in range(n):
                trans(i)
                if i + 2 < n:
                    chain(i + 2)
                if i >= 1:
                    if inject and i + 2 >= n:
                        ch, pe = inject.pop(0)
                        ch()
                        kv_block(st, i - 1, kb0, n, wk, wv)
                        pe()
                    else:
                        kv_block(st, i - 1, kb0, n, wk, wv)
            if inject:
                ch, pe = inject.pop(0)
                ch()
                kv_block(st, n - 1, kb0, n, wk, wv)
                pe()
            else:
                kv_block(st, n - 1, kb0, n, wk, wv)
            for ch, pe in inject:
                ch()
                pe()
            wrel(2)
            S.tag = "B"
            rope_evac(kb0, n, ktok[:, 0:n, :].rearrange("p n (h d) -> p n h d", h=4), (B_ktok,), rb, new[0] - st["base"])
            pending_k.append(lambda: k_transposes(st))

        def k_transposes(st):
            new = st["new"]
            n = len(new)
            S.tag = "B"
            for i0 in range(0, n, 2):
                bank = alloc_banks(1)
                pb = psbf(bank)
                m = min(2, n - i0)
                for ii in range(m):
                    for h in range(4):
                        tr(pb[:, ii * 512 + h * 128: ii * 512 + (h + 1) * 128], ktok[:, i0 + ii, h * 128:(h + 1) * 128], ident[:],
                           r=(B_ktok, B_const), w=(B_ps[bank],))
                for ii in range(m):
                    sl = kslot(new[i0 + ii])
                    cp("dve", kT[:, sl, :], pb[:, ii * 512:(ii + 1) * 512], r=(B_ps[bank],), w=(B_kT[sl],))

        def kv_block(st, i, kb0, n, wk, wv):
            S.tag = "B"
            b = st["new"][i]
            cg = b - st["base"]
            for kc in range(8):
                mm(ps[:, kb0 + i, :], hT[:, kc, cg * 128:(cg + 1) * 128], wring[:, wk, kc, :], kc == 0, kc == 7,
                   r=(B_hT[cg], B_w[wk]), w=(B_ps[kb0 + i],))
            bank = alloc_banks(1)
            if kb0 <= bank < kb0 + n:
                state["bank"] = (kb0 + n) % 8
                bank = alloc_banks(1)
            for kc in range(8):
                mm(ps[:, bank, :], hT[:, kc, cg * 128:(cg + 1) * 128], wring[:, wv, kc, :], kc == 0, kc == 7,
                   r=(B_hT[cg], B_w[wv]), w=(B_ps[bank],))
            sl = kslot(b)
            cp("act", vv[:, sl, :], ps[:, bank, :], r=(B_ps[bank],), w=(B_v[sl],))

        def phase_C(st, part="main"):
            si = st["si"]
            sg = sinfo[si]
            kind, n_ext = sg["kind"], sg["n_ext"]
            qbs = st["qbs"]
            n = len(qbs)
            rb = st["ri"]
            cg0 = qbs[0] - st["base"]

            held = {}

            def Qpart(j, i):
                jb = j % 2
                S.tag = "C-q"
                if ("q", j) not in held:
                    held[("q", j)] = wget(("q", j))
                ws = held[("q", j)]
                b0 = alloc_banks(1)
                cg = qbs[i] - st["base"]
                for kc in range(8):
                    mm(ps[:, b0, :], hT[:, kc, cg * 128:(cg + 1) * 128], wring[:, ws, kc, :], kc == 0, kc == 7,
                       r=(B_hT[cg], B_w[ws]), w=(B_ps[b0],))
                rope_evac(b0, 1, qtok[:, jb, i:i + 1, :].rearrange("p n (h d) -> p n h d", h=4), (B_qtok[jb],), rb, cg)

            def Gpart(j, m):
                S.tag = "C-g"
                if ("g", j) not in held:
                    held[("g", j)] = wget(("g", j))
                ws = held[("g", j)]
                bank = alloc_banks(1)
                for kc in range(8):
                    mm(ps[:, bank, 0:n * 128], wring[:, ws, kc, m * 128:(m + 1) * 128], hT[:, kc, cg0 * 128:(cg0 + n) * 128],
                       kc == 0, kc == 7, r=tuple(B_hT[cg0 + i] for i in range(n)) + (B_w[ws],), w=(B_ps[bank],))
                gdst = aT[:, 4 * j + m, 0:n * 128]
                act(gdst, ps[:, bank, 0:n * 128], AF.Tanh, r=(B_ps[bank],), w=tuple(B_aT[i] for i in range(n)), scale=0.5)
                stt(gdst, gdst, 1.0, ps[:, bank, 0:n * 128], ALU.add, ALU.mult, r=(B_ps[bank],), w=tuple(B_aT[i] for i in range(n)))

            def Qp(j):
                jb = j % 2
                S.tag = "C-q"
                ws = wget(("q", j))
                b0 = alloc_banks(n)
                for i, b in enumerate(qbs):
                    cg = b - st["base"]
                    for kc in range(8):
                        mm(ps[:, b0 + i, :], hT[:, kc, cg * 128:(cg + 1) * 128], wring[:, ws, kc, :], kc == 0, kc == 7,
                           r=(B_hT[cg], B_w[ws]), w=(B_ps[b0 + i],))
                wrel()
                rope_evac(b0, n, qtok[:, jb, 0:n, :].rearrange("p n (h d) -> p n h d", h=4), (B_qtok[jb],), rb, cg0)

            def Gp(j):
                for m in range(4):
                    Gpart(j, m)
                wrel()

            def Tq(j):
                jb = j % 2
                S.tag = "C-T"
                for i0 in range(0, n, 2):
                    bank = alloc_banks(1)
                    pb = psbf(bank)
                    m2 = min(2, n - i0)
                    for ii in range(m2):
                        for h in range(4):
                            tr(pb[:, ii * 512 + h * 128: ii * 512 + (h + 1) * 128], qtok[:, jb, i0 + ii, h * 128:(h + 1) * 128], ident[:],
                               r=(B_qtok[jb], B_const), w=(B_ps[bank],))
                    cp("dve", qT[:, jb, i0:i0 + m2, :], pb[:, 0:m2 * 512].rearrange("p (n c) -> p n c", n=m2), r=(B_ps[bank],), w=(B_qT[jb],))

            def key_blocks(b):
                kbs = []
                if b - 1 >= 0:
                    kbs.append((b - 1, 2 if (kind == "halo" and b == 2) else 0))
                kbs.append((b, None))
                if b + 1 <= n_ext - 1:
                    kbs.append((b + 1, 3 if (kind == "halo" and b == n_ext - 3) else 1))
                return kbs

            def Sst(j, i):
                S.tag = "C-S"
                jb = j % 2
                pbi = (j * n + i) % 2
                kbs = key_blocks(qbs[i])
                nk = len(kbs)
                b0 = alloc_banks(nk)
                for kidx, (kb, mi) in enumerate(kbs):
                    bank = b0 + kidx
                    sl = kslot(kb)
                    mm(ps[:, bank, :], kT[:, sl, j * 128:(j + 1) * 128], qT[:, jb, i, :], True, mi is None,
                       r=(B_kT[sl], B_qT[jb]), w=(B_ps[bank],))
                    if mi is not None:
                        mm(ps[:, bank, :], ident[:], masks[:, mi, :].unsqueeze(1).to_broadcast([128, 4, 128]), False, True, r=(B_const,), w=(B_ps[bank],))
                act(pT[:, pbi, 0:nk, :], ps[:, b0:b0 + nk, :], AF.Exp, r=tuple(B_ps[b0 + k] for k in range(nk)), w=(B_pT[pbi],), scale=QSCALE)

            def PVs(j, i):
                S.tag = "C-PV"
                pbi = (j * n + i) % 2
                kbs = key_blocks(qbs[i])
                nk = len(kbs)
                bd = alloc_banks(1)
                for kidx in range(nk):
                    mm(ps[:, bd, :], ones[:], pT[:, pbi, kidx, :], kidx == 0, False,
                       r=(B_const, B_pT[pbi]), w=(B_ps[bd],))
                mm(ps[:, bd, :], ones[0:1, :], esrow[0:1, 4 * j:4 * j + 4].unsqueeze(2).to_broadcast([1, 4, 128]), False, True, r=(B_const,), w=(B_ps[bd],))
                bo = alloc_banks(1)
                for kidx, (kb, mi) in enumerate(kbs):
                    sl = kslot(kb)
                    mm(ps[:, bo, :], vv[:, sl, j * 128:(j + 1) * 128], pT[:, pbi, kidx, :], kidx == 0, kidx == nk - 1,
                       r=(B_v[sl], B_pT[pbi]), w=(B_ps[bo],))
                rd = rden[:, pbi, :]
                rd3 = rd.rearrange("p (h t) -> p h t", h=4)
                ot3 = otmp[:, pbi, :].rearrange("p (h t) -> p h t", h=4)
                av = aT[:, 4 * j:4 * j + 4, i * 128:(i + 1) * 128]
                tt("dve", ot3, ps[:, bo, :].rearrange("p (h t) -> p h t", h=4), av, ALU.mult, r=(B_ps[bo], B_aT[i]), w=(B_otmp[pbi],))
                recip(rd, ps[:, bd, :], r=(B_ps[bd],), w=(B_rden[pbi],))
                tt("pool", av, ot3, rd3, ALU.mult, r=(B_otmp[pbi], B_rden[pbi]), w=(B_aT[i],))

            if part == "pre":
                Qp(0)
                Qp(1)
                Tq(0)
                return
            Gp(0)
            items = [(j, i) for j in range(4) for i in range(n)]
            for idx in range(len(items) + 1):
                if idx < len(items):
                    j, i = items[idx]
                    Sst(j, i)
                    if i == min(1, n - 1) and j + 1 < 4:
                        Gp(j + 1)
                        Tq(j + 1)
                    if i == n - 1 and j + 2 < 4:
                        Qp(j + 2)
                if idx >= 1:
                    PVs(*items[idx - 1])

        def out_proj(st, blocks, layer, okey, after_block=None):
            si = st["si"]
            slots = {}
            for h in range(2):
                for kk in range(2):
                    slots[(h, kk)] = wget((okey, h, kk))
            for i, b in enumerate(blocks):
                S.tag = "D" if layer == 0 else "H"
                sl = xslot(si, b)
                b0 = alloc_banks(2)
                for h in range(2):
                    for c in range(16):
                        ws = slots[(h, c // 8)]
                        mm(ps[:, b0 + h, :], aT[:, c, i * 128:(i + 1) * 128], wring[:, ws, c % 8, :], c == 0, c == 15,
                           r=(B_aT[i], B_w[ws]), w=(B_ps[b0 + h],))
                if after_block is not None:
                    after_block(i)
                pv = ps[:, b0:b0 + 2, :].rearrange("p a b -> p (a b)")
                pbufs = (B_ps[b0], B_ps[b0 + 1])
                s0 = alloc_stat()
                act(otmp[:, 0, :].bitcast(BF16), pv, AF.Square, r=pbufs, w=(B_otmp[0], B_stat[s0]), accum=stat[:, s0, 0:1])
                act(stat[:, s0, 1:2], stat[:, s0, 0:1], AF.Sqrt, r=(B_const,), w=(B_stat[s0],), scale=1.0 / D, bias=epsb[:, 0:1])
                recip(stat[:, s0, 2:3], stat[:, s0, 1:2], r=(), w=(B_stat[s0],))
                tt("dve", pv, pv, gpost[:, layer, :], ALU.mult, r=(B_const,), w=pbufs)
                stt(xres[:, sl, :], pv, stat[:, s0, 2:3], xres[:, sl, :], ALU.mult, ALU.add, r=pbufs + (B_stat[s0],), w=(B_x[sl],))
            wrel(4)

        def E_pe(hs, col0, ncols, first_tok=None):
            g1 = gpre[:, 1, :]
            bank = alloc_banks(1)
            pb = psbf(bank)
            if first_tok == "head8":
                for kc in range(8):
                    tr(pb[:, kc * 8:(kc + 1) * 8], hb[0:8, hs, kc * 128:(kc + 1) * 128], ident[0:8, 0:8], r=(B_hb[hs], B_const), w=(B_ps[bank],))
                src = pb[:, 0:64].rearrange("p (c t) -> p c t", c=8)
            else:
                for kc in range(8):
                    tr(pb[:, kc * 128:(kc + 1) * 128], hb[:, hs, kc * 128:(kc + 1) * 128], ident[:], r=(B_hb[hs], B_const), w=(B_ps[bank],))
                src = pb.rearrange("p (c t) -> p c t", c=8)
                if first_tok == "tail8":
                    src = src[:, :, 120:128]
            hb_out.discard(hs)
            tt("dve", h2T[:, :, col0:col0 + ncols], src, g1.unsqueeze(2).to_broadcast([128, 8, ncols]), ALU.mult,
               r=(B_ps[bank], B_const), w=(B_h2T,))

        def E_tr_block(sl, col0, ncols, first_tok=None):
            hs = rmsnorm_to_hb(sl, 1)
            E_pe(hs, col0, ncols, first_tok)

        def E2_units(st):
            si = st["si"]
            l2 = st["l2bs"]
            n = len(l2)
            units = []
            while st["e_done"] < n:
                i = st["e_done"]
                st["e_done"] += 1
                box = {}

                def ch(i=i, box=box):
                    S.tag = "E"
                    box["hs"] = rmsnorm_to_hb(xslot(si, l2[i]), 1)

                def pe(i=i, box=box):
                    S.tag = "E"
                    E_pe(box["hs"], 8 + i * 128, 128)

                units.append((ch, pe))
            c0 = 8 + n * 128
            br = right_halo_block(st)
            box = {}

            def ch_r(box=box):
                S.tag = "E"
                if br is not None:
                    box["hs"] = rmsnorm_to_hb(xslot(si, br), 1)

            def pe_r(box=box):
                S.tag = "E"
                if br is not None:
                    E_pe(box["hs"], c0, 8, "head8")
                else:
                    memset("pool", h2T[:, :, c0:c0 + 8], 0.0, w=(B_h2T,))
                if not st["l2last"]:
                    cp("pool", h2halo[:], h2T[:, :, c0 - 8:c0], r=(B_h2T,), w=(B_h2halo,))

            units.append((ch_r, pe_r))
            return units

        def E_left(st):
            si = st["si"]
            kind = sinfo[si]["kind"]
            l2 = st["l2bs"]
            S.tag = "E"
            if st["l2first"]:
                if kind == "halo":
                    E_tr_block(xslot(si, l2[0] - 1), 0, 8, "tail8")
                else:
                    memset("pool", h2T[:, :, 0:8], 0.0, w=(B_h2T,))
            else:
                cp("pool", h2T[:, :, 0:8], h2halo[:], r=(B_h2halo,), w=(B_h2T,))

        def E_block(st, i):
            S.tag = "E"
            E_tr_block(xslot(st["si"], st["l2bs"][i]), 8 + i * 128, 128)

        def phase_DE(st, nxt=None):
            qbs, l2 = st["qbs"], st["l2bs"]
            st["e_done"] = 0
            if nxt is not None and len(nxt["new"]) >= 2:
                S.tag = "A"
                nxt["pre_chains"] = {k: rmsnorm_to_hb(xslot(nxt["si"], nxt["new"][k]), 0) for k in range(2)}
            if l2:
                E_left(st)

            def ready(bq, upto):
                return (bq not in qbs) or (qbs.index(bq) <= upto)

            pend = []

            def hook(idx):
                while pend:
                    hs, i = pend.pop(0)
                    S.tag = "E"
                    E_pe(hs, 8 + i * 128, 128)
                if l2 and st["e_done"] < len(l2) and ready(l2[st["e_done"]], idx - 1):
                    i = st["e_done"]
                    st["e_done"] += 1
                    S.tag = "E"
                    pend.append((rmsnorm_to_hb(xslot(st["si"], l2[i]), 1), i))

            if l2:
                hook(-1)
            out_proj(st, qbs, 0, "o1", after_block=hook if l2 else None)
            while pend:
                hs, i = pend.pop(0)
                S.tag = "E"
                E_pe(hs, 8 + i * 128, 128)

        def phase_E2(st):
            for ch, pe in E2_units(st):
                ch()
                pe()

        def phase_F(st):
            si = st["si"]
            sg = sinfo[si]
            kidx = 0 if sg["kind"] == "halo" else 1
            l2 = st["l2bs"]
            n = len(l2)
            N = n * 128
            W = N + 16
            if W <= 512:
                ranges = [(0, W)]
            else:
                ranges = [(0, W // 2), (W // 2, W)]
            cnt = dict(k=0)

            def G2p(g):
                S.tag = "F-g2"
                wg = wget(("g2", g))
                for m in range(4):
                    bank = alloc_banks(1)
                    for kc in range(8):
                        mm(ps[:, bank, 0:N], wring[:, wg, kc, m * 128:(m + 1) * 128], h2T[:, kc, 8:8 + N], kc == 0, kc == 7,
                           r=(B_h2T, B_w[wg]), w=(B_ps[bank],))
                    act(aT[:, 4 * g + m, 0:N], ps[:, bank, 0:N], AF.Silu, r=(B_ps[bank],), w=tuple(B_aT[i] for i in range(n)))
                wrel()

            def Up(g):
                S.tag = "F-u"
                wu = wget(("u", g))
                for m in range(4):
                    for (c0, c1) in ranges:
                        bank = alloc_banks(1)
                        for kc in range(8):
                            mm(ps[:, bank, 0:c1 - c0], wring[:, wu, kc, m * 128:(m + 1) * 128], h2T[:, kc, c0:c1], kc == 0, kc == 7,
                               r=(B_h2T, B_w[wu]), w=(B_ps[bank],))
                        cp("act", uT[:, m, c0:c1], ps[:, bank, 0:c1 - c0], r=(B_ps[bank],), w=(B_uT[m],))
                    if m == 3:
                        wrel()
                    U = uT[:, m, :]
                    Bu = B_uT[m]
                    pe_ = "pool" if m % 2 == 0 else "dve"
                    o_ = 0 if m % 2 == 0 else 2
                    pa = ptmp[:, o_, :]
                    pb_ = ptmp[:, o_ + 1, :]
                    Ba, Bb = B_ptmp[o_], B_ptmp[o_ + 1]
                    w_ = 2 ** (g + 1)
                    if g == 0:
                        box, Bbox = None, None
                    elif g == 1:
                        tt(pe_, pa[:, 0:N + 2], U[:, 6:8 + N], U[:, 7:9 + N], ALU.add, r=(Bu,), w=(Ba,))
                        tt(pe_, pb_[:, 0:N], pa[:, 0:N], pa[:, 2:N + 2], ALU.add, r=(Ba,), w=(Bb,))
                        box, Bbox = pb_, Bb
                    elif g == 2:
                        tt(pe_, pa[:, 0:N + 6], U[:, 4:10 + N], U[:, 5:11 + N], ALU.add, r=(Bu,), w=(Ba,))
                        tt(pe_, pb_[:, 0:N + 4], pa[:, 0:N + 4], pa[:, 2:N + 6], ALU.add, r=(Ba,), w=(Bb,))
                        tt(pe_, pa[:, 0:N], pb_[:, 0:N], pb_[:, 4:N + 4], ALU.add, r=(Bb,), w=(Ba,))
                        box, Bbox = pa, Ba
                    else:
                        tt(pe_, pa[:, 0:N + 14], U[:, 0:N + 14], U[:, 1:N + 15], ALU.add, r=(Bu,), w=(Ba,))
                        tt(pe_, pb_[:, 0:N + 12], pa[:, 0:N + 12], pa[:, 2:N + 14], ALU.add, r=(Ba,), w=(Bb,))
                        tt(pe_, pa[:, 0:N + 8], pb_[:, 0:N + 8], pb_[:, 4:N + 12], ALU.add, r=(Bb,), w=(Ba,))
                        tt(pe_, pb_[:, 0:N], pa[:, 0:N], pa[:, 8:N + 8], ALU.add, r=(Ba,), w=(Bb,))
                        box, Bbox = pb_, Bb
                    pbi = cnt["k"] % 2
                    cnt["k"] += 1
                    pm = pmix[:, pbi, :]
                    oth = "dve" if pe_ == "pool" else "pool"
                    if g == 0:
                        tt(pe_, pm[:, 0:N], U[:, 7:7 + N], U[:, 8:8 + N], ALU.subtract, r=(Bu,), w=(B_pmix[pbi],))
                    else:
                        stt(pm[:, 0:N], U[:, 8:8 + N], -float(w_), box[:, 0:N], ALU.mult, ALU.add, r=(Bbox, Bu), w=(B_pmix[pbi],))
                    for (flag, fl, cs) in ((st["l2first"], 0, 0), (st["l2last"], 1, N - 8)):
                        if flag:
                            if g == 0:
                                bx = ptmp[:, o_, 0:8]
                                tt("dve", bx, U[:, 7 + cs:15 + cs], U[:, 8 + cs:16 + cs], ALU.add, r=(Bu,), w=(Ba,))
                                tt("dve", pfix[:, :], bx, icnt[:, kidx, fl, g, :], ALU.mult, r=(Ba, B_const), w=(B_pfix,))
                            else:
                                tt("dve", pfix[:, :], box[:, cs:cs + 8], icnt[:, kidx, fl, g, :], ALU.mult, r=(Bbox, B_const), w=(B_pfix,))
                            stt(pm[:, cs:cs + 8], U[:, 8 + cs:16 + cs], -float(w_), pfix[:, :], ALU.mult, ALU.add,
                                r=(B_pfix, Bu), w=(B_pmix[pbi],))
                    c = 4 * g + m
                    tt(oth, aT[:, c, 0:N], pm[:, 0:N], aT[:, c, 0:N], ALU.mult, r=(B_pmix[pbi],), w=tuple(B_aT[i] for i in range(n)))

            for k_, g in enumerate((3, 0, 2, 1)):
                G2p(g)
                Up(g)
                if k_ == 0:
                    flush_k()

        def phase_store(st):
            si = st["si"]
            sg = sinfo[si]
            for b in st["l2bs"]:
                sl = xslot(si, b)
                ob = b - 2 if sg["kind"] == "halo" else b
                row = (sg["yo"] + ob) * 128
                dma("sp", y_d[row:row + 128, :], xres[:, sl, :], r=(B_x[sl],), w=())

        dbg_outs = {}

        def dbg(name, ap, shape, dt, bufs):
            if not debug or name in dbg_outs:
                return
            t = nc.dram_tensor("dbg_" + name, list(shape), dt, kind="ExternalOutput").ap()
            dbg_outs[name] = t
            dma("sp", t, ap, r=bufs, w=())

        def prologue_fuse():
            S.tag = "P"
            vin2_f = w_in2.rearrange("(kc p) n -> p kc n", p=128)
            vgrp_f = w_grp.rearrange("(kc p) n -> p kc n", p=128)
            sA, sB, sC, sD = 0, 1, 2, 3
            wC = wring[:, sC, :, :].rearrange("p a b -> p (a b)").rearrange("p (cc r) -> p cc r", cc=4)
            for g in range(4):
                B_cast[("u", g)] = Buf("fused")
                sct = pmix[:, g % 2, :]
                dma("sp", sct, pscale_d[:, g, :], r=(), w=(B_pmix[g % 2],))
                ts("dve", sct, sct, 1.0 / (2 ** (g + 1)), ALU.mult, r=(), w=(B_pmix[g % 2],))
                dma("pool", wring[:, sA, :, :], vin2_f[:, :, 512 * g:512 * g + 512], r=(), w=(B_w[sA],))
                dma("pool", wring[:, sB, 0:4, :], vgrp_f[:, 4 * g:4 * g + 4, :], r=(), w=(B_w[sB],))
                for cc in range(4):
                    bank = alloc_banks(1)
                    pb = psbf(bank)
                    for m in range(8):
                        tr(pb[:, m * 128:(m + 1) * 128], wring[:, sA, m, cc * 128:(cc + 1) * 128], ident[:], r=(B_w[sA], B_const), w=(B_ps[bank],))
                    cp("dve" if cc % 2 else "act", wC[:, cc, :], pb[:, :], r=(B_ps[bank],), w=(B_w[sC],))
                for m in range(8):
                    bank = alloc_banks(1)
                    for cc in range(4):
                        mm(ps[:, bank, :], wC[:, cc, m * 128:(m + 1) * 128], wring[:, sB, cc, :], cc == 0, cc == 3,
                           r=(B_w[sC], B_w[sB]), w=(B_ps[bank],))
                    tt("dve", wring[:, sD, m, :], ps[:, bank, :], sct, ALU.mult, r=(B_ps[bank], B_pmix[g % 2]), w=(B_w[sD],))
                dma("sp", v_in2[:, :, 512 * g:512 * g + 512], wring[:, sD, :, :], r=(B_w[sD],), w=(B_cast[("u", g)],))

        prologue_fuse()

        w_issue_upto(NW)
        issue_rope(steps[0])
        if len(steps) > 1:
            issue_rope(steps[1])
        try_issue_x(1)
        nph = 0
        for pos, (ph, i) in enumerate(sched):
            if limit is not None and nph >= limit:
                break
            nph += 1
            st = steps[i]
            S.tag = ph
            if ph == "AB":
                if st["new"]:
                    need_upto = max(k for k, (ii, _, _) in enumerate(xq) if ii == i)
                    assert xst["next"] > need_upto, ("x load not issued in time", i)
                inj = E2_units(steps[i - 1]) if (i >= 1 and steps[i - 1]["l2bs"]) else None
                phase_AB(st, inj)
                dbg("hT", hT[:], [128, 8, 640], BF16, tuple(B_hT))
                dbg("kT", kT[:], [128, 8, 512], BF16, tuple(B_kT))
                dbg("vv", vv[:], [128, 8, 512], BF16, tuple(B_v))
                dbg("ktok", ktok[:], [128, 4, 512], BF16, (B_ktok,))
            elif ph == "Cpre":
                flush_k()
                phase_C(st, "pre")
            elif ph == "C":
                phase_C(st)
                dbg("qtok", qtok[:], [128, 2, 4, 512], BF16, tuple(B_qtok))
                dbg("qT", qT[:], [128, 2, 4, 512], BF16, tuple(B_qT))
                dbg("aT", aT[:], [128, 16, 512], BF16, tuple(B_aT))
            elif ph == "DE":
                nx = steps[i + 1] if i + 1 < len(steps) else None
                if nx is not None and nx["new"]:
                    need_upto = max(k for k, (ii, _, _) in enumerate(xq) if ii == i + 1)
                    if xst["next"] <= need_upto:
                        nx = None
                phase_DE(st, nx)
                if debug:
                    for b in st["qbs"]:
                        sl = xslot(st["si"], b)
                        row = (sinfo[st["si"]]["xo"] + b) * 128
                        dma("sp", dbg_d[row:row + 128, :], xres[:, sl, :], r=(B_x[sl],), w=())
            elif ph == "E2":
                phase_E2(st)
            elif ph == "F":
                phase_F(st)
            elif ph == "H":
                out_proj(st, st["l2bs"], 1, "o2")
                phase_store(st)
            elif ph == "END":
                if i + 2 < len(steps):
                    issue_rope(steps[i + 2])
            release_x(pos)
            try_issue_x(i + 2)
        last = [o for o in S.q["sp"] if o.dma]
        tail = last[-S.NDS["sp"]:]
        fin = Buf("fin")
        for o in tail:
            fin.r.append(o)
        S.op("sp", lambda E: E.nop(), r=(), w=(fin,))
        assert limit is not None or wst["used"] == len(allp) == wst["rel"], (wst, len(allp))

        need = S.emit(nc, None, None, None)
        sem_stack = ExitStack()
        with sem_stack:
            esems = {e: [sem_stack.enter_context(nc.semaphore(f"s_{e}_{k}")) for k in range(need[e])] for e in S.ENGS}
            dsems = {e: [sem_stack.enter_context(nc.semaphore(f"d_{e}_{k}")) for k in range(n)] for e, n in S.NDS.items()}
            with nc.Block() as block:
                @block.tensor
                def _(E):
                    S.run_engine("pe", E, esems, dsems)

                @block.scalar
                def _(E):
                    S.run_engine("act", E, esems, dsems)

                @block.vector
                def _(E):
                    S.run_engine("dve", E, esems, dsems)

                @block.gpsimd
                def _(E):
                    S.run_engine("pool", E, esems, dsems)

                @block.sync
                def _(E):
                    S.run_engine("sp", E, esems, dsems)
    counts = {e: len(S.q[e]) for e in S.ENGS}
    counts["tags"] = {e: [o.tag for o in S.q[e]] for e in S.ENGS}
    return nc, counts


def rope_table(pos):
    half = 16
    inv_freq = (np.float32(ROPE_THETA) ** (-(np.arange(half, dtype=np.float32) * np.float32(2.0) / np.float32(32)))).astype(np.float32)
    ang = (pos.astype(np.float32)[:, None] * inv_freq[None, :]).astype(np.float32)
    c = np.cos(ang).astype(np.float32)
    s = np.sin(ang).astype(np.float32)
    return np.concatenate([c, c, -s, s], axis=1).astype(np.float32)


def inv_counts(S_len, first):
    out = np.zeros((4, 8), np.float32)
    for g, w in enumerate((2, 4, 8, 16)):
        for i in range(8):
            t = i if first else S_len - 8 + i
            lo = max(t - w // 2, 0)
            hi = min(t + w // 2 - 1, S_len - 1)
            out[g, i] = float(w) / float(hi - lo + 1)
    return out


def tri_masks():
    kl = np.arange(128)[:, None]
    ql = np.arange(128)[None, :]
    NEG = np.float32(-30000.0)
    triL = np.where(ql <= kl, np.float32(0.0), NEG).astype(np.float32)
    triU = np.where(kl <= ql, np.float32(0.0), NEG).astype(np.float32)
    return triL, triU


def core_inputs(seg_specs, params):
    xs, ropes = [], []
    triL, triU = tri_masks()
    m = np.zeros((128, 4, 128), np.float32)
    m[:, 0] = triL
    m[:, 1] = triU
    ic = np.zeros((2, 2, 4, 8), np.float32)
    for kind, xseq, a, b in seg_specs:
        S_len = xseq.shape[0]
        if kind == "halo":
            lo, hi = a - 256, b + 256
            xe = np.zeros((hi - lo, D), np.float32)
            s0, s1 = max(lo, 0), min(hi, S_len)
            xe[s0 - lo:s1 - lo] = xseq[s0:s1]
            pos = np.clip(np.arange(lo, hi), 0, S_len - 1)
            m[:, 2] = triL if a > 0 else -30000.0
            m[:, 3] = triU if b < S_len else -30000.0
            wv = np.ones((4, 8), np.float32)
            ic[0, 0] = inv_counts(S_len, True) if a == 0 else wv
            ic[0, 1] = inv_counts(S_len, False) if b == S_len else wv
        else:
            xe = xseq[a:b]
            pos = np.arange(a, b)
            ic[1, 0] = inv_counts(S_len, True)
            ic[1, 1] = inv_counts(S_len, False)
        xs.append(xe)
        ropes.append(rope_table(pos))
    d = dict(params)
    d["x"] = np.ascontiguousarray(np.concatenate(xs, 0))
    d["rope"] = np.ascontiguousarray(np.concatenate(ropes, 0))
    d["masks"] = m.astype(ml_dtypes.bfloat16)
    d["icnt"] = np.ascontiguousarray(np.broadcast_to(ic[None], (128, 2, 2, 4, 8))).astype(np.float32)
    return d


def shared_params(norm_pre, norm_post, attn_w_in, attn_sink, attn_w_out, pool_w_in, pool_w_group, pool_scale, pool_w_out):
    p = {}
    p["ident"] = np.eye(128, dtype=np.float32).astype(ml_dtypes.bfloat16)
    p["gpre"] = np.ascontiguousarray(np.asarray(norm_pre, np.float32).reshape(2, 8, 128).transpose(2, 0, 1))
    p["gpost"] = np.ascontiguousarray(np.broadcast_to(np.asarray(norm_post, np.float32)[None], (128, 2, D)))
    p["sinkb"] = np.ascontiguousarray(np.broadcast_to(np.asarray(attn_sink, np.float32).reshape(1, NH), (128, NH)))
    p["pscale"] = np.ascontiguousarray(np.broadcast_to(np.asarray(pool_scale, np.float32).reshape(1, 4, 512), (128, 4, 512)))
    p["attn_w_in"] = np.ascontiguousarray(np.asarray(attn_w_in, np.float32).reshape(D, ATT_IN))
    p["attn_w_out"] = np.ascontiguousarray(np.asarray(attn_w_out, np.float32).reshape(BW, D))
    p["pool_w_in"] = np.ascontiguousarray(np.asarray(pool_w_in, np.float32).reshape(D, POOL_IN))
    p["pool_w_group"] = np.ascontiguousarray(np.asarray(pool_w_group, np.float32).reshape(BW, 512))
    p["pool_w_out"] = np.ascontiguousarray(np.asarray(pool_w_out, np.float32).reshape(BW, D))
    return p


_PROG_CACHE = {}


def get_program(segs):
    key = tuple(segs)
    if key not in _PROG_CACHE:
        _PROG_CACHE[key] = build_program(list(segs))
    return _PROG_CACHE[key]


def kernel(x_prompt, x_sample, norm_pre, norm_post, attn_w_in, attn_sink, attn_w_out,
           pool_w_in, pool_w_group, pool_scale, pool_w_out):
    x_prompt = np.asarray(x_prompt, np.float32)
    x_sample = np.asarray(x_sample, np.float32)
    n_cores = 8
    PB, PS = x_prompt.shape[0], x_prompt.shape[1]
    SB, SS = x_sample.shape[0], x_sample.shape[1]
    per = n_cores // PB
    q_len = PS // per
    s_per = SB // n_cores
    segs = [("halo", q_len // 128 + 4)] + [("full", SS // 128)] * s_per
    nc, _ = get_program(tuple(segs))
    params = shared_params(norm_pre, norm_post, attn_w_in, attn_sink, attn_w_out, pool_w_in, pool_w_group, pool_scale, pool_w_out)
    in_maps = []
    for c in range(n_cores):
        pb, qi = c // per, c % per
        specs = [("halo", x_prompt[pb], qi * q_len, (qi + 1) * q_len)]
        for k in range(s_per):
            specs.append(("full", x_sample[c * s_per + k], 0, SS))
        in_maps.append(core_inputs(specs, params))
    res = run_bass_kernel_spmd(nc, in_maps, core_ids=list(range(n_cores)))
    y_prompt = np.empty_like(x_prompt)
    y_sample = np.empty_like(x_sample)
    for c in range(n_cores):
        y = res.results[c]["y"]
        pb, qi = c // per, c % per
        y_prompt[pb, qi * q_len:(qi + 1) * q_len] = y[:q_len]
        for k in range(s_per):
            y_sample[c * s_per + k] = y[q_len + k * SS: q_len + (k + 1) * SS]
    return (y_prompt, y_sample)
```
